# Optimizing a Trainium2 kernel written in Bass

```python
import jax, jax.numpy as jnp
from jax import lax
import numpy as np

D_MODEL = 1024
BATCH = 16
SEQ = 4096
DEPTH = 1

CHUNK = 64
D_PLE = 256
D_FF = 2816
D_CONV = 512
CONV_WIDTH = 31
N_HEADS_GLA = 4
D_GLA_K = 256
D_GLA_V = 512
HEAD_K = D_GLA_K // N_HEADS_GLA
HEAD_V = D_GLA_V // N_HEADS_GLA
GATE_RANK = 16
GATE_TAU = 16.0
D_MIX = D_CONV + D_GLA_V
D_IN = 2 * D_CONV + 2 * D_GLA_K + 2 * D_GLA_V + GATE_RANK
EPS = 1e-6
SPLIT_POINTS = (
    D_CONV,
    2 * D_CONV,
    2 * D_CONV + D_GLA_K,
    2 * D_CONV + 2 * D_GLA_K,
    2 * D_CONV + 2 * D_GLA_K + D_GLA_V,
    2 * D_CONV + 2 * D_GLA_K + 2 * D_GLA_V,
)

kernel_name = "hymba_conformer_gla_macaron_block"


def _rmsnorm(x, g):
    xf = x.astype(jnp.float32)
    y = xf * lax.rsqrt(jnp.mean(xf * xf, axis=-1, keepdims=True) + EPS)
    return (y * g.astype(jnp.float32)).astype(x.dtype)


def _layernorm(x, g, b):
    xf = x.astype(jnp.float32)
    mu = jnp.mean(xf, axis=-1, keepdims=True)
    var = jnp.mean(jnp.square(xf - mu), axis=-1, keepdims=True)
    y = (xf - mu) * lax.rsqrt(var + EPS)
    return (y * g.astype(jnp.float32) + b.astype(jnp.float32)).astype(x.dtype)


def _swiglu(x, w_in, w_out):
    gate, up = jnp.split(x @ w_in, 2, axis=-1)
    return (jax.nn.silu(gate) * up) @ w_out


def _conv_module(a, b, w_dw, b_dw, ln_g, ln_b):
    glu = a * jax.nn.sigmoid(b)
    y = lax.conv_general_dilated(
        glu, w_dw[:, None, :],
        window_strides=(1,),
        padding=((CONV_WIDTH - 1, 0),),
        dimension_numbers=("NWC", "WIO", "NWC"),
        feature_group_count=D_CONV,
    ) + b_dw
    return jax.nn.silu(_layernorm(y, ln_g, ln_b))


def _to_chunks(t, n_heads):
    bsz, seq, width = t.shape
    return t.reshape(bsz, seq // CHUNK, CHUNK, n_heads, width // n_heads).transpose(0, 1, 3, 2, 4)


def _gla(q, k, v, r, g_lr, w_gate_up, b_gate, o_norm):
    bsz, seq, _ = q.shape
    f32 = jnp.float32
    log_a = jax.nn.log_sigmoid((g_lr @ w_gate_up + b_gate).astype(f32)) / GATE_TAU
    qc = _to_chunks(q.astype(f32), N_HEADS_GLA) * (HEAD_K ** -0.5)
    kc = _to_chunks(k.astype(f32), N_HEADS_GLA)
    vc = _to_chunks(v.astype(f32), N_HEADS_GLA)
    cum = jnp.cumsum(_to_chunks(log_a, N_HEADS_GLA), axis=3)
    e_pos = jnp.exp(cum)
    e_neg = jnp.exp(-cum)
    q_dec = qc * e_pos
    a_low = jnp.einsum("bnhtd,bnhsd->bnhts", q_dec, kc * e_neg)
    a_up = jnp.einsum("bnhtd,bnhsd->bnhts", qc * e_neg, kc * e_pos)
    pos = jnp.arange(CHUNK)
    scores = jnp.where(pos[:, None] >= pos[None, :], a_low, a_up)
    o_intra = jnp.einsum("bnhts,bnhsv->bnhtv", scores, vc)
    cum_end = cum[:, :, :, -1:, :]
    kv = jnp.einsum("bnhsd,bnhsv->bnhdv", kc * jnp.exp(cum_end - cum), vc)
    decay = jnp.exp(cum_end[:, :, :, 0, :])

    def step(state, inp):
        dec, kv_c = inp
        return dec[..., None] * state + kv_c, state

    s0 = jnp.zeros((bsz, N_HEADS_GLA, HEAD_K, HEAD_V), f32)
    _, s_prev = lax.scan(step, s0, (jnp.moveaxis(decay, 1, 0), jnp.moveaxis(kv, 1, 0)))
    s_prev = jnp.moveaxis(s_prev, 0, 1)
    o_inter = jnp.einsum("bnhtd,bnhdv->bnhtv", q_dec, s_prev)
    o = (o_intra + o_inter).transpose(0, 1, 3, 2, 4).reshape(bsz, seq, N_HEADS_GLA, HEAD_V)
    o = o * lax.rsqrt(jnp.mean(o * o, axis=-1, keepdims=True) + EPS)
    o = o * o_norm.astype(f32).reshape(N_HEADS_GLA, HEAD_V)
    return (o.reshape(bsz, seq, D_GLA_V) * jax.nn.silu(r.astype(f32))).astype(q.dtype)


def setup_inputs(seed: int = 0) -> dict:
    key = jax.random.key(seed)
    ks = jax.random.split(key, 24)
    f32 = jnp.float32

    def w(k, shape, fan_in):
        return jax.random.normal(k, shape, f32) * (fan_in ** -0.5)

    def gain(k, shape):
        return 1.0 + 0.05 * jax.random.normal(k, shape, f32)

    def bias(k, shape, scale=0.02):
        return scale * jax.random.normal(k, shape, f32)

    return {
        "x": jax.random.normal(ks[0], (BATCH, SEQ, D_MODEL), f32),
        "p": jax.random.normal(ks[1], (DEPTH, BATCH, SEQ, D_PLE), f32),
        "ffn1_norm": gain(ks[2], (DEPTH, D_MODEL)),
        "ffn1_w_in": w(ks[3], (DEPTH, D_MODEL, 2 * D_FF), D_MODEL),
        "ffn1_w_out": w(ks[4], (DEPTH, D_FF, D_MODEL), D_FF),
        "mix_norm": gain(ks[5], (DEPTH, D_MODEL)),
        "w_in": w(ks[6], (DEPTH, D_MODEL, D_IN), D_MODEL),
        "conv_w": w(ks[7], (DEPTH, CONV_WIDTH, D_CONV), CONV_WIDTH),
        "conv_b": bias(ks[8], (DEPTH, D_CONV)),
        "conv_ln_g": gain(ks[9], (DEPTH, D_CONV)),
        "conv_ln_b": bias(ks[10], (DEPTH, D_CONV)),
        "gate_w_up": w(ks[11], (DEPTH, GATE_RANK, D_GLA_K), GATE_RANK),
        "gate_b": bias(ks[12], (DEPTH, D_GLA_K), 0.1),
        "gla_norm": gain(ks[13], (DEPTH, D_GLA_V)),
        "w_out": w(ks[14], (DEPTH, D_MIX, D_MODEL), D_MIX),
        "ffn2_norm": gain(ks[15], (DEPTH, D_MODEL)),
        "ffn2_w_in": w(ks[16], (DEPTH, D_MODEL, 2 * D_FF), D_MODEL),
        "ffn2_w_out": w(ks[17], (DEPTH, D_FF, D_MODEL), D_FF),
        "ple_norm": gain(ks[18], (DEPTH, D_MODEL)),
        "ple_w_gate": w(ks[19], (DEPTH, D_MODEL, D_MODEL), D_MODEL),
        "ple_w_proj": w(ks[20], (DEPTH, D_PLE, D_MODEL), D_PLE),
        "final_norm": gain(ks[21], (D_MODEL,)),
    }


def reference(x, p, ffn1_norm, ffn1_w_in, ffn1_w_out, mix_norm, w_in, conv_w, conv_b,
              conv_ln_g, conv_ln_b, gate_w_up, gate_b, gla_norm, w_out, ffn2_norm,
              ffn2_w_in, ffn2_w_out, ple_norm, ple_w_gate, ple_w_proj, final_norm):
    h = x
    for i in range(DEPTH):
        h = h + 0.5 * _swiglu(_rmsnorm(h, ffn1_norm[i]), ffn1_w_in[i], ffn1_w_out[i])
        z = _rmsnorm(h, mix_norm[i]) @ w_in[i]
        conv_a, conv_gt, q, k, v, r, g_lr = jnp.split(z, SPLIT_POINTS, axis=-1)
        y_conv = _conv_module(conv_a, conv_gt, conv_w[i], conv_b[i], conv_ln_g[i], conv_ln_b[i])
        y_gla = _gla(q, k, v, r, g_lr, gate_w_up[i], gate_b[i], gla_norm[i])
        h = h + jnp.concatenate([y_conv, y_gla], axis=-1) @ w_out[i]
        h = h + 0.5 * _swiglu(_rmsnorm(h, ffn2_norm[i]), ffn2_w_in[i], ffn2_w_out[i])
        gate = jax.nn.sigmoid(_rmsnorm(h, ple_norm[i]) @ ple_w_gate[i])
        h = h + gate * (p[i] @ ple_w_proj[i])
    return _rmsnorm(h, final_norm)
```

```python
import bisect
import numpy as np
import concourse.bass as bass
import concourse.mybir as mybir

F32 = mybir.dt.float32
BF16 = mybir.dt.bfloat16
AF = mybir.ActivationFunctionType
ALU = mybir.AluOpType

ENGS = ("pe", "act", "dve", "pool", "sp")
N_DMA_SEMS = 24


class _Track:
    def __init__(self):
        self.b = [0]
        self.w = [None]
        self.r = [{}]

    def _split(self, x):
        i = bisect.bisect_right(self.b, x) - 1
        if self.b[i] == x:
            return i
        self.b.insert(i + 1, x)
        self.w.insert(i + 1, self.w[i])
        self.r.insert(i + 1, dict(self.r[i]))
        return i + 1

    def access(self, lo, hi, op, write):
        i = self._split(lo)
        j = self._split(hi)
        deps = set()
        for s in range(i, j):
            if self.w[s] is not None:
                deps.add(self.w[s])
            if write:
                deps.update(self.r[s].values())
                self.w[s] = op
                self.r[s] = {}
            else:
                self.r[s][op.eng] = op
        if write and j - i > 4:
            del self.b[i + 1:j]
            del self.w[i + 1:j]
            del self.r[i + 1:j]
        return deps


class View:
    __slots__ = ("ap", "space", "lo", "hi")

    def __init__(self, ap, space, lo, hi):
        self.ap = ap
        self.space = space
        self.lo = lo
        self.hi = hi


class Buf:
    def __init__(self, handle, shape, esize, space, base):
        self.h = handle
        self.shape = tuple(shape)
        self.esize = esize
        self.space = space
        self.base = base
        st = [1] * len(self.shape)
        for k in range(len(self.shape) - 2, 0, -1):
            st[k] = st[k + 1] * self.shape[k + 1]
        self.strides = st

    def v(self, *idx, p=None):
        nfree = len(self.shape) - 1
        idx = list(idx) + [slice(None)] * (nfree - len(idx))
        lo = 0
        hi = 0
        for k, ix in enumerate(idx):
            dim = self.shape[k + 1]
            stv = self.strides[k + 1]
            if isinstance(ix, int):
                a, bnd = ix, ix
            else:
                a = 0 if ix.start is None else ix.start
                e = dim if ix.stop is None else ix.stop
                assert ix.step in (None, 1)
                bnd = e - 1
            assert 0 <= a <= bnd < dim, (self.shape, idx)
            lo += a * stv
            hi += bnd * stv
        ps = slice(None) if p is None else p
        ap = self.h[(ps,) + tuple(idx)]
        return View(ap, self.space, self.base + lo * self.esize, self.base + (hi + 1) * self.esize)


class Op:
    __slots__ = ("eng", "fn", "deps", "sig", "count", "dsem", "dcount", "is_dma", "idx", "redirect", "line", "wl")

    def __init__(self, eng, fn, is_dma):
        self.eng = eng
        self.fn = fn
        self.deps = []
        self.sig = False
        self.count = None
        self.is_dma = is_dma
        self.dsem = None
        self.dcount = None
        self.redirect = None


class Prog:
    def __init__(self, nc, same_sync=("act", "dve", "pool")):
        self.nc = nc
        self.ops = {e: [] for e in ENGS}
        self.trk = {}
        self.same_sync = set(same_sync)
        self.n_dma = {e: 0 for e in ENGS}
        self.sb_off = 16512
        self.sb_cap = 229344
        self.all_ops = 0

    def sbuf(self, name, shape, dtype, at=None):
        es = 2 if dtype == BF16 else 4
        nbytes = int(np.prod(shape[1:])) * es
        if at is None:
            at = (self.sb_off + 63) // 64 * 64
            self.sb_off = at + nbytes
        assert at >= 16512 and at + nbytes <= self.sb_cap, (name, at, nbytes)
        h = self.nc.alloc_sbuf_tensor_at(name, list(shape), dtype, offset=at)
        return Buf(h, shape, es, "sbuf", at)

    def wrap(self, handle, shape, dtype, space):
        es = 2 if dtype == BF16 else 4
        return Buf(handle, shape, es, space, 0)

    def add(self, eng, fn, reads=(), writes=(), dma=False):
        op = Op(eng, fn, dma)
        import sys as _s
        f_ = _s._getframe(1)
        ln_ = []
        while f_ is not None and len(ln_) < 3:
            ln_.append(f_.f_lineno)
            f_ = f_.f_back
        op.line = ln_
        deps = set()
        for v in reads:
            if v.space is None:
                continue
            t = self.trk.setdefault(v.space, _Track())
            lo, hi = v.lo, v.hi
            if v.space == "psum":
                lo, hi = lo // 2048 * 2048, (hi + 2047) // 2048 * 2048
            deps |= t.access(lo, hi, op, False)
        for v in writes:
            if v.space is None:
                continue
            t = self.trk.setdefault(v.space, _Track())
            lo, hi = v.lo, v.hi
            if v.space == "psum":
                lo, hi = lo // 2048 * 2048, (hi + 2047) // 2048 * 2048
            deps |= t.access(lo, hi, op, True)
        deps.discard(op)
        deps = {(d.redirect or d) for d in deps}
        deps.discard(op)
        for d in deps:
            if d.eng == eng and not d.is_dma and eng not in self.same_sync:
                continue
            if not d.is_dma:
                d.sig = True
            op.deps.append(d)
        if dma:
            k = self.n_dma[eng]
            self.n_dma[eng] += 1
            op.dsem = k % N_DMA_SEMS
            op.dcount = 16 * (k // N_DMA_SEMS + 1)
        self.ops[eng].append(op)
        self.all_ops += 1
        return op

    def dma(self, out, in_, eng="sp", **kw):
        return self.add(eng, lambda e: e.dma_start(out=out.ap, in_=in_.ap, **kw),
                        reads=[in_], writes=[out], dma=True)

    def emit(self, stack):
        nc = self.nc
        esem = {e: stack.enter_context(nc.semaphore("s_" + e)) for e in ENGS if e != "sp"}
        dsem = {}
        for e in ENGS:
            if self.n_dma[e]:
                dsem[e] = [stack.enter_context(nc.semaphore("d_%s_%d" % (e, i)))
                           for i in range(min(N_DMA_SEMS, self.n_dma[e]))]
        for e in ENGS:
            c = 0
            for op in self.ops[e]:
                if op.sig and not op.is_dma:
                    c += 1
                    op.count = c
        final_d = {e: {} for e in ENGS}
        for e in ENGS:
            for op in self.ops[e]:
                if op.is_dma:
                    final_d[e][op.dsem] = op.dcount

        def gen(e):
            def body(engobj):
                waited = {}
                for op in self.ops[e]:
                    need = {}
                    for d in op.deps:
                        if d.is_dma:
                            key = ("d", d.eng, d.dsem)
                            val = d.dcount
                        else:
                            key = ("e", d.eng)
                            val = d.count
                        if need.get(key, 0) < val:
                            need[key] = val
                    if op.is_dma and op.dcount > 16:
                        key = ("d", e, op.dsem)
                        need[key] = max(need.get(key, 0), op.dcount - 16)
                    op.wl = []
                    for key, val in need.items():
                        if waited.get(key, 0) >= val:
                            continue
                        op.wl.append((key, val))
                        waited[key] = val
                        sem = esem[key[1]] if key[0] == "e" else dsem[key[1]][key[2]]
                        engobj.wait_ge(sem, val)
                    ins = op.fn(engobj)
                    if op.is_dma:
                        ins.then_inc(dsem[e][op.dsem], 16)
                    elif op.sig:
                        ins.then_inc(esem[e], 1)
                for si, val in final_d[e].items():
                    if waited.get(("d", e, si), 0) < val:
                        engobj.wait_ge(dsem[e][si], val)
            return body

        with nc.Block() as block:
            block.sync(gen("sp"))
            block.tensor(gen("pe"))
            block.scalar(gen("act"))
            block.vector(gen("dve"))
            block.gpsimd(gen("pool"))
from contextlib import ExitStack

D = 1024
KC = 8
DFF = 2816
JF = 22
DPLE = 256
TT = 512
ST = 4
EPS = 1e-6
NSLOT = 6
import os as _os
DBG = _os.environ.get('KDBG', '')
SLOTW = 2816
CONVW = 31
HALO = CONVW - 1

WNAMES = ["ffn1_norm", "ffn1_w_in", "ffn1_w_out", "mix_norm", "w_in", "conv_w", "conv_b",
          "conv_ln_g", "conv_ln_b", "gate_w_up", "gate_b", "gla_norm", "w_out", "ffn2_norm",
          "ffn2_w_in", "ffn2_w_out", "ple_norm", "ple_w_gate", "ple_w_proj", "final_norm"]
WSHAPES = {"ffn1_norm": [1, 1024], "ffn1_w_in": [1, 1024, 5632], "ffn1_w_out": [1, 2816, 1024],
           "mix_norm": [1, 1024], "w_in": [1, 1024, 2576], "conv_w": [1, 31, 512], "conv_b": [1, 512],
           "conv_ln_g": [1, 512], "conv_ln_b": [1, 512], "gate_w_up": [1, 16, 256], "gate_b": [1, 256],
           "gla_norm": [1, 512], "w_out": [1, 1024, 1024], "ffn2_norm": [1, 1024],
           "ffn2_w_in": [1, 1024, 5632], "ffn2_w_out": [1, 2816, 1024], "ple_norm": [1, 1024],
           "ple_w_gate": [1, 1024, 1024], "ple_w_proj": [1, 256, 1024], "final_norm": [1024]}


def build_program(n_seq, seq_len, stop_after=None, same_sync=("act", "dve", "pool"), mix_stop=99):
    nc = bass.Bass("TRN2", target_bir_lowering=False)
    n_tiles = seq_len // TT
    x_d = nc.dram_tensor("x", [n_seq, seq_len, D], F32, kind="ExternalInput").ap()
    p_d = nc.dram_tensor("p", [n_seq, seq_len, DPLE], F32, kind="ExternalInput").ap()
    wd = {n: nc.dram_tensor(n, WSHAPES[n], F32, kind="ExternalInput").ap() for n in WNAMES}
    out_d = nc.dram_tensor("out", [n_seq, seq_len, D], F32, kind="ExternalOutput").ap()
    NBLK = 85
    wsc_d = nc.dram_tensor("wsc", [NBLK, 128, SLOTW], BF16, kind="Internal").ap()

    st = ExitStack()
    P = Prog(nc, same_sync=same_sync)

    def ext(ap):
        return View(ap, None, 0, 0)

    def A(func, out, in_, scale=None, bias=None, eng="act", accum=None):
        rd = [in_]
        kw = {}
        wr = [out]
        if accum is not None:
            kw["accum_out"] = accum.ap
            wr.append(accum)
        if scale is not None:
            if isinstance(scale, View):
                rd.append(scale); kw["scale"] = scale.ap
            else:
                kw["scale"] = float(scale)
        if bias is not None:
            if isinstance(bias, View):
                rd.append(bias); kw["bias"] = bias.ap
            else:
                kw["bias"] = float(bias)
        return P.add("act", lambda e: e.activation(out=out.ap, in_=in_.ap, func=func, **kw), reads=rd, writes=wr)

    def CP(eng, out, in_):
        if eng == "act":
            return A(AF.Copy, out, in_)
        return P.add(eng, lambda e: e.tensor_copy(out.ap, in_.ap), reads=[in_], writes=[out])

    def TTop(eng, out, a, b, op):
        return P.add(eng, lambda e: e.tensor_tensor(out=out.ap, in0=a.ap, in1=b.ap, op=op), reads=[a, b], writes=[out])

    def TS(eng, out, a, s1, s2, op0, op1=None):
        rd = [a]
        v1 = s1
        v2 = s2
        if isinstance(s1, View):
            rd.append(s1); v1 = s1.ap
        if isinstance(s2, View):
            rd.append(s2); v2 = s2.ap
        if op1 is None:
            return P.add(eng, lambda e: e.tensor_scalar(out=out.ap, in0=a.ap, scalar1=v1, scalar2=None, op0=op0), reads=rd, writes=[out])
        return P.add(eng, lambda e: e.tensor_scalar(out=out.ap, in0=a.ap, scalar1=v1, scalar2=v2, op0=op0, op1=op1), reads=rd, writes=[out])

    def STT(eng, out, a, s, b, op0, op1):
        rd = [a, b]
        sv = s
        if isinstance(s, View):
            rd.append(s); sv = s.ap
        return P.add(eng, lambda e: e.scalar_tensor_tensor(out=out.ap, in0=a.ap, scalar=sv, in1=b.ap, op0=op0, op1=op1), reads=rd, writes=[out])

    grp = {}

    def MM(out, lhsT, rhs, start, stop):
        op = P.add("pe", lambda e: e.matmul(out.ap, lhsT=lhsT.ap, rhs=rhs.ap, start=start, stop=stop), reads=[lhsT, rhs], writes=[out])
        key = out.lo
        if start:
            grp[key] = []
        grp.setdefault(key, []).append(op)
        if stop:
            for g_ in grp[key][:-1]:
                g_.redirect = op
            del grp[key]
        if pg["on"]:
            pg["ops"].append(op)
        return op

    pg = {"on": False, "ops": []}

    def TR(out, in_, idn):
        op = P.add("pe", lambda e: e.transpose(out.ap, in_.ap, idn.ap), reads=[in_, idn], writes=[out])
        if pg["on"]:
            pg["ops"].append(op)
        return op

    def gstart():
        pg["on"] = True
        pg["ops"] = []

    def gend():
        ops_ = pg["ops"]
        for o_ in ops_[:-1]:
            o_.redirect = ops_[-1]
        pg["on"] = False
        pg["ops"] = []

    def MS(eng, v, val):
        return P.add(eng, lambda e: e.memset(v.ap, val), writes=[v])

    banks = []
    for i in range(8):
        b = P.wrap(st.enter_context(nc.psum_tensor("ps%d" % i, [128, 512], F32)), [128, 512], F32, "psum")
        b.base = i * 2048
        banks.append(b)
    bctr = [0]

    def bank():
        b = banks[bctr[0] % 8]
        bctr[0] += 1
        return b

    def bank_bf(b):
        ap = b.h[:, :].bitcast(BF16)

        def v(lo, hi):
            return View(ap[:, lo:hi], "psum", b.base + lo * 2, b.base + hi * 2)
        return v

    sb = P.sbuf
    ident = sb("ident", [128, 128], F32)
    identb = sb("identb", [128, 128], BF16)
    c1024 = sb("c1024", [128, 128], BF16)
    c512 = sb("c512", [128, 128], BF16)
    c128 = sb("c128", [128, 128], BF16)
    triT2 = sb("triT2", [128, 2, 128], F32)
    u2 = sb("u2", [128, 2, 128], F32)
    gcols = sb("gcols", [128, 4, 8], F32)
    gfin = sb("gfin", [128, 8], F32)
    cw = sb("cw", [128, 4, CONVW], F32)
    cb = sb("cb", [128, 4], F32)
    lng = sb("lng", [128, 4], F32)
    lnb = sb("lnb", [128, 4], F32)
    gnorm = sb("gnorm", [128, 4], F32)
    waug_f = sb("waug_f", [128, 256], F32)
    waug = sb("waug", [128, 256], BF16)
    wglr_f = sb("wglr_f", [128, 8, 16], F32)
    wglr = sb("wglr", [128, 8, 128], BF16)
    gaug = sb("gaug", [128, TT], BF16)
    Sst = [sb("S%d" % i, [128, 2, 128], F32) for i in range(n_seq)]
    Sprev = sb("Sprev", [128, 2, 8, 128], BF16)
    hT = sb("hT", [128, KC, TT], F32)
    xT = sb("xT", [128, KC, TT], BF16)
    xin = sb("xin", [128, ST, D], F32)
    osb = sb("osb", [128, ST, D], F32, at=xin.base + ST * D * 4)
    stgpad = sb("stgpad", [128, 256], F32, at=xin.base + 2 * ST * D * 4)
    P.sb_off = xin.base + 2 * ST * D * 4 + 1024
    pin = sb("pin", [128, ST, DPLE], F32)
    pbf = sb("pbf", [128, ST, DPLE], BF16)
    pT = sb("pT", [128, 2, TT], BF16)
    wr_addr = []
    wr_in, wr_out, wr_pp = [], [], []
    for i in range(NSLOT):
        b = sb("wr%d" % i, [128, KC, 256], BF16)
        P.sb_off = b.base + SLOTW * 2
        wr_in.append(b)
        wr_out.append(sb("wro%d" % i, [128, JF, 128], BF16, at=b.base))
        wr_pp.append(sb("wrp%d" % i, [128, 2, 1024], BF16, at=b.base))
    sgb = [sb("sg%d" % i, [128, TT], F32) for i in range(3)]
    sqb = [sb("sq%d" % i, [128, TT], BF16) for i in range(3)]
    rstd_t = [sb("rstd%d" % i, [128, TT], F32) for i in range(2)]
    lnt = sb("lnt", [128, TT], F32)
    qd = sb("qd", [128, 2, TT], BF16)
    qn = sb("qn", [128, 2, TT], BF16)
    kn = sb("kn", [128, 2, TT], BF16)
    kp = sb("kp", [128, 2, TT], BF16)
    vtok = sb("vtok", [128, ST, 512], BF16)
    kdtok = sb("kdtok", [128, ST, 256], BF16)
    scT = [sb("scT%d" % i, [128, 2, 2, 128], BF16) for i in range(2)]
    mt1 = [sb("mt1_%d" % i, [128, 2, 128], F32) for i in range(2)]
    mt2 = [sb("mt2_%d" % i, [128, 2, 128], F32) for i in range(2)]
    osq = sb("osq", [128, 4, TT], BF16)
    region0 = (P.sb_off + 63) // 64 * 64
    hid = sb("hid", [128, JF, TT], BF16)
    ymix = sb("ymix", [128, 8, TT], BF16)
    acc = sb("acc", [128, 4, TT], F32)
    glu = sb("glu", [128, 4, HALO + TT], BF16)
    epos = sb("epos", [128, 2, TT], F32)
    eneg = sb("eneg", [128, 2, TT], F32)
    region1 = P.sb_off
    assert qn.base == qd.base + 2048 and kn.base == qd.base + 4096 and kp.base == qd.base + 6144
    xTn = sb("xTn", [128, KC, TT], BF16, at=qd.base)
    xnb = [sb("xnb%d" % i, [128, D], BF16, at=osq.base + i * 2048) for i in range(2)]
    sscol = sb("sscol", [128, ST], F32)
    tcol = sb("tcol", [128, ST], F32)
    rcol = sb("rcol", [128, ST], F32)
    rsil = sb("rsil", [128, 4, TT], F32, at=hid.base)
    o_sb = sb("o_sb", [128, 4, TT], F32, at=hid.base + 8192)
    Lb = sb("Lb", [128, ST, 256], F32, at=hid.base + 16384)
    eR = [sb("eR%d" % i, [128, 256], F32, at=hid.base + 20480 + i * 1024) for i in range(2)]
    assert hid.base + 22528 >= hid.base + 20480 + 2048
    ctmp = [sb("ctmp0", [128, TT], F32, at=pbf.base), sb("ctmp1", [128, TT], F32, at=pT.base)]
    NSTG = 3
    stg_f = []
    a0 = xin.base
    for i in range(NSTG):
        stg_f.append({"in": sb("sfi%d" % i, [128, KC, 256], F32, at=a0),
                      "out": sb("sfo%d" % i, [128, JF, 128], F32, at=a0),
                      "pp": sb("sfp%d" % i, [128, 2, 1024], F32, at=a0)})
        a0 += SLOTW * 4
    assert a0 <= xin.base + 2 * ST * D * 4 + 1024
    print("SBUF used up to", P.sb_off, "cap", P.sb_cap)

    MS("pool", ident.v(), 0.0)
    P.add("pool", lambda e: e.affine_select(out=ident.v().ap, in_=ident.v().ap, pattern=[[-1, 128]],
                                           compare_op=ALU.not_equal, fill=1.0, base=0, channel_multiplier=1),
          reads=[ident.v()], writes=[ident.v()])
    CP("dve", identb.v(), ident.v())
    MS("pool", c1024.v(), 1.0 / 1024)
    MS("pool", c512.v(), 1.0 / 512)
    MS("pool", c128.v(), 1.0 / 128)
    for e in range(2):
        MS("pool", triT2.v(e), 1.0)
        P.add("pool", (lambda e_: lambda g: g.affine_select(out=triT2.v(e_).ap, in_=triT2.v(e_).ap, pattern=[[1, 128]],
                                                            compare_op=ALU.is_ge, fill=0.0, base=0, channel_multiplier=-1))(e),
              reads=[triT2.v(e)], writes=[triT2.v(e)])
        MS("pool", triT2.v(e, slice(64, 128), p=slice(0, 64)), 0.0)
        MS("pool", u2.v(e), 1.0)
        P.add("pool", (lambda e_: lambda g: g.affine_select(out=u2.v(e_).ap, in_=u2.v(e_).ap, pattern=[[-1, 128]],
                                                            compare_op=ALU.is_gt, fill=0.0, base=0, channel_multiplier=1))(e),
              reads=[u2.v(e)], writes=[u2.v(e)])
        MS("pool", u2.v(e, slice(0, 64), p=slice(64, 128)), 0.0)

    def small_dma(out, ap):
        P.add("sp", lambda e: e.dma_start(out=out.ap, in_=ap, allow_slow_non_contiguous=True), writes=[out], dma=True)

    for i, nm in enumerate(["ffn1_norm", "mix_norm", "ffn2_norm", "ple_norm"]):
        small_dma(gcols.v(i), wd[nm].rearrange("o (c p) -> p (o c)", p=128))
    small_dma(gfin.v(), wd["final_norm"].rearrange("(c p) -> p c", p=128))
    for cc_ in range(4):
        small_dma(cw.v(cc_), wd["conv_w"][0][:, cc_ * 128:(cc_ + 1) * 128].rearrange("j p -> p j"))
    small_dma(cb.v(), wd["conv_b"].rearrange("o (c p) -> p (o c)", p=128))
    small_dma(lng.v(), wd["conv_ln_g"].rearrange("o (c p) -> p (o c)", p=128))
    small_dma(lnb.v(), wd["conv_ln_b"].rearrange("o (c p) -> p (o c)", p=128))
    small_dma(gnorm.v(), wd["gla_norm"].rearrange("o (c p) -> p (o c)", p=128))
    MS("pool", waug_f.v(), 0.0)
    P.dma(waug_f.v(p=slice(0, 16)), ext(wd["gate_w_up"][0]))
    P.dma(waug_f.v(p=slice(16, 17)), ext(wd["gate_b"]))
    CP("dve", waug.v(), waug_f.v())
    MS("pool", gaug.v(), 1.0)
    P.dma(wglr_f.v(), ext(wd["w_in"][0][:, 2560:2576].rearrange("(c p) x -> p c x", p=128)))
    MS("pool", wglr.v(), 0.0)
    for c in range(KC):
        TS("dve", wglr.v(c, slice(0, 16)), wglr_f.v(c), gcols.v(1, slice(c, c + 1)), None, ALU.mult)
    for i in range(n_seq):
        MS("pool", Sst[i].v(), 0.0)

    blocks = []

    def rin(w, c0, n):
        return w[:, c0:c0 + n].rearrange("(c p) x -> p c x", p=128)

    def add_ffn(pref, gi):
        wi = wd[pref + "_w_in"][0]
        wo = wd[pref + "_w_out"][0]
        for j in range(JF):
            blocks.append(("in", [(rin(wi, j * 128, 128), 0, 128), (rin(wi, DFF + j * 128, 128), 128, 256)], gi, None))
        for m in range(KC):
            blocks.append(("out", [(rin(wo, m * 128, 128), 0, 128)], None, 0.5))

    add_ffn("ffn1", 0)
    BLK_MIX = len(blocks)
    wm = wd["w_in"][0]
    for c0 in (0, 256, 512, 768):
        blocks.append(("in", [(rin(wm, c0, 256), 0, 256)], 1, None))
    for b_ in range(6):
        blocks.append(("dg", [], b_, None))
    for c0 in (1024, 1280, 1536, 1792, 2048, 2304):
        blocks.append(("in", [(rin(wm, c0, 256), 0, 256)], 1, None))
    for q in range(4):
        blocks.append(("in", [(rin(wd["w_out"][0], q * 256, 256), 0, 256)], None, None))
    BLK_FFN2 = len(blocks)
    add_ffn("ffn2", 2)
    BLK_PLE = len(blocks)
    for q in range(4):
        blocks.append(("in", [(rin(wd["ple_w_gate"][0], q * 256, 256), 0, 256)], 3, None))
    blocks.append(("pp", [(wd["ple_w_proj"][0].rearrange("(c p) x -> p c x", p=128), 0, 1024)], None, None))
    assert len(blocks) == NBLK
    BLKN = {"in": 2048, "out": 2816, "pp": 2048, "dg": 2816}

    def wsc_view(bi, n):
        return View(wsc_d[bi, :, 0:n], "wsc", bi, bi + 1)

    cmap = ["dve", "act", "dve", "act", "dve", "dve", "act", "dve"]

    def cast1(eng, dst, src, scal):
        if eng == "act":
            A(AF.Copy, dst, src, scale=scal)
        elif scal is None:
            CP(eng, dst, src)
        else:
            TS(eng, dst, src, scal, None, ALU.mult)

    def pro_cast_store(bi, slot):
        kind, srcs, gi, cs = blocks[bi]
        n = BLKN[kind]
        flat_src = wr_out[slot] if n > 2048 else wr_in[slot]
        flat = View(flat_src.h[:, :, :].rearrange("p a b -> p (a b)"), "sbuf", wr_in[slot].base, wr_in[slot].base + n * 2)
        if kind == "dg":
            sbb = wr_out[slot]
            for k_ in range(22):
                idx_ = gi * 22 + k_
                if idx_ >= 4 * CONVW:
                    MS("pool", sbb.v(k_), 0.0)
                    continue
                cc_, j_ = idx_ // CONVW, idx_ % CONVW
                cast1("dve" if k_ % 2 else "act", sbb.v(k_), ident.v(), cw.v(cc_, slice(j_, j_ + 1)))
            P.dma(wsc_view(bi, n), flat, eng="act")
            return
        sf = stg_f[pro["nload_idx"][bi] % NSTG][kind]
        sbb = {"in": wr_in, "out": wr_out, "pp": wr_pp}[kind][slot]
        if gi is not None:
            for c in range(KC):
                cast1(cmap[(c + bi) % 8], sbb.v(c), sf.v(c), gcols.v(gi, slice(c, c + 1)))
        else:
            nk = sf.shape[1]
            cuts = [0, (nk * 5) // 9, nk]
            for eng, a_, b_ in zip(("dve", "act"), cuts[:-1], cuts[1:]):
                if b_ > a_:
                    sc_ = cs if cs is not None else (1.0 if eng == "act" else None)
                    cast1(eng, sbb.v(slice(a_, b_)), sf.v(slice(a_, b_)), sc_)
        P.dma(wsc_view(bi, n), flat, eng="act")

    pro = {"loaded": 0, "nload": 0, "nload_idx": {}}

    def pro_prefetch(upto):
        while pro["loaded"] < min(upto, NBLK):
            bi = pro["loaded"]
            if blocks[bi][0] != "dg":
                pro["nload_idx"][bi] = pro["nload"]
                kind, srcs, gi, cs = blocks[bi]
                sf = stg_f[pro["nload"] % NSTG][kind]
                for (ap, lo, hi) in srcs:
                    P.dma(sf.v(slice(None), slice(lo, hi)), ext(ap))
                pro["nload"] += 1
            pro["loaded"] += 1

    tiles = [(sq_, ti) for sq_ in range(n_seq) for ti in range(n_tiles)]
    total_blocks = len(tiles) * NBLK
    wst = {"issued": 0, "consumed": 0, "released": 0}

    def _wissue():
        while wst["issued"] < total_blocks and wst["issued"] - NSLOT < wst["released"]:
            k = wst["issued"]
            bi = k % NBLK
            n = BLKN[blocks[bi][0]]
            slot = k % NSLOT
            if k < NBLK:
                pro_prefetch(k + NSTG)
                pro_cast_store(bi, slot)
            else:
                src_buf = wr_out[slot] if n > 2048 else wr_in[slot]
                dst = View(src_buf.h[:, :, :].rearrange("p a b -> p (a b)"), "sbuf", wr_in[slot].base, wr_in[slot].base + n * 2)
                P.dma(dst, wsc_view(bi, n))
            wst["issued"] += 1

    def wnext():
        _wissue()
        k = wst["consumed"]
        assert k < wst["issued"], "weight ring too small for the number of blocks held"
        wst["consumed"] += 1
        kind = blocks[k % NBLK][0]
        slot = k % NSLOT
        return {"in": wr_in, "out": wr_out, "pp": wr_pp, "dg": wr_out}[kind][slot]

    def wdone(n=1):
        wst["released"] += n
        assert wst["released"] <= wst["consumed"]
        _wissue()

    rctr = [0]

    def rmsnorm_stats(src, cmat, nchunks, eps_val=EPS, sqdst=None):
        bk = bank()
        for m in range(nchunks):
            sq = sqb[m % 3].v() if sqdst is None else sqdst(m)
            A(AF.Square, sq, src(m))
            MM(bk.v(), cmat.v(), sq, m == 0, m == nchunks - 1)
        A(AF.Ln, lnt.v(), bk.v(), bias=eps_col.v())
        r = rstd_t[rctr[0] % 2]
        rctr[0] += 1
        A(AF.Exp, r.v(), lnt.v(), scale=-0.5)
        return r

    def norm_to_xT():
        r = rmsnorm_stats(lambda m: hT.v(m), c1024, KC, sqdst=lambda m: xT.v(m))
        for m in range(KC):
            TTop("dve", xT.v(m), hT.v(m), r.v(), ALU.mult)

    def ffn_block(j, xsrc):
        W = wnext()
        bg = bank()
        for c in range(KC):
            MM(bg.v(), W.v(c, slice(0, 128)), xsrc.v(c), c == 0, c == KC - 1)
        bu = bank()
        for c in range(KC):
            MM(bu.v(), W.v(c, slice(128, 256)), xsrc.v(c), c == 0, c == KC - 1)
        sg = sgb[j % 3]
        A(AF.Silu, sg.v(), bg.v())
        TTop("dve", hid.v(j), sg.v(), bu.v(), ALU.mult)
        wdone()

    def ffn_up(js, xsrc, hooks=None):
        for j in js:
            ffn_block(j, xsrc)
            if hooks and j in hooks:
                for h_ in hooks[j]:
                    h_()

    def ffn_up_first2(xsrc):
        bb = [(wnext(), bank(), bank()), (wnext(), bank(), bank())]
        for c in range(KC):
            for (W, bg, bu) in bb:
                MM(bg.v(), W.v(c, slice(0, 128)), xsrc.v(c), c == 0, c == KC - 1)
                MM(bu.v(), W.v(c, slice(128, 256)), xsrc.v(c), c == 0, c == KC - 1)
        for jj, (W, bg, bu) in enumerate(bb):
            sg = sgb[jj % 3]
            A(AF.Silu, sg.v(), bg.v())
            TTop("dve", hid.v(jj), sg.v(), bu.v(), ALU.mult)
        wdone(2)

    def ffn_down():
        for m in range(KC):
            W = wnext()
            bk = bank()
            for j in range(JF):
                MM(bk.v(), W.v(j), hid.v(j), j == 0, j == JF - 1)
            TTop("dve", hT.v(m), hT.v(m), bk.v(), ALU.add)
            wdone()

    def prepA(s):
        xb_ = xnb[s % 2]
        A(AF.Square, xb_.v(), xin.v(s), accum=sscol.v(slice(s, s + 1)))
        A(AF.Ln, tcol.v(slice(s, s + 1)), sscol.v(slice(s, s + 1)), scale=1.0 / D, bias=eps_col.v())
        A(AF.Exp, rcol.v(slice(s, s + 1)), tcol.v(slice(s, s + 1)), scale=-0.5)
        A(AF.Copy, xb_.v(), xin.v(s), scale=rcol.v(slice(s, s + 1)))

    def prepB(s):
        xb_ = xnb[s % 2]
        bk = bank()
        bv = bank_bf(bk)
        gstart()
        for c in range(KC):
            TR(bv(c * 128, (c + 1) * 128), xb_.v(slice(c * 128, (c + 1) * 128)), identb.v())
        gend()
        src = View(bk.h[:, :].bitcast(BF16).rearrange("p (c t) -> p c t", c=KC), "psum", bk.base, bk.base + 2048)
        CP("dve", xTn.v(slice(None), slice(s * 128, (s + 1) * 128)), src)

    def load_x(sq_, ti):
        P.dma(xin.v(), ext(x_d[sq_, ti * TT:(ti + 1) * TT, :].rearrange("(s p) d -> p s d", p=128)))

    def load_p(sq_, ti):
        P.dma(pin.v(), ext(p_d[sq_, ti * TT:(ti + 1) * TT, :].rearrange("(s p) d -> p s d", p=128)))

    def x_to_hT():
        for m in range(KC):
            bk = bank()
            gstart()
            for s in range(ST):
                TR(bk.v(slice(s * 128, (s + 1) * 128)), xin.v(s, slice(m * 128, (m + 1) * 128)), ident.v())
            gend()
            CP("act" if m % 2 == 0 else "dve", hT.v(m), bk.v())

    def final_part():
        r = rmsnorm_stats(lambda m: hT.v(m), c1024, KC)
        for m in range(KC):
            STT("dve", hT.v(m), hT.v(m), gfin.v(slice(m, m + 1)), r.v(), ALU.mult, ALU.mult)

    def store_out(sq_, ti, final):
        for s in range(ST):
            for half in range(2):
                bk = bank()
                gstart()
                for mm in range(4):
                    m = half * 4 + mm
                    TR(bk.v(slice(mm * 128, (mm + 1) * 128)), hT.v(m, slice(s * 128, (s + 1) * 128)), ident.v())
                gend()
                CP("act" if half == 0 else "dve", osb.v(s, slice(half * 512, (half + 1) * 512)), bk.v())
        P.dma(View(out_d[sq_, ti * TT:(ti + 1) * TT, :].rearrange("(s p) d -> p s d", p=128), "out", 0, 1), osb.v())

    def mix(sq_, ti):
        S = Sst[sq_]
        norm_to_xT()
        Wa = [wnext(), wnext()]
        Wg = [wnext(), wnext()]
        if ti == 0:
            MS("pool", glu.v(slice(None), slice(0, HALO)), 0.0)
        else:
            CP("pool", glu.v(slice(None), slice(0, HALO)), glu.v(slice(None), slice(TT, TT + HALO)))
        bb = [(cc, bank(), bank()) for cc in range(2)]
        for c in range(KC):
            for (cc, bg, ba) in bb:
                MM(bg.v(), Wg[0].v(c, slice(cc * 128, cc * 128 + 128)), xT.v(c), c == 0, c == KC - 1)
                MM(ba.v(), Wa[0].v(c, slice(cc * 128, cc * 128 + 128)), xT.v(c), c == 0, c == KC - 1)
        for (cc, bg, ba) in bb:
            sg = sgb[cc % 3]
            A(AF.Sigmoid, sg.v(), bg.v())
            TTop("dve", glu.v(cc, slice(HALO, HALO + TT)), sg.v(), ba.v(), ALU.mult)
        for cc in range(2, 4):
            bg = bank()
            for c in range(KC):
                MM(bg.v(), Wg[cc // 2].v(c, slice((cc % 2) * 128, (cc % 2) * 128 + 128)), xT.v(c), c == 0, c == KC - 1)
            ba = bank()
            for c in range(KC):
                MM(ba.v(), Wa[cc // 2].v(c, slice((cc % 2) * 128, (cc % 2) * 128 + 128)), xT.v(c), c == 0, c == KC - 1)
            sg = sgb[cc % 3]
            A(AF.Sigmoid, sg.v(), bg.v())
            TTop("dve", glu.v(cc, slice(HALO, HALO + TT)), sg.v(), ba.v(), ALU.mult)
        wdone(4)
        held = {}
        for cc in range(4):
            bc = bank()
            for j in range(CONVW):
                idx = cc * CONVW + j
                blk = idx // 22
                if blk not in held:
                    held[blk] = wnext()
                MM(bc.v(), held[blk].v(idx % 22), glu.v(cc, slice(j, j + TT)), j == 0, j == CONVW - 1)
                if idx % 22 == 21 or idx == 4 * CONVW - 1:
                    wdone()
            A(AF.Identity, acc.v(cc), bc.v(), bias=cb.v(slice(cc, cc + 1)))

        def conv_emit(n):
            return
        conv_emit(8)
        bk = bank()
        for c in range(KC):
            MM(bk.v(), wglr.v(c), xT.v(c), c == 0, c == KC - 1)
        CP("act", gaug.v(p=slice(0, 16)), bk.v(p=slice(0, 16)))
        for s2 in range(2):
            bpre = bank()
            gstart()
            for s in (2 * s2, 2 * s2 + 1):
                dst = bpre.v(slice((s % 2) * 256, (s % 2) * 256 + 256))
                MM(dst, gaug.v(slice(s * 128, (s + 1) * 128)), waug.v(), True, True)
            gend()
            for s in (2 * s2, 2 * s2 + 1):
                dst = bpre.v(slice((s % 2) * 256, (s % 2) * 256 + 256))
                A(AF.Exp, Lb.v(s), dst, scale=-1.0)
        for s in range(ST):
            A(AF.Ln, Lb.v(s), Lb.v(s), bias=one_col.v())
        bcum = [bank(), bank()]
        for cd in range(2):
            gstart()
            for s in range(ST):
                MM(bcum[cd].v(slice(s * 128, (s + 1) * 128)), Lb.v(s, slice(cd * 128, (cd + 1) * 128)), triT2.v(0), True, True)
            gend()
        for cd in range(2):
            A(AF.Exp, epos.v(cd), bcum[cd].v(), scale=-1.0 / 16)
            A(AF.Exp, eneg.v(cd), bcum[cd].v(), scale=1.0 / 16)
        conv_emit(12)
        Wq = wnext()
        Wk = wnext()
        for cd in range(2):
            bq = bank()
            for c in range(KC):
                MM(bq.v(), Wq.v(c, slice(cd * 128, (cd + 1) * 128)), xT.v(c), c == 0, c == KC - 1)
            STT("dve", qd.v(cd), bq.v(), 0.125, epos.v(cd), ALU.mult, ALU.mult)
            STT("dve", qn.v(cd), bq.v(), 0.125, eneg.v(cd), ALU.mult, ALU.mult)
            bkk = bank()
            for c in range(KC):
                MM(bkk.v(), Wk.v(c, slice(cd * 128, (cd + 1) * 128)), xT.v(c), c == 0, c == KC - 1)
            TTop("dve", kn.v(cd), bkk.v(), eneg.v(cd), ALU.mult)
            TTop("dve", kp.v(cd), bkk.v(), epos.v(cd), ALU.mult)
            conv_emit(6)
        for s in range(ST):
            br = bank()
            MM(br.v(slice(0, 256)), u2.v(0), Lb.v(s), True, True)
            er = eR[s % 2]
            A(AF.Exp, er.v(), br.v(slice(0, 256)), scale=-1.0 / 16)
            bk2_ = bank()
            for c in range(KC):
                MM(bk2_.v(slice(0, 256)), xT.v(c, slice(s * 128, (s + 1) * 128)), Wk.v(c), c == 0, c == KC - 1)
            TTop("dve", kdtok.v(s), bk2_.v(slice(0, 256)), er.v(), ALU.mult)
            conv_emit(3)
        wdone(2)
        Wv = [wnext(), wnext()]
        Wr = [wnext(), wnext()]

        def vtok_proj(s):
            bv = bank()
            gstart()
            for piece in range(2):
                for c in range(KC):
                    MM(bv.v(slice(piece * 256, (piece + 1) * 256)), xT.v(c, slice(s * 128, (s + 1) * 128)), Wv[piece].v(c),
                       c == 0, c == KC - 1)
            gend()
            CP("act", vtok.v(s), bv.v())

        def r_proj(h):
            brr = bank()
            for c in range(KC):
                MM(brr.v(), Wr[h // 2].v(c, slice((h % 2) * 128, (h % 2) * 128 + 128)), xT.v(c), c == 0, c == KC - 1)
            A(AF.Silu, rsil.v(h), brr.v())

        vtok_proj(0)
        for pr in range(2):
            CP("act", Sprev.v(pr, 0), S.v(pr))

        def chain(c8):
            s, cc = c8 // 2, c8 % 2
            rows = slice(cc * 64, (cc + 1) * 64)
            bkv = bank()
            gstart()
            for pr in range(2):
                MM(bkv.v(slice(pr * 256, (pr + 1) * 256)), kdtok.v(s, slice(pr * 128, (pr + 1) * 128), p=rows),
                   vtok.v(s, slice(pr * 256, (pr + 1) * 256), p=rows), True, True)
            gend()
            tl = s * 128 + cc * 64 + 63
            for snap in (True, False):
                if snap and c8 == 7:
                    continue
                for pr in range(2):
                    for e in range(2):
                        pp_ = slice(e * 64, (e + 1) * 64)
                        dst = Sprev.v(pr, c8 + 1, p=pp_) if snap else S.v(pr, p=pp_)
                        STT("dve", dst, S.v(pr, p=pp_), epos.v(pr, slice(tl, tl + 1), p=pp_),
                            bkv.v(slice(pr * 256 + e * 128, pr * 256 + e * 128 + 128), p=pp_), ALU.mult, ALU.add)

        lnst = {}

        def ln_mid(s):
            if s == 0:
                for cc in range(4):
                    CP("act", ymix.v(cc), acc.v(cc))
            elif s == 1:
                for cc in range(4):
                    TTop("dve", acc.v(cc), acc.v(cc), lnst["bm"].v(), ALU.subtract)
                    A(AF.Square, ymix.v(cc), acc.v(cc))
            elif s == 2:
                A(AF.Ln, lnt.v(), lnst["bv"].v(), bias=eps_col.v())
                r_ = rstd_t[rctr[0] % 2]
                rctr[0] += 1
                A(AF.Exp, r_.v(), lnt.v(), scale=-0.5)
                lnst["rc"] = r_
            else:
                for cc in range(4):
                    A(AF.Silu, ymix.v(cc), acc.v(cc), scale=lng.v(slice(cc, cc + 1)), bias=lnb.v(slice(cc, cc + 1)))

        def ln_end(s):
            if s == 0:
                lnst["bm"] = bank()
                for cc in range(4):
                    MM(lnst["bm"].v(), c512.v(), ymix.v(cc), cc == 0, cc == 3)
            elif s == 1:
                lnst["bv"] = bank()
                for cc in range(4):
                    MM(lnst["bv"].v(), c512.v(), ymix.v(cc), cc == 0, cc == 3)
            elif s == 2:
                for cc in range(4):
                    TTop("dve", acc.v(cc), acc.v(cc), lnst["rc"].v(), ALU.mult)

        for s in range(ST):
            ts = slice(s * 128, (s + 1) * 128)
            sc = scT[s % 2]
            chain(2 * s)
            conv_emit(2)
            bse = [bank(), bank()]
            for e in range(2):
                hp = slice(e * 64, (e + 1) * 64)
                gstart()
                for pr in range(2):
                    MM(bse[e].v(slice(pr * 256, pr * 256 + 128)), kn.v(pr, ts, p=hp), qd.v(pr, ts, p=hp), True, True)
                    MM(bse[e].v(slice(pr * 256 + 128, pr * 256 + 256)), kp.v(pr, ts, p=hp), qn.v(pr, ts, p=hp), True, True)
                gend()
            for e in range(2):
                b4 = bse[e].h[:, :].rearrange("p (r l t) -> p r l t", r=2, l=2)
                lo_v = View(b4[:, :, 0, :], "psum", bse[e].base, bse[e].base + 2048)
                up_v = View(b4[:, :, 1, :], "psum", bse[e].base, bse[e].base + 2048)
                t1 = mt1[e]
                t2 = mt2[e]
                TTop("dve", t1.v(), lo_v, triT2.v(), ALU.mult)
                TTop("dve", t2.v(), up_v, u2.v(), ALU.mult)
                TTop("pool", sc.v(slice(None), e), t1.v(), t2.v(), ALU.add)
            ln_mid(s)
            r_proj(s)
            if s + 1 < ST:
                vtok_proj(s + 1)
            chain(2 * s + 1)
            conv_emit(4)
            bo = bank()
            gstart()
            for h in range(4):
                pr, e = h // 2, h % 2
                hp = slice(e * 64, (e + 1) * 64)
                for cc in range(2):
                    c8 = 2 * s + cc
                    cols = slice(h * 128 + cc * 64, h * 128 + cc * 64 + 64)
                    MM(bo.v(cols), vtok.v(s, slice(h * 128, (h + 1) * 128)), sc.v(pr, e, slice(cc * 64, cc * 64 + 64)), True, False)
                    MM(bo.v(cols), Sprev.v(pr, c8, p=hp), qd.v(pr, slice(s * 128 + cc * 64, s * 128 + cc * 64 + 64), p=hp), False, True)
            gend()
            b3 = bo.h[:, :].rearrange("p (h t) -> p h t", h=4)
            bo_v = View(b3, "psum", bo.base, bo.base + 2048)
            A(AF.Copy, o_sb.v(slice(None), ts), bo_v)
            A(AF.Square, osq.v(slice(None), ts), bo_v)
            ln_end(s)
            conv_emit(6)
        wdone(4)
        conv_emit(1000)
        for h in range(4):
            bk2 = bank()
            MM(bk2.v(), c128.v(), osq.v(h), True, True)
            A(AF.Ln, lnt.v(), bk2.v(), bias=eps_col.v())
            r = rstd_t[rctr[0] % 2]
            rctr[0] += 1
            A(AF.Exp, r.v(), lnt.v(), scale=-0.5)
            TTop("pool", rsil.v(h), rsil.v(h), r.v(), ALU.mult)
            STT("dve", ymix.v(4 + h), o_sb.v(h), gnorm.v(slice(h, h + 1)), rsil.v(h), ALU.mult, ALU.mult)
        Wo = [wnext() for _ in range(4)]
        for m in range(KC):
            bk3 = bank()
            for kc in range(8):
                MM(bk3.v(), Wo[m // 2].v(kc, slice((m % 2) * 128, (m % 2) * 128 + 128)), ymix.v(kc), kc == 0, kc == 7)
            TTop("dve", hT.v(m), hT.v(m), bk3.v(), ALU.add)
        wdone(4)

    def ple():
        CP("pool", pbf.v(), pin.v())
        for kc in range(2):
            bk = bank()
            bv = bank_bf(bk)
            gstart()
            for s in range(ST):
                TR(bv(s * 128, (s + 1) * 128), pbf.v(s, slice(kc * 128, (kc + 1) * 128)), identb.v())
            gend()
            CP("act", pT.v(kc), bv(0, 512))
        norm_to_xT()
        Wg = [wnext() for _ in range(4)]
        Wp = wnext()

        def pproj(m):
            bp = bank()
            for kc in range(2):
                MM(bp.v(), Wp.v(kc, slice(m * 128, (m + 1) * 128)), pT.v(kc), kc == 0, kc == 1)
            return bp

        def fin(m, bg, bp):
            sg = sgb[m % 3]
            A(AF.Sigmoid, sg.v(), bg.v())
            TTop("dve", sg.v(), sg.v(), bp.v(), ALU.mult)
            TTop("pool", hT.v(m), hT.v(m), sg.v(), ALU.add)

        bps = [pproj(m) for m in range(4)]
        bgs = [bank() for _ in range(4)]
        for c in range(KC):
            for m in range(4):
                MM(bgs[m].v(), Wg[m // 2].v(c, slice((m % 2) * 128, (m % 2) * 128 + 128)), xT.v(c), c == 0, c == KC - 1)
        for m in range(4):
            fin(m, bgs[m], bps[m])
        for m in range(4, KC):
            bg = bank()
            for c in range(KC):
                MM(bg.v(), Wg[m // 2].v(c, slice((m % 2) * 128, (m % 2) * 128 + 128)), xT.v(c), c == 0, c == KC - 1)
            bp = pproj(m)
            fin(m, bg, bp)
        wdone(5)

    eps_col = sb("eps_col", [128, 1], F32)
    one_col = sb("one_col", [128, 1], F32)
    MS("pool", eps_col.v(), EPS)
    MS("pool", one_col.v(), 1.0)

    NPRE = 6
    assert stop_after is None
    load_x(*tiles[0])
    load_p(*tiles[0])
    for s_ in range(ST):
        prepA(s_)
        prepB(s_)
    x_to_hT()
    ffn_up(range(0, NPRE), xTn)
    for idx, (sq_, ti) in enumerate(tiles):
        has_next = idx + 1 < len(tiles)
        if idx > 0:
            x_to_hT()
        if has_next and idx > 0:
            load_x(*tiles[idx + 1])
        ffn_up(range(NPRE, JF), xTn)
        ffn_down()
        mix(sq_, ti)
        hooks = None
        if has_next and idx > 0:
            hooks = {3: [lambda: prepA(0)], 6: [lambda: prepB(0), lambda: prepA(1)], 9: [lambda: prepB(1), lambda: prepA(2)],
                     12: [lambda: prepB(2), lambda: prepA(3)], 15: [lambda: prepB(3)]}
        norm_to_xT()
        ffn_up_first2(xT)
        ffn_up(range(2, JF), xT, hooks)
        ffn_down()
        ple()
        if has_next:
            load_p(*tiles[idx + 1])
            if idx == 0:
                load_x(*tiles[idx + 1])
                for s_ in range(ST):
                    prepA(s_)
                    prepB(s_)
        final_part()
        if has_next:
            ffn_up(range(0, NPRE), xTn)
        store_out(sq_, ti, True)
    P.emit(st)
    st.close()
    return nc, P

from concourse.bass_utils import run_bass_kernel_spmd

N_CORES = 8
_PROG_CACHE = {}


def kernel(**inputs):
    x = np.ascontiguousarray(np.asarray(inputs["x"], dtype=np.float32))
    p = np.ascontiguousarray(np.asarray(inputs["p"], dtype=np.float32))
    B, T, _ = x.shape
    spc = B // N_CORES
    key = (spc, T)
    if key not in _PROG_CACHE:
        _PROG_CACHE[key] = build_program(spc, T)[0]
    nc = _PROG_CACHE[key]
    wts = {n: np.ascontiguousarray(np.asarray(inputs[n], dtype=np.float32)) for n in WNAMES}
    in_maps = []
    for c in range(N_CORES):
        m = {"x": x[c * spc:(c + 1) * spc], "p": p[0, c * spc:(c + 1) * spc]}
        m.update(wts)
        in_maps.append(m)
    res = run_bass_kernel_spmd(nc, in_maps, core_ids=list(range(N_CORES)))
    out = np.concatenate([np.asarray(r["out"], dtype=np.float32) for r in res.results], axis=0)
    return out
```

```python
import bisect
import numpy as np
import concourse.bass as bass
import concourse.mybir as mybir

F32 = mybir.dt.float32
BF16 = mybir.dt.bfloat16
AF = mybir.ActivationFunctionType
ALU = mybir.AluOpType

ENGS = ("pe", "act", "dve", "pool", "sp")
N_DMA_SEMS = 24


class _Track:
    def __init__(self):
        self.b = [0]
        self.w = [None]
        self.r = [{}]

    def _split(self, x):
        i = bisect.bisect_right(self.b, x) - 1
        if self.b[i] == x:
            return i
        self.b.insert(i + 1, x)
        self.w.insert(i + 1, self.w[i])
        self.r.insert(i + 1, dict(self.r[i]))
        return i + 1

    def access(self, lo, hi, op, write):
        i = self._split(lo)
        j = self._split(hi)
        deps = set()
        for s in range(i, j):
            if self.w[s] is not None:
                deps.add(self.w[s])
            if write:
                deps.update(self.r[s].values())
                self.w[s] = op
                self.r[s] = {}
            else:
                self.r[s][op.eng] = op
        if write and j - i > 4:
            del self.b[i + 1:j]
            del self.w[i + 1:j]
            del self.r[i + 1:j]
        return deps


class View:
    __slots__ = ("ap", "space", "lo", "hi")

    def __init__(self, ap, space, lo, hi):
        self.ap = ap
        self.space = space
        self.lo = lo
        self.hi = hi


class Buf:
    def __init__(self, handle, shape, esize, space, base):
        self.h = handle
        self.shape = tuple(shape)
        self.esize = esize
        self.space = space
        self.base = base
        st = [1] * len(self.shape)
        for k in range(len(self.shape) - 2, 0, -1):
            st[k] = st[k + 1] * self.shape[k + 1]
        self.strides = st

    def v(self, *idx, p=None):
        nfree = len(self.shape) - 1
        idx = list(idx) + [slice(None)] * (nfree - len(idx))
        lo = 0
        hi = 0
        for k, ix in enumerate(idx):
            dim = self.shape[k + 1]
            stv = self.strides[k + 1]
            if isinstance(ix, int):
                a, bnd = ix, ix
            else:
                a = 0 if ix.start is None else ix.start
                e = dim if ix.stop is None else ix.stop
                assert ix.step in (None, 1)
                bnd = e - 1
            assert 0 <= a <= bnd < dim, (self.shape, idx)
            lo += a * stv
            hi += bnd * stv
        ps = slice(None) if p is None else p
        ap = self.h[(ps,) + tuple(idx)]
        return View(ap, self.space, self.base + lo * self.esize, self.base + (hi + 1) * self.esize)


class Op:
    __slots__ = ("eng", "fn", "deps", "sig", "count", "dsem", "dcount", "is_dma", "idx", "redirect", "line", "wl")

    def __init__(self, eng, fn, is_dma):
        self.eng = eng
        self.fn = fn
        self.deps = []
        self.sig = False
        self.count = None
        self.is_dma = is_dma
        self.dsem = None
        self.dcount = None
        self.redirect = None


class Prog:
    def __init__(self, nc, same_sync=("act", "dve", "pool")):
        self.nc = nc
        self.ops = {e: [] for e in ENGS}
        self.trk = {}
        self.same_sync = set(same_sync)
        self.n_dma = {e: 0 for e in ENGS}
        self.sb_off = 16512
        self.sb_cap = 229344
        self.all_ops = 0

    def sbuf(self, name, shape, dtype, at=None):
        es = 2 if dtype == BF16 else 4
        nbytes = int(np.prod(shape[1:])) * es
        if at is None:
            at = (self.sb_off + 63) // 64 * 64
            self.sb_off = at + nbytes
        assert at >= 16512 and at + nbytes <= self.sb_cap, (name, at, nbytes)
        h = self.nc.alloc_sbuf_tensor_at(name, list(shape), dtype, offset=at)
        return Buf(h, shape, es, "sbuf", at)

    def wrap(self, handle, shape, dtype, space):
        es = 2 if dtype == BF16 else 4
        return Buf(handle, shape, es, space, 0)

    def add(self, eng, fn, reads=(), writes=(), dma=False):
        op = Op(eng, fn, dma)
        import sys as _s
        f_ = _s._getframe(1)
        ln_ = []
        while f_ is not None and len(ln_) < 3:
            ln_.append(f_.f_lineno)
            f_ = f_.f_back
        op.line = ln_
        deps = set()
        for v in reads:
            if v.space is None:
                continue
            t = self.trk.setdefault(v.space, _Track())
            lo, hi = v.lo, v.hi
            if v.space == "psum":
                lo, hi = lo // 2048 * 2048, (hi + 2047) // 2048 * 2048
            deps |= t.access(lo, hi, op, False)
        for v in writes:
            if v.space is None:
                continue
            t = self.trk.setdefault(v.space, _Track())
            lo, hi = v.lo, v.hi
            if v.space == "psum":
                lo, hi = lo // 2048 * 2048, (hi + 2047) // 2048 * 2048
            deps |= t.access(lo, hi, op, True)
        deps.discard(op)
        deps = {(d.redirect or d) for d in deps}
        deps.discard(op)
        for d in deps:
            if d.eng == eng and not d.is_dma and eng not in self.same_sync:
                continue
            if not d.is_dma:
                d.sig = True
            op.deps.append(d)
        if dma:
            k = self.n_dma[eng]
            self.n_dma[eng] += 1
            op.dsem = k % N_DMA_SEMS
            op.dcount = 16 * (k // N_DMA_SEMS + 1)
        self.ops[eng].append(op)
        self.all_ops += 1
        return op

    def dma(self, out, in_, eng="sp", **kw):
        return self.add(eng, lambda e: e.dma_start(out=out.ap, in_=in_.ap, **kw),
                        reads=[in_], writes=[out], dma=True)

    def emit(self, stack):
        nc = self.nc
        esem = {e: stack.enter_context(nc.semaphore("s_" + e)) for e in ENGS if e != "sp"}
        dsem = {}
        for e in ENGS:
            if self.n_dma[e]:
                dsem[e] = [stack.enter_context(nc.semaphore("d_%s_%d" % (e, i)))
                           for i in range(min(N_DMA_SEMS, self.n_dma[e]))]
        for e in ENGS:
            c = 0
            for op in self.ops[e]:
                if op.sig and not op.is_dma:
                    c += 1
                    op.count = c
        final_d = {e: {} for e in ENGS}
        for e in ENGS:
            for op in self.ops[e]:
                if op.is_dma:
                    final_d[e][op.dsem] = op.dcount

        def gen(e):
            def body(engobj):
                waited = {}
                for op in self.ops[e]:
                    need = {}
                    for d in op.deps:
                        if d.is_dma:
                            key = ("d", d.eng, d.dsem)
                            val = d.dcount
                        else:
                            key = ("e", d.eng)
                            val = d.count
                        if need.get(key, 0) < val:
                            need[key] = val
                    if op.is_dma and op.dcount > 16:
                        key = ("d", e, op.dsem)
                        need[key] = max(need.get(key, 0), op.dcount - 16)
                    op.wl = []
                    for key, val in need.items():
                        if waited.get(key, 0) >= val:
                            continue
                        op.wl.append((key, val))
                        waited[key] = val
                        sem = esem[key[1]] if key[0] == "e" else dsem[key[1]][key[2]]
                        engobj.wait_ge(sem, val)
                    ins = op.fn(engobj)
                    if op.is_dma:
                        ins.then_inc(dsem[e][op.dsem], 16)
                    elif op.sig:
                        ins.then_inc(esem[e], 1)
                for si, val in final_d[e].items():
                    if waited.get(("d", e, si), 0) < val:
                        engobj.wait_ge(dsem[e][si], val)
            return body

        with nc.Block() as block:
            block.sync(gen("sp"))
            block.tensor(gen("pe"))
            block.scalar(gen("act"))
            block.vector(gen("dve"))
            block.gpsimd(gen("pool"))
from contextlib import ExitStack

D = 1024
KC = 8
DFF = 2816
JF = 22
DPLE = 256
TT = 512
ST = 4
EPS = 1e-6
NSLOT = 6
import os as _os
DBG = _os.environ.get('KDBG', '')
SLOTW = 2816
CONVW = 31
HALO = CONVW - 1

WNAMES = ["ffn1_norm", "ffn1_w_in", "ffn1_w_out", "mix_norm", "w_in", "conv_w", "conv_b",
          "conv_ln_g", "conv_ln_b", "gate_w_up", "gate_b", "gla_norm", "w_out", "ffn2_norm",
          "ffn2_w_in", "ffn2_w_out", "ple_norm", "ple_w_gate", "ple_w_proj", "final_norm"]
WSHAPES = {"ffn1_norm": [1, 1024], "ffn1_w_in": [1, 1024, 5632], "ffn1_w_out": [1, 2816, 1024],
           "mix_norm": [1, 1024], "w_in": [1, 1024, 2576], "conv_w": [1, 31, 512], "conv_b": [1, 512],
           "conv_ln_g": [1, 512], "conv_ln_b": [1, 512], "gate_w_up": [1, 16, 256], "gate_b": [1, 256],
           "gla_norm": [1, 512], "w_out": [1, 1024, 1024], "ffn2_norm": [1, 1024],
           "ffn2_w_in": [1, 1024, 5632], "ffn2_w_out": [1, 2816, 1024], "ple_norm": [1, 1024],
           "ple_w_gate": [1, 1024, 1024], "ple_w_proj": [1, 256, 1024], "final_norm": [1024]}


def build_program(n_seq, seq_len, stop_after=None, same_sync=("act", "dve", "pool"), mix_stop=99):
    nc = bass.Bass("TRN2", target_bir_lowering=False)
    n_tiles = seq_len // TT
    x_d = nc.dram_tensor("x", [n_seq, seq_len, D], F32, kind="ExternalInput").ap()
    p_d = nc.dram_tensor("p", [n_seq, seq_len, DPLE], F32, kind="ExternalInput").ap()
    wd = {n: nc.dram_tensor(n, WSHAPES[n], F32, kind="ExternalInput").ap() for n in WNAMES}
    out_d = nc.dram_tensor("out", [n_seq, seq_len, D], F32, kind="ExternalOutput").ap()
    NBLK = 85
    wsc_d = nc.dram_tensor("wsc", [NBLK, 128, SLOTW], BF16, kind="Internal").ap()

    st = ExitStack()
    P = Prog(nc, same_sync=same_sync)

    def ext(ap):
        return View(ap, None, 0, 0)

    def A(func, out, in_, scale=None, bias=None, eng="act", accum=None):
        rd = [in_]
        kw = {}
        wr = [out]
        if accum is not None:
            kw["accum_out"] = accum.ap
            wr.append(accum)
        if scale is not None:
            if isinstance(scale, View):
                rd.append(scale); kw["scale"] = scale.ap
            else:
                kw["scale"] = float(scale)
        if bias is not None:
            if isinstance(bias, View):
                rd.append(bias); kw["bias"] = bias.ap
            else:
                kw["bias"] = float(bias)
        return P.add("act", lambda e: e.activation(out=out.ap, in_=in_.ap, func=func, **kw), reads=rd, writes=wr)

    def CP(eng, out, in_):
        if eng == "act":
            return A(AF.Copy, out, in_)
        return P.add(eng, lambda e: e.tensor_copy(out.ap, in_.ap), reads=[in_], writes=[out])

    def TTop(eng, out, a, b, op):
        return P.add(eng, lambda e: e.tensor_tensor(out=out.ap, in0=a.ap, in1=b.ap, op=op), reads=[a, b], writes=[out])

    def TS(eng, out, a, s1, s2, op0, op1=None):
        rd = [a]
        v1 = s1
        v2 = s2
        if isinstance(s1, View):
            rd.append(s1); v1 = s1.ap
        if isinstance(s2, View):
            rd.append(s2); v2 = s2.ap
        if op1 is None:
            return P.add(eng, lambda e: e.tensor_scalar(out=out.ap, in0=a.ap, scalar1=v1, scalar2=None, op0=op0), reads=rd, writes=[out])
        return P.add(eng, lambda e: e.tensor_scalar(out=out.ap, in0=a.ap, scalar1=v1, scalar2=v2, op0=op0, op1=op1), reads=rd, writes=[out])

    def STT(eng, out, a, s, b, op0, op1):
        rd = [a, b]
        sv = s
        if isinstance(s, View):
            rd.append(s); sv = s.ap
        return P.add(eng, lambda e: e.scalar_tensor_tensor(out=out.ap, in0=a.ap, scalar=sv, in1=b.ap, op0=op0, op1=op1), reads=rd, writes=[out])

    grp = {}

    def MM(out, lhsT, rhs, start, stop):
        op = P.add("pe", lambda e: e.matmul(out.ap, lhsT=lhsT.ap, rhs=rhs.ap, start=start, stop=stop), reads=[lhsT, rhs], writes=[out])
        key = out.lo
        if start:
            grp[key] = []
        grp.setdefault(key, []).append(op)
        if stop:
            for g_ in grp[key][:-1]:
                g_.redirect = op
            del grp[key]
        if pg["on"]:
            pg["ops"].append(op)
        return op

    pg = {"on": False, "ops": []}

    def TR(out, in_, idn):
        op = P.add("pe", lambda e: e.transpose(out.ap, in_.ap, idn.ap), reads=[in_, idn], writes=[out])
        if pg["on"]:
            pg["ops"].append(op)
        return op

    def gstart():
        pg["on"] = True
        pg["ops"] = []

    def gend():
        ops_ = pg["ops"]
        for o_ in ops_[:-1]:
            o_.redirect = ops_[-1]
        pg["on"] = False
        pg["ops"] = []

    def MS(eng, v, val):
        return P.add(eng, lambda e: e.memset(v.ap, val), writes=[v])

    banks = []
    for i in range(8):
        b = P.wrap(st.enter_context(nc.psum_tensor("ps%d" % i, [128, 512], F32)), [128, 512], F32, "psum")
        b.base = i * 2048
        banks.append(b)
    bctr = [0]

    def bank():
        b = banks[bctr[0] % 8]
        bctr[0] += 1
        return b

    def bank_bf(b):
        ap = b.h[:, :].bitcast(BF16)

        def v(lo, hi):
            return View(ap[:, lo:hi], "psum", b.base + lo * 2, b.base + hi * 2)
        return v

    sb = P.sbuf
    ident = sb("ident", [128, 128], F32)
    identb = sb("identb", [128, 128], BF16)
    c1024 = sb("c1024", [128, 128], BF16)
    c512 = sb("c512", [128, 128], BF16)
    c128 = sb("c128", [128, 128], BF16)
    triT2 = sb("triT2", [128, 2, 128], F32)
    u2 = sb("u2", [128, 2, 128], F32)
    gcols = sb("gcols", [128, 4, 8], F32)
    gfin = sb("gfin", [128, 8], F32)
    cw = sb("cw", [128, 4, CONVW], F32)
    cb = sb("cb", [128, 4], F32)
    lng = sb("lng", [128, 4], F32)
    lnb = sb("lnb", [128, 4], F32)
    gnorm = sb("gnorm", [128, 4], F32)
    waug_f = sb("waug_f", [128, 256], F32)
    waug = sb("waug", [128, 256], BF16)
    wglr_f = sb("wglr_f", [128, 8, 16], F32)
    wglr = sb("wglr", [128, 8, 128], BF16)
    gaug = sb("gaug", [128, TT], BF16)
    Sst = [sb("S%d" % i, [128, 2, 128], F32) for i in range(n_seq)]
    Sprev = sb("Sprev", [128, 2, 8, 128], BF16)
    hT = sb("hT", [128, KC, TT], F32)
    xT = sb("xT", [128, KC, TT], BF16)
    xin = sb("xin", [128, ST, D], F32)
    osb = sb("osb", [128, ST, D], F32, at=xin.base + ST * D * 4)
    stgpad = sb("stgpad", [128, 256], F32, at=xin.base + 2 * ST * D * 4)
    P.sb_off = xin.base + 2 * ST * D * 4 + 1024
    pin = sb("pin", [128, ST, DPLE], F32)
    pbf = sb("pbf", [128, ST, DPLE], BF16)
    pT = sb("pT", [128, 2, TT], BF16)
    wr_addr = []
    wr_in, wr_out, wr_pp = [], [], []
    for i in range(NSLOT):
        b = sb("wr%d" % i, [128, KC, 256], BF16)
        P.sb_off = b.base + SLOTW * 2
        wr_in.append(b)
        wr_out.append(sb("wro%d" % i, [128, JF, 128], BF16, at=b.base))
        wr_pp.append(sb("wrp%d" % i, [128, 2, 1024], BF16, at=b.base))
    sgb = [sb("sg%d" % i, [128, TT], F32) for i in range(3)]
    sqb = [sb("sq%d" % i, [128, TT], BF16) for i in range(3)]
    rstd_t = [sb("rstd%d" % i, [128, TT], F32) for i in range(2)]
    lnt = sb("lnt", [128, TT], F32)
    qd = sb("qd", [128, 2, TT], BF16)
    qn = sb("qn", [128, 2, TT], BF16)
    kn = sb("kn", [128, 2, TT], BF16)
    kp = sb("kp", [128, 2, TT], BF16)
    vtok = sb("vtok", [128, ST, 512], BF16)
    kdtok = sb("kdtok", [128, ST, 256], BF16)
    scT = [sb("scT%d" % i, [128, 2, 2, 128], BF16) for i in range(2)]
    mt1 = [sb("mt1_%d" % i, [128, 2, 128], F32) for i in range(2)]
    mt2 = [sb("mt2_%d" % i, [128, 2, 128], F32) for i in range(2)]
    osq = sb("osq", [128, 4, TT], BF16)
    region0 = (P.sb_off + 63) // 64 * 64
    hid = sb("hid", [128, JF, TT], BF16)
    ymix = sb("ymix", [128, 8, TT], BF16)
    acc = sb("acc", [128, 4, TT], F32)
    glu = sb("glu", [128, 4, HALO + TT], BF16)
    epos = sb("epos", [128, 2, TT], F32)
    eneg = sb("eneg", [128, 2, TT], F32)
    region1 = P.sb_off
    assert qn.base == qd.base + 2048 and kn.base == qd.base + 4096 and kp.base == qd.base + 6144
    xTn = sb("xTn", [128, KC, TT], BF16, at=qd.base)
    xnb = [sb("xnb%d" % i, [128, D], BF16, at=osq.base + i * 2048) for i in range(2)]
    sscol = sb("sscol", [128, ST], F32)
    tcol = sb("tcol", [128, ST], F32)
    rcol = sb("rcol", [128, ST], F32)
    rsil = sb("rsil", [128, 4, TT], F32, at=hid.base)
    o_sb = sb("o_sb", [128, 4, TT], F32, at=hid.base + 8192)
    Lb = sb("Lb", [128, ST, 256], F32, at=hid.base + 16384)
    eR = [sb("eR%d" % i, [128, 256], F32, at=hid.base + 20480 + i * 1024) for i in range(2)]
    assert hid.base + 22528 >= hid.base + 20480 + 2048
    ctmp = [sb("ctmp0", [128, TT], F32, at=pbf.base), sb("ctmp1", [128, TT], F32, at=pT.base)]
    NSTG = 3
    stg_f = []
    a0 = xin.base
    for i in range(NSTG):
        stg_f.append({"in": sb("sfi%d" % i, [128, KC, 256], F32, at=a0),
                      "out": sb("sfo%d" % i, [128, JF, 128], F32, at=a0),
                      "pp": sb("sfp%d" % i, [128, 2, 1024], F32, at=a0)})
        a0 += SLOTW * 4
    assert a0 <= xin.base + 2 * ST * D * 4 + 1024
    print("SBUF used up to", P.sb_off, "cap", P.sb_cap)

    MS("pool", ident.v(), 0.0)
    P.add("pool", lambda e: e.affine_select(out=ident.v().ap, in_=ident.v().ap, pattern=[[-1, 128]],
                                           compare_op=ALU.not_equal, fill=1.0, base=0, channel_multiplier=1),
          reads=[ident.v()], writes=[ident.v()])
    CP("dve", identb.v(), ident.v())
    MS("pool", c1024.v(), 1.0 / 1024)
    MS("pool", c512.v(), 1.0 / 512)
    MS("pool", c128.v(), 1.0 / 128)
    for e in range(2):
        MS("pool", triT2.v(e), 1.0)
        P.add("pool", (lambda e_: lambda g: g.affine_select(out=triT2.v(e_).ap, in_=triT2.v(e_).ap, pattern=[[1, 128]],
                                                            compare_op=ALU.is_ge, fill=0.0, base=0, channel_multiplier=-1))(e),
              reads=[triT2.v(e)], writes=[triT2.v(e)])
        MS("pool", triT2.v(e, slice(64, 128), p=slice(0, 64)), 0.0)
        MS("pool", u2.v(e), 1.0)
        P.add("pool", (lambda e_: lambda g: g.affine_select(out=u2.v(e_).ap, in_=u2.v(e_).ap, pattern=[[-1, 128]],
                                                            compare_op=ALU.is_gt, fill=0.0, base=0, channel_multiplier=1))(e),
              reads=[u2.v(e)], writes=[u2.v(e)])
        MS("pool", u2.v(e, slice(0, 64), p=slice(64, 128)), 0.0)

    def small_dma(out, ap):
        P.add("sp", lambda e: e.dma_start(out=out.ap, in_=ap, allow_slow_non_contiguous=True), writes=[out], dma=True)

    for i, nm in enumerate(["ffn1_norm", "mix_norm", "ffn2_norm", "ple_norm"]):
        small_dma(gcols.v(i), wd[nm].rearrange("o (c p) -> p (o c)", p=128))
    small_dma(gfin.v(), wd["final_norm"].rearrange("(c p) -> p c", p=128))
    for cc_ in range(4):
        small_dma(cw.v(cc_), wd["conv_w"][0][:, cc_ * 128:(cc_ + 1) * 128].rearrange("j p -> p j"))
    small_dma(cb.v(), wd["conv_b"].rearrange("o (c p) -> p (o c)", p=128))
    small_dma(lng.v(), wd["conv_ln_g"].rearrange("o (c p) -> p (o c)", p=128))
    small_dma(lnb.v(), wd["conv_ln_b"].rearrange("o (c p) -> p (o c)", p=128))
    small_dma(gnorm.v(), wd["gla_norm"].rearrange("o (c p) -> p (o c)", p=128))
    MS("pool", waug_f.v(), 0.0)
    P.dma(waug_f.v(p=slice(0, 16)), ext(wd["gate_w_up"][0]))
    P.dma(waug_f.v(p=slice(16, 17)), ext(wd["gate_b"]))
    CP("dve", waug.v(), waug_f.v())
    MS("pool", gaug.v(), 1.0)
    P.dma(wglr_f.v(), ext(wd["w_in"][0][:, 2560:2576].rearrange("(c p) x -> p c x", p=128)))
    MS("pool", wglr.v(), 0.0)
    for c in range(KC):
        TS("dve", wglr.v(c, slice(0, 16)), wglr_f.v(c), gcols.v(1, slice(c, c + 1)), None, ALU.mult)
    for i in range(n_seq):
        MS("pool", Sst[i].v(), 0.0)

    blocks = []

    def rin(w, c0, n):
        return w[:, c0:c0 + n].rearrange("(c p) x -> p c x", p=128)

    def add_ffn(pref, gi):
        wi = wd[pref + "_w_in"][0]
        wo = wd[pref + "_w_out"][0]
        for j in range(JF):
            blocks.append(("in", [(rin(wi, j * 128, 128), 0, 128), (rin(wi, DFF + j * 128, 128), 128, 256)], gi, None))
        for m in range(KC):
            blocks.append(("out", [(rin(wo, m * 128, 128), 0, 128)], None, 0.5))

    add_ffn("ffn1", 0)
    BLK_MIX = len(blocks)
    wm = wd["w_in"][0]
    for c0 in (0, 256, 512, 768):
        blocks.append(("in", [(rin(wm, c0, 256), 0, 256)], 1, None))
    for b_ in range(6):
        blocks.append(("dg", [], b_, None))
    for c0 in (1024, 1280, 1536, 1792, 2048, 2304):
        blocks.append(("in", [(rin(wm, c0, 256), 0, 256)], 1, None))
    for q in range(4):
        blocks.append(("in", [(rin(wd["w_out"][0], q * 256, 256), 0, 256)], None, None))
    BLK_FFN2 = len(blocks)
    add_ffn("ffn2", 2)
    BLK_PLE = len(blocks)
    for q in range(4):
        blocks.append(("in", [(rin(wd["ple_w_gate"][0], q * 256, 256), 0, 256)], 3, None))
    blocks.append(("pp", [(wd["ple_w_proj"][0].rearrange("(c p) x -> p c x", p=128), 0, 1024)], None, None))
    assert len(blocks) == NBLK
    BLKN = {"in": 2048, "out": 2816, "pp": 2048, "dg": 2816}

    def wsc_view(bi, n):
        return View(wsc_d[bi, :, 0:n], "wsc", bi, bi + 1)

    cmap = ["dve", "act", "dve", "act", "dve", "dve", "act", "dve"]

    def cast1(eng, dst, src, scal):
        if eng == "act":
            A(AF.Copy, dst, src, scale=scal)
        elif scal is None:
            CP(eng, dst, src)
        else:
            TS(eng, dst, src, scal, None, ALU.mult)

    def pro_cast_store(bi, slot):
        kind, srcs, gi, cs = blocks[bi]
        n = BLKN[kind]
        flat_src = wr_out[slot] if n > 2048 else wr_in[slot]
        flat = View(flat_src.h[:, :, :].rearrange("p a b -> p (a b)"), "sbuf", wr_in[slot].base, wr_in[slot].base + n * 2)
        if kind == "dg":
            sbb = wr_out[slot]
            for k_ in range(22):
                idx_ = gi * 22 + k_
                if idx_ >= 4 * CONVW:
                    MS("pool", sbb.v(k_), 0.0)
                    continue
                cc_, j_ = idx_ // CONVW, idx_ % CONVW
                cast1("dve" if k_ % 2 else "act", sbb.v(k_), ident.v(), cw.v(cc_, slice(j_, j_ + 1)))
            P.dma(wsc_view(bi, n), flat, eng="act")
            return
        sf = stg_f[pro["nload_idx"][bi] % NSTG][kind]
        sbb = {"in": wr_in, "out": wr_out, "pp": wr_pp}[kind][slot]
        if gi is not None:
            for c in range(KC):
                cast1(cmap[(c + bi) % 8], sbb.v(c), sf.v(c), gcols.v(gi, slice(c, c + 1)))
        else:
            nk = sf.shape[1]
            cuts = [0, (nk * 5) // 9, nk]
            for eng, a_, b_ in zip(("dve", "act"), cuts[:-1], cuts[1:]):
                if b_ > a_:
                    sc_ = cs if cs is not None else (1.0 if eng == "act" else None)
                    cast1(eng, sbb.v(slice(a_, b_)), sf.v(slice(a_, b_)), sc_)
        P.dma(wsc_view(bi, n), flat, eng="act")

    pro = {"loaded": 0, "nload": 0, "nload_idx": {}}

    def pro_prefetch(upto):
        while pro["loaded"] < min(upto, NBLK):
            bi = pro["loaded"]
            if blocks[bi][0] != "dg":
                pro["nload_idx"][bi] = pro["nload"]
                kind, srcs, gi, cs = blocks[bi]
                sf = stg_f[pro["nload"] % NSTG][kind]
                for (ap, lo, hi) in srcs:
                    P.dma(sf.v(slice(None), slice(lo, hi)), ext(ap))
                pro["nload"] += 1
            pro["loaded"] += 1

    tiles = [(sq_, ti) for sq_ in range(n_seq) for ti in range(n_tiles)]
    total_blocks = len(tiles) * NBLK
    wst = {"issued": 0, "consumed": 0, "released": 0}

    def _wissue():
        while wst["issued"] < total_blocks and wst["issued"] - NSLOT < wst["released"]:
            k = wst["issued"]
            bi = k % NBLK
            n = BLKN[blocks[bi][0]]
            slot = k % NSLOT
            if k < NBLK:
                pro_prefetch(k + NSTG)
                pro_cast_store(bi, slot)
            else:
                src_buf = wr_out[slot] if n > 2048 else wr_in[slot]
                dst = View(src_buf.h[:, :, :].rearrange("p a b -> p (a b)"), "sbuf", wr_in[slot].base, wr_in[slot].base + n * 2)
                P.dma(dst, wsc_view(bi, n))
            wst["issued"] += 1

    def wnext():
        _wissue()
        k = wst["consumed"]
        assert k < wst["issued"], "weight ring too small for the number of blocks held"
        wst["consumed"] += 1
        kind = blocks[k % NBLK][0]
        slot = k % NSLOT
        return {"in": wr_in, "out": wr_out, "pp": wr_pp, "dg": wr_out}[kind][slot]

    def wdone(n=1):
        wst["released"] += n
        assert wst["released"] <= wst["consumed"]
        _wissue()

    rctr = [0]

    def rmsnorm_stats(src, cmat, nchunks, eps_val=EPS, sqdst=None):
        bk = bank()
        for m in range(nchunks):
            sq = sqb[m % 3].v() if sqdst is None else sqdst(m)
            A(AF.Square, sq, src(m))
            MM(bk.v(), cmat.v(), sq, m == 0, m == nchunks - 1)
        A(AF.Ln, lnt.v(), bk.v(), bias=eps_col.v())
        r = rstd_t[rctr[0] % 2]
        rctr[0] += 1
        A(AF.Exp, r.v(), lnt.v(), scale=-0.5)
        return r

    def norm_to_xT():
        r = rmsnorm_stats(lambda m: hT.v(m), c1024, KC, sqdst=lambda m: xT.v(m))
        for m in range(KC):
            TTop("dve", xT.v(m), hT.v(m), r.v(), ALU.mult)

    def ffn_block(j, xsrc):
        W = wnext()
        bg = bank()
        for c in range(KC):
            MM(bg.v(), W.v(c, slice(0, 128)), xsrc.v(c), c == 0, c == KC - 1)
        bu = bank()
        for c in range(KC):
            MM(bu.v(), W.v(c, slice(128, 256)), xsrc.v(c), c == 0, c == KC - 1)
        sg = sgb[j % 3]
        A(AF.Silu, sg.v(), bg.v())
        TTop("dve", hid.v(j), sg.v(), bu.v(), ALU.mult)
        wdone()

    def ffn_up(js, xsrc, hooks=None):
        for j in js:
            ffn_block(j, xsrc)
            if hooks and j in hooks:
                for h_ in hooks[j]:
                    h_()

    def ffn_up_first2(xsrc):
        bb = [(wnext(), bank(), bank()), (wnext(), bank(), bank())]
        for c in range(KC):
            for (W, bg, bu) in bb:
                MM(bg.v(), W.v(c, slice(0, 128)), xsrc.v(c), c == 0, c == KC - 1)
                MM(bu.v(), W.v(c, slice(128, 256)), xsrc.v(c), c == 0, c == KC - 1)
        for jj, (W, bg, bu) in enumerate(bb):
            sg = sgb[jj % 3]
            A(AF.Silu, sg.v(), bg.v())
            TTop("dve", hid.v(jj), sg.v(), bu.v(), ALU.mult)
        wdone(2)

    def ffn_down():
        for m in range(KC):
            W = wnext()
            bk = bank()
            for j in range(JF):
                MM(bk.v(), W.v(j), hid.v(j), j == 0, j == JF - 1)
            TTop("dve", hT.v(m), hT.v(m), bk.v(), ALU.add)
            wdone()

    def prepA(s):
        xb_ = xnb[s % 2]
        A(AF.Square, xb_.v(), xin.v(s), accum=sscol.v(slice(s, s + 1)))
        A(AF.Ln, tcol.v(slice(s, s + 1)), sscol.v(slice(s, s + 1)), scale=1.0 / D, bias=eps_col.v())
        A(AF.Exp, rcol.v(slice(s, s + 1)), tcol.v(slice(s, s + 1)), scale=-0.5)
        A(AF.Copy, xb_.v(), xin.v(s), scale=rcol.v(slice(s, s + 1)))

    def prepB(s):
        xb_ = xnb[s % 2]
        bk = bank()
        bv = bank_bf(bk)
        gstart()
        for c in range(KC):
            TR(bv(c * 128, (c + 1) * 128), xb_.v(slice(c * 128, (c + 1) * 128)), identb.v())
        gend()
        src = View(bk.h[:, :].bitcast(BF16).rearrange("p (c t) -> p c t", c=KC), "psum", bk.base, bk.base + 2048)
        CP("dve", xTn.v(slice(None), slice(s * 128, (s + 1) * 128)), src)

    def load_x(sq_, ti):
        P.dma(xin.v(), ext(x_d[sq_, ti * TT:(ti + 1) * TT, :].rearrange("(s p) d -> p s d", p=128)))

    def load_p(sq_, ti):
        P.dma(pin.v(), ext(p_d[sq_, ti * TT:(ti + 1) * TT, :].rearrange("(s p) d -> p s d", p=128)))

    def x_to_hT():
        for m in range(KC):
            bk = bank()
            gstart()
            for s in range(ST):
                TR(bk.v(slice(s * 128, (s + 1) * 128)), xin.v(s, slice(m * 128, (m + 1) * 128)), ident.v())
            gend()
            CP("act" if m % 2 == 0 else "dve", hT.v(m), bk.v())

    def final_part():
        r = rmsnorm_stats(lambda m: hT.v(m), c1024, KC)
        for m in range(KC):
            STT("dve", hT.v(m), hT.v(m), gfin.v(slice(m, m + 1)), r.v(), ALU.mult, ALU.mult)

    def store_out(sq_, ti, final):
        for s in range(ST):
            for half in range(2):
                bk = bank()
                gstart()
                for mm in range(4):
                    m = half * 4 + mm
                    TR(bk.v(slice(mm * 128, (mm + 1) * 128)), hT.v(m, slice(s * 128, (s + 1) * 128)), ident.v())
                gend()
                CP("act" if half == 0 else "dve", osb.v(s, slice(half * 512, (half + 1) * 512)), bk.v())
        P.dma(View(out_d[sq_, ti * TT:(ti + 1) * TT, :].rearrange("(s p) d -> p s d", p=128), "out", 0, 1), osb.v())

    def mix(sq_, ti):
        S = Sst[sq_]
        norm_to_xT()
        Wa = [wnext(), wnext()]
        Wg = [wnext(), wnext()]
        if ti == 0:
            MS("pool", glu.v(slice(None), slice(0, HALO)), 0.0)
        else:
            CP("pool", glu.v(slice(None), slice(0, HALO)), glu.v(slice(None), slice(TT, TT + HALO)))
        bb = [(cc, bank(), bank()) for cc in range(2)]
        for c in range(KC):
            for (cc, bg, ba) in bb:
                MM(bg.v(), Wg[0].v(c, slice(cc * 128, cc * 128 + 128)), xT.v(c), c == 0, c == KC - 1)
                MM(ba.v(), Wa[0].v(c, slice(cc * 128, cc * 128 + 128)), xT.v(c), c == 0, c == KC - 1)
        for (cc, bg, ba) in bb:
            sg = sgb[cc % 3]
            A(AF.Sigmoid, sg.v(), bg.v())
            TTop("dve", glu.v(cc, slice(HALO, HALO + TT)), sg.v(), ba.v(), ALU.mult)
        for cc in range(2, 4):
            bg = bank()
            for c in range(KC):
                MM(bg.v(), Wg[cc // 2].v(c, slice((cc % 2) * 128, (cc % 2) * 128 + 128)), xT.v(c), c == 0, c == KC - 1)
            ba = bank()
            for c in range(KC):
                MM(ba.v(), Wa[cc // 2].v(c, slice((cc % 2) * 128, (cc % 2) * 128 + 128)), xT.v(c), c == 0, c == KC - 1)
            sg = sgb[cc % 3]
            A(AF.Sigmoid, sg.v(), bg.v())
            TTop("dve", glu.v(cc, slice(HALO, HALO + TT)), sg.v(), ba.v(), ALU.mult)
        wdone(4)
        held = {}
        for cc in range(4):
            bc = bank()
            for j in range(CONVW):
                idx = cc * CONVW + j
                blk = idx // 22
                if blk not in held:
                    held[blk] = wnext()
                MM(bc.v(), held[blk].v(idx % 22), glu.v(cc, slice(j, j + TT)), j == 0, j == CONVW - 1)
                if idx % 22 == 21 or idx == 4 * CONVW - 1:
                    wdone()
            A(AF.Identity, acc.v(cc), bc.v(), bias=cb.v(slice(cc, cc + 1)))

        def conv_emit(n):
            return
        conv_emit(8)
        bk = bank()
        for c in range(KC):
            MM(bk.v(), wglr.v(c), xT.v(c), c == 0, c == KC - 1)
        CP("act", gaug.v(p=slice(0, 16)), bk.v(p=slice(0, 16)))
        for s2 in range(2):
            bpre = bank()
            gstart()
            for s in (2 * s2, 2 * s2 + 1):
                dst = bpre.v(slice((s % 2) * 256, (s % 2) * 256 + 256))
                MM(dst, gaug.v(slice(s * 128, (s + 1) * 128)), waug.v(), True, True)
            gend()
            for s in (2 * s2, 2 * s2 + 1):
                dst = bpre.v(slice((s % 2) * 256, (s % 2) * 256 + 256))
                A(AF.Exp, Lb.v(s), dst, scale=-1.0)
        for s in range(ST):
            A(AF.Ln, Lb.v(s), Lb.v(s), bias=one_col.v())
        bcum = [bank(), bank()]
        for cd in range(2):
            gstart()
            for s in range(ST):
                MM(bcum[cd].v(slice(s * 128, (s + 1) * 128)), Lb.v(s, slice(cd * 128, (cd + 1) * 128)), triT2.v(0), True, True)
            gend()
        for cd in range(2):
            A(AF.Exp, epos.v(cd), bcum[cd].v(), scale=-1.0 / 16)
            A(AF.Exp, eneg.v(cd), bcum[cd].v(), scale=1.0 / 16)
        conv_emit(12)
        Wq = wnext()
        Wk = wnext()
        for cd in range(2):
            bq = bank()
            for c in range(KC):
                MM(bq.v(), Wq.v(c, slice(cd * 128, (cd + 1) * 128)), xT.v(c), c == 0, c == KC - 1)
            STT("dve", qd.v(cd), bq.v(), 0.125, epos.v(cd), ALU.mult, ALU.mult)
            STT("dve", qn.v(cd), bq.v(), 0.125, eneg.v(cd), ALU.mult, ALU.mult)
            bkk = bank()
            for c in range(KC):
                MM(bkk.v(), Wk.v(c, slice(cd * 128, (cd + 1) * 128)), xT.v(c), c == 0, c == KC - 1)
            TTop("dve", kn.v(cd), bkk.v(), eneg.v(cd), ALU.mult)
            TTop("dve", kp.v(cd), bkk.v(), epos.v(cd), ALU.mult)
            conv_emit(6)
        for s in range(ST):
            br = bank()
            MM(br.v(slice(0, 256)), u2.v(0), Lb.v(s), True, True)
            er = eR[s % 2]
            A(AF.Exp, er.v(), br.v(slice(0, 256)), scale=-1.0 / 16)
            bk2_ = bank()
            for c in range(KC):
                MM(bk2_.v(slice(0, 256)), xT.v(c, slice(s * 128, (s + 1) * 128)), Wk.v(c), c == 0, c == KC - 1)
            TTop("dve", kdtok.v(s), bk2_.v(slice(0, 256)), er.v(), ALU.mult)
            conv_emit(3)
        wdone(2)
        Wv = [wnext(), wnext()]
        Wr = [wnext(), wnext()]

        def vtok_proj(s):
            bv = bank()
            gstart()
            for piece in range(2):
                for c in range(KC):
                    MM(bv.v(slice(piece * 256, (piece + 1) * 256)), xT.v(c, slice(s * 128, (s + 1) * 128)), Wv[piece].v(c),
                       c == 0, c == KC - 1)
            gend()
            CP("act", vtok.v(s), bv.v())

        def r_proj(h):
            brr = bank()
            for c in range(KC):
                MM(brr.v(), Wr[h // 2].v(c, slice((h % 2) * 128, (h % 2) * 128 + 128)), xT.v(c), c == 0, c == KC - 1)
            A(AF.Silu, rsil.v(h), brr.v())

        vtok_proj(0)
        for pr in range(2):
            CP("act", Sprev.v(pr, 0), S.v(pr))

        def chain(c8):
            s, cc = c8 // 2, c8 % 2
            rows = slice(cc * 64, (cc + 1) * 64)
            bkv = bank()
            gstart()
            for pr in range(2):
                MM(bkv.v(slice(pr * 256, (pr + 1) * 256)), kdtok.v(s, slice(pr * 128, (pr + 1) * 128), p=rows),
                   vtok.v(s, slice(pr * 256, (pr + 1) * 256), p=rows), True, True)
            gend()
            tl = s * 128 + cc * 64 + 63
            for snap in (True, False):
                if snap and c8 == 7:
                    continue
                for pr in range(2):
                    for e in range(2):
                        pp_ = slice(e * 64, (e + 1) * 64)
                        dst = Sprev.v(pr, c8 + 1, p=pp_) if snap else S.v(pr, p=pp_)
                        STT("dve", dst, S.v(pr, p=pp_), epos.v(pr, slice(tl, tl + 1), p=pp_),
                            bkv.v(slice(pr * 256 + e * 128, pr * 256 + e * 128 + 128), p=pp_), ALU.mult, ALU.add)

        lnst = {}

        def ln_mid(s):
            if s == 0:
                for cc in range(4):
                    CP("act", ymix.v(cc), acc.v(cc))
            elif s == 1:
                for cc in range(4):
                    TTop("dve", acc.v(cc), acc.v(cc), lnst["bm"].v(), ALU.subtract)
                    A(AF.Square, ymix.v(cc), acc.v(cc))
            elif s == 2:
                A(AF.Ln, lnt.v(), lnst["bv"].v(), bias=eps_col.v())
                r_ = rstd_t[rctr[0] % 2]
                rctr[0] += 1
                A(AF.Exp, r_.v(), lnt.v(), scale=-0.5)
                lnst["rc"] = r_
            else:
                for cc in range(4):
                    A(AF.Silu, ymix.v(cc), acc.v(cc), scale=lng.v(slice(cc, cc + 1)), bias=lnb.v(slice(cc, cc + 1)))

        def ln_end(s):
            if s == 0:
                lnst["bm"] = bank()
                for cc in range(4):
                    MM(lnst["bm"].v(), c512.v(), ymix.v(cc), cc == 0, cc == 3)
            elif s == 1:
                lnst["bv"] = bank()
                for cc in range(4):
                    MM(lnst["bv"].v(), c512.v(), ymix.v(cc), cc == 0, cc == 3)
            elif s == 2:
                for cc in range(4):
                    TTop("dve", acc.v(cc), acc.v(cc), lnst["rc"].v(), ALU.mult)

        for s in range(ST):
            ts = slice(s * 128, (s + 1) * 128)
            sc = scT[s % 2]
            chain(2 * s)
            conv_emit(2)
            bse = [bank(), bank()]
            for e in range(2):
                hp = slice(e * 64, (e + 1) * 64)
                gstart()
                for pr in range(2):
                    MM(bse[e].v(slice(pr * 256, pr * 256 + 128)), kn.v(pr, ts, p=hp), qd.v(pr, ts, p=hp), True, True)
                    MM(bse[e].v(slice(pr * 256 + 128, pr * 256 + 256)), kp.v(pr, ts, p=hp), qn.v(pr, ts, p=hp), True, True)
                gend()
            for e in range(2):
                b4 = bse[e].h[:, :].rearrange("p (r l t) -> p r l t", r=2, l=2)
                lo_v = View(b4[:, :, 0, :], "psum", bse[e].base, bse[e].base + 2048)
                up_v = View(b4[:, :, 1, :], "psum", bse[e].base, bse[e].base + 2048)
                t1 = mt1[e]
                t2 = mt2[e]
                TTop("dve", t1.v(), lo_v, triT2.v(), ALU.mult)
                TTop("dve", t2.v(), up_v, u2.v(), ALU.mult)
                TTop("pool", sc.v(slice(None), e), t1.v(), t2.v(), ALU.add)
            ln_mid(s)
            r_proj(s)
            if s + 1 < ST:
                vtok_proj(s + 1)
            chain(2 * s + 1)
            conv_emit(4)
            bo = bank()
            gstart()
            for h in range(4):
                pr, e = h // 2, h % 2
                hp = slice(e * 64, (e + 1) * 64)
                for cc in range(2):
                    c8 = 2 * s + cc
                    cols = slice(h * 128 + cc * 64, h * 128 + cc * 64 + 64)
                    MM(bo.v(cols), vtok.v(s, slice(h * 128, (h + 1) * 128)), sc.v(pr, e, slice(cc * 64, cc * 64 + 64)), True, False)
                    MM(bo.v(cols), Sprev.v(pr, c8, p=hp), qd.v(pr, slice(s * 128 + cc * 64, s * 128 + cc * 64 + 64), p=hp), False, True)
            gend()
            b3 = bo.h[:, :].rearrange("p (h t) -> p h t", h=4)
            bo_v = View(b3, "psum", bo.base, bo.base + 2048)
            A(AF.Copy, o_sb.v(slice(None), ts), bo_v)
            A(AF.Square, osq.v(slice(None), ts), bo_v)
            ln_end(s)
            conv_emit(6)
        wdone(4)
        conv_emit(1000)
        for h in range(4):
            bk2 = bank()
            MM(bk2.v(), c128.v(), osq.v(h), True, True)
            A(AF.Ln, lnt.v(), bk2.v(), bias=eps_col.v())
            r = rstd_t[rctr[0] % 2]
            rctr[0] += 1
            A(AF.Exp, r.v(), lnt.v(), scale=-0.5)
            TTop("pool", rsil.v(h), rsil.v(h), r.v(), ALU.mult)
            STT("dve", ymix.v(4 + h), o_sb.v(h), gnorm.v(slice(h, h + 1)), rsil.v(h), ALU.mult, ALU.mult)
        Wo = [wnext() for _ in range(4)]
        bks = [bank() for _ in range(4)]
        for m in range(4):
            for kc in range(4):
                MM(bks[m].v(), Wo[m // 2].v(kc, slice((m % 2) * 128, (m % 2) * 128 + 128)), ymix.v(kc), kc == 0, False)
        for m in range(4):
            for kc in range(4, 8):
                MM(bks[m].v(), Wo[m // 2].v(kc, slice((m % 2) * 128, (m % 2) * 128 + 128)), ymix.v(kc), False, kc == 7)
            TTop("dve", hT.v(m), hT.v(m), bks[m].v(), ALU.add)
        for m in range(4, KC):
            bk3 = bank()
            for kc in range(8):
                MM(bk3.v(), Wo[m // 2].v(kc, slice((m % 2) * 128, (m % 2) * 128 + 128)), ymix.v(kc), kc == 0, kc == 7)
            TTop("dve", hT.v(m), hT.v(m), bk3.v(), ALU.add)
        wdone(4)

    def ple():
        CP("pool", pbf.v(), pin.v())
        for kc in range(2):
            bk = bank()
            bv = bank_bf(bk)
            gstart()
            for s in range(ST):
                TR(bv(s * 128, (s + 1) * 128), pbf.v(s, slice(kc * 128, (kc + 1) * 128)), identb.v())
            gend()
            CP("act", pT.v(kc), bv(0, 512))
        norm_to_xT()
        Wg = [wnext() for _ in range(4)]
        Wp = wnext()

        def pproj(m):
            bp = bank()
            for kc in range(2):
                MM(bp.v(), Wp.v(kc, slice(m * 128, (m + 1) * 128)), pT.v(kc), kc == 0, kc == 1)
            return bp

        def fin(m, bg, bp):
            sg = sgb[m % 3]
            A(AF.Sigmoid, sg.v(), bg.v())
            TTop("dve", sg.v(), sg.v(), bp.v(), ALU.mult)
            TTop("pool", hT.v(m), hT.v(m), sg.v(), ALU.add)

        bps = [pproj(m) for m in range(4)]
        bgs = [bank() for _ in range(4)]
        for c in range(KC):
            for m in range(4):
                MM(bgs[m].v(), Wg[m // 2].v(c, slice((m % 2) * 128, (m % 2) * 128 + 128)), xT.v(c), c == 0, c == KC - 1)
        for m in range(4):
            fin(m, bgs[m], bps[m])
        for m in range(4, KC):
            bg = bank()
            for c in range(KC):
                MM(bg.v(), Wg[m // 2].v(c, slice((m % 2) * 128, (m % 2) * 128 + 128)), xT.v(c), c == 0, c == KC - 1)
            bp = pproj(m)
            fin(m, bg, bp)
        wdone(5)

    eps_col = sb("eps_col", [128, 1], F32)
    one_col = sb("one_col", [128, 1], F32)
    MS("pool", eps_col.v(), EPS)
    MS("pool", one_col.v(), 1.0)

    NPRE = 6
    assert stop_after is None
    load_x(*tiles[0])
    load_p(*tiles[0])
    for s_ in range(ST):
        prepA(s_)
        prepB(s_)
    x_to_hT()
    ffn_up(range(0, NPRE), xTn)
    for idx, (sq_, ti) in enumerate(tiles):
        has_next = idx + 1 < len(tiles)
        if idx > 0:
            x_to_hT()
        if has_next and idx > 0:
            load_x(*tiles[idx + 1])
        ffn_up(range(NPRE, JF), xTn)
        ffn_down()
        mix(sq_, ti)
        hooks = None
        if has_next and idx > 0:
            hooks = {3: [lambda: prepA(0)], 6: [lambda: prepB(0), lambda: prepA(1)], 9: [lambda: prepB(1), lambda: prepA(2)],
                     12: [lambda: prepB(2), lambda: prepA(3)], 15: [lambda: prepB(3)]}
        norm_to_xT()
        ffn_up_first2(xT)
        ffn_up(range(2, JF), xT, hooks)
        ffn_down()
        ple()
        if has_next:
            load_p(*tiles[idx + 1])
            if idx == 0:
                load_x(*tiles[idx + 1])
                for s_ in range(ST):
                    prepA(s_)
                    prepB(s_)
        final_part()
        if has_next:
            ffn_up(range(0, NPRE), xTn)
        store_out(sq_, ti, True)
    P.emit(st)
    st.close()
    return nc, P

from concourse.bass_utils import run_bass_kernel_spmd

N_CORES = 8
_PROG_CACHE = {}


def kernel(**inputs):
    x = np.ascontiguousarray(np.asarray(inputs["x"], dtype=np.float32))
    p = np.ascontiguousarray(np.asarray(inputs["p"], dtype=np.float32))
    B, T, _ = x.shape
    spc = B // N_CORES
    key = (spc, T)
    if key not in _PROG_CACHE:
        _PROG_CACHE[key] = build_program(spc, T)[0]
    nc = _PROG_CACHE[key]
    wts = {n: np.ascontiguousarray(np.asarray(inputs[n], dtype=np.float32)) for n in WNAMES}
    in_maps = []
    for c in range(N_CORES):
        m = {"x": x[c * spc:(c + 1) * spc], "p": p[0, c * spc:(c + 1) * spc]}
        m.update(wts)
        in_maps.append(m)
    res = run_bass_kernel_spmd(nc, in_maps, core_ids=list(range(N_CORES)))
    out = np.concatenate([np.asarray(r["out"], dtype=np.float32) for r in res.results], axis=0)
    return out
```

```python
import bisect
import numpy as np
import concourse.bass as bass
import concourse.mybir as mybir

F32 = mybir.dt.float32
BF16 = mybir.dt.bfloat16
AF = mybir.ActivationFunctionType
ALU = mybir.AluOpType

ENGS = ("pe", "act", "dve", "pool", "sp")
N_DMA_SEMS = 24


class _Track:
    def __init__(self):
        self.b = [0]
        self.w = [None]
        self.r = [{}]

    def _split(self, x):
        i = bisect.bisect_right(self.b, x) - 1
        if self.b[i] == x:
            return i
        self.b.insert(i + 1, x)
        self.w.insert(i + 1, self.w[i])
        self.r.insert(i + 1, dict(self.r[i]))
        return i + 1

    def access(self, lo, hi, op, write):
        i = self._split(lo)
        j = self._split(hi)
        deps = set()
        for s in range(i, j):
            if self.w[s] is not None:
                deps.add(self.w[s])
            if write:
                deps.update(self.r[s].values())
                self.w[s] = op
                self.r[s] = {}
            else:
                self.r[s][op.eng] = op
        if write and j - i > 4:
            del self.b[i + 1:j]
            del self.w[i + 1:j]
            del self.r[i + 1:j]
        return deps


class View:
    __slots__ = ("ap", "space", "lo", "hi")

    def __init__(self, ap, space, lo, hi):
        self.ap = ap
        self.space = space
        self.lo = lo
        self.hi = hi


class Buf:
    def __init__(self, handle, shape, esize, space, base):
        self.h = handle
        self.shape = tuple(shape)
        self.esize = esize
        self.space = space
        self.base = base
        st = [1] * len(self.shape)
        for k in range(len(self.shape) - 2, 0, -1):
            st[k] = st[k + 1] * self.shape[k + 1]
        self.strides = st

    def v(self, *idx, p=None):
        nfree = len(self.shape) - 1
        idx = list(idx) + [slice(None)] * (nfree - len(idx))
        lo = 0
        hi = 0
        for k, ix in enumerate(idx):
            dim = self.shape[k + 1]
            stv = self.strides[k + 1]
            if isinstance(ix, int):
                a, bnd = ix, ix
            else:
                a = 0 if ix.start is None else ix.start
                e = dim if ix.stop is None else ix.stop
                assert ix.step in (None, 1)
                bnd = e - 1
            assert 0 <= a <= bnd < dim, (self.shape, idx)
            lo += a * stv
            hi += bnd * stv
        ps = slice(None) if p is None else p
        ap = self.h[(ps,) + tuple(idx)]
        return View(ap, self.space, self.base + lo * self.esize, self.base + (hi + 1) * self.esize)


class Op:
    __slots__ = ("eng", "fn", "deps", "sig", "count", "dsem", "dcount", "is_dma", "idx", "redirect", "line", "wl")

    def __init__(self, eng, fn, is_dma):
        self.eng = eng
        self.fn = fn
        self.deps = []
        self.sig = False
        self.count = None
        self.is_dma = is_dma
        self.dsem = None
        self.dcount = None
        self.redirect = None


class Prog:
    def __init__(self, nc, same_sync=("act", "dve", "pool")):
        self.nc = nc
        self.ops = {e: [] for e in ENGS}
        self.trk = {}
        self.same_sync = set(same_sync)
        self.n_dma = {e: 0 for e in ENGS}
        self.sb_off = 16512
        self.sb_cap = 229344
        self.all_ops = 0

    def sbuf(self, name, shape, dtype, at=None):
        es = 2 if dtype == BF16 else 4
        nbytes = int(np.prod(shape[1:])) * es
        if at is None:
            at = (self.sb_off + 63) // 64 * 64
            self.sb_off = at + nbytes
        assert at >= 16512 and at + nbytes <= self.sb_cap, (name, at, nbytes)
        h = self.nc.alloc_sbuf_tensor_at(name, list(shape), dtype, offset=at)
        return Buf(h, shape, es, "sbuf", at)

    def wrap(self, handle, shape, dtype, space):
        es = 2 if dtype == BF16 else 4
        return Buf(handle, shape, es, space, 0)

    def add(self, eng, fn, reads=(), writes=(), dma=False):
        op = Op(eng, fn, dma)
        import sys as _s
        f_ = _s._getframe(1)
        ln_ = []
        while f_ is not None and len(ln_) < 3:
            ln_.append(f_.f_lineno)
            f_ = f_.f_back
        op.line = ln_
        deps = set()
        for v in reads:
            if v.space is None:
                continue
            t = self.trk.setdefault(v.space, _Track())
            lo, hi = v.lo, v.hi
            if v.space == "psum":
                lo, hi = lo // 2048 * 2048, (hi + 2047) // 2048 * 2048
            deps |= t.access(lo, hi, op, False)
        for v in writes:
            if v.space is None:
                continue
            t = self.trk.setdefault(v.space, _Track())
            lo, hi = v.lo, v.hi
            if v.space == "psum":
                lo, hi = lo // 2048 * 2048, (hi + 2047) // 2048 * 2048
            deps |= t.access(lo, hi, op, True)
        deps.discard(op)
        deps = {(d.redirect or d) for d in deps}
        deps.discard(op)
        for d in deps:
            if d.eng == eng and not d.is_dma and eng not in self.same_sync:
                continue
            if not d.is_dma:
                d.sig = True
            op.deps.append(d)
        if dma:
            k = self.n_dma[eng]
            self.n_dma[eng] += 1
            op.dsem = k % N_DMA_SEMS
            op.dcount = 16 * (k // N_DMA_SEMS + 1)
        self.ops[eng].append(op)
        self.all_ops += 1
        return op

    def dma(self, out, in_, eng="sp", **kw):
        return self.add(eng, lambda e: e.dma_start(out=out.ap, in_=in_.ap, **kw),
                        reads=[in_], writes=[out], dma=True)

    def emit(self, stack):
        nc = self.nc
        esem = {e: stack.enter_context(nc.semaphore("s_" + e)) for e in ENGS if e != "sp"}
        dsem = {}
        for e in ENGS:
            if self.n_dma[e]:
                dsem[e] = [stack.enter_context(nc.semaphore("d_%s_%d" % (e, i)))
                           for i in range(min(N_DMA_SEMS, self.n_dma[e]))]
        for e in ENGS:
            c = 0
            for op in self.ops[e]:
                if op.sig and not op.is_dma:
                    c += 1
                    op.count = c
        final_d = {e: {} for e in ENGS}
        for e in ENGS:
            for op in self.ops[e]:
                if op.is_dma:
                    final_d[e][op.dsem] = op.dcount

        def gen(e):
            def body(engobj):
                waited = {}
                for op in self.ops[e]:
                    need = {}
                    for d in op.deps:
                        if d.is_dma:
                            key = ("d", d.eng, d.dsem)
                            val = d.dcount
                        else:
                            key = ("e", d.eng)
                            val = d.count
                        if need.get(key, 0) < val:
                            need[key] = val
                    if op.is_dma and op.dcount > 16:
                        key = ("d", e, op.dsem)
                        need[key] = max(need.get(key, 0), op.dcount - 16)
                    op.wl = []
                    for key, val in need.items():
                        if waited.get(key, 0) >= val:
                            continue
                        op.wl.append((key, val))
                        waited[key] = val
                        sem = esem[key[1]] if key[0] == "e" else dsem[key[1]][key[2]]
                        engobj.wait_ge(sem, val)
                    ins = op.fn(engobj)
                    if op.is_dma:
                        ins.then_inc(dsem[e][op.dsem], 16)
                    elif op.sig:
                        ins.then_inc(esem[e], 1)
                for si, val in final_d[e].items():
                    if waited.get(("d", e, si), 0) < val:
                        engobj.wait_ge(dsem[e][si], val)
            return body

        with nc.Block() as block:
            block.sync(gen("sp"))
            block.tensor(gen("pe"))
            block.scalar(gen("act"))
            block.vector(gen("dve"))
            block.gpsimd(gen("pool"))
from contextlib import ExitStack

D = 1024
KC = 8
DFF = 2816
JF = 22
DPLE = 256
TT = 512
ST = 4
EPS = 1e-6
NSLOT = 6
import os as _os
DBG = _os.environ.get('KDBG', '')
SLOTW = 2816
CONVW = 31
HALO = CONVW - 1

WNAMES = ["ffn1_norm", "ffn1_w_in", "ffn1_w_out", "mix_norm", "w_in", "conv_w", "conv_b",
          "conv_ln_g", "conv_ln_b", "gate_w_up", "gate_b", "gla_norm", "w_out", "ffn2_norm",
          "ffn2_w_in", "ffn2_w_out", "ple_norm", "ple_w_gate", "ple_w_proj", "final_norm"]
WSHAPES = {"ffn1_norm": [1, 1024], "ffn1_w_in": [1, 1024, 5632], "ffn1_w_out": [1, 2816, 1024],
           "mix_norm": [1, 1024], "w_in": [1, 1024, 2576], "conv_w": [1, 31, 512], "conv_b": [1, 512],
           "conv_ln_g": [1, 512], "conv_ln_b": [1, 512], "gate_w_up": [1, 16, 256], "gate_b": [1, 256],
           "gla_norm": [1, 512], "w_out": [1, 1024, 1024], "ffn2_norm": [1, 1024],
           "ffn2_w_in": [1, 1024, 5632], "ffn2_w_out": [1, 2816, 1024], "ple_norm": [1, 1024],
           "ple_w_gate": [1, 1024, 1024], "ple_w_proj": [1, 256, 1024], "final_norm": [1024]}


def build_program(n_seq, seq_len, stop_after=None, same_sync=("act", "dve", "pool"), mix_stop=99):
    nc = bass.Bass("TRN2", target_bir_lowering=False)
    n_tiles = seq_len // TT
    x_d = nc.dram_tensor("x", [n_seq, seq_len, D], F32, kind="ExternalInput").ap()
    p_d = nc.dram_tensor("p", [n_seq, seq_len, DPLE], F32, kind="ExternalInput").ap()
    wd = {n: nc.dram_tensor(n, WSHAPES[n], F32, kind="ExternalInput").ap() for n in WNAMES}
    out_d = nc.dram_tensor("out", [n_seq, seq_len, D], F32, kind="ExternalOutput").ap()
    NBLK = 85
    wsc_d = nc.dram_tensor("wsc", [NBLK, 128, SLOTW], BF16, kind="Internal").ap()

    st = ExitStack()
    P = Prog(nc, same_sync=same_sync)

    def ext(ap):
        return View(ap, None, 0, 0)

    def A(func, out, in_, scale=None, bias=None, eng="act", accum=None):
        rd = [in_]
        kw = {}
        wr = [out]
        if accum is not None:
            kw["accum_out"] = accum.ap
            wr.append(accum)
        if scale is not None:
            if isinstance(scale, View):
                rd.append(scale); kw["scale"] = scale.ap
            else:
                kw["scale"] = float(scale)
        if bias is not None:
            if isinstance(bias, View):
                rd.append(bias); kw["bias"] = bias.ap
            else:
                kw["bias"] = float(bias)
        return P.add("act", lambda e: e.activation(out=out.ap, in_=in_.ap, func=func, **kw), reads=rd, writes=wr)

    def CP(eng, out, in_):
        if eng == "act":
            return A(AF.Copy, out, in_)
        return P.add(eng, lambda e: e.tensor_copy(out.ap, in_.ap), reads=[in_], writes=[out])

    def TTop(eng, out, a, b, op):
        return P.add(eng, lambda e: e.tensor_tensor(out=out.ap, in0=a.ap, in1=b.ap, op=op), reads=[a, b], writes=[out])

    def TS(eng, out, a, s1, s2, op0, op1=None):
        rd = [a]
        v1 = s1
        v2 = s2
        if isinstance(s1, View):
            rd.append(s1); v1 = s1.ap
        if isinstance(s2, View):
            rd.append(s2); v2 = s2.ap
        if op1 is None:
            return P.add(eng, lambda e: e.tensor_scalar(out=out.ap, in0=a.ap, scalar1=v1, scalar2=None, op0=op0), reads=rd, writes=[out])
        return P.add(eng, lambda e: e.tensor_scalar(out=out.ap, in0=a.ap, scalar1=v1, scalar2=v2, op0=op0, op1=op1), reads=rd, writes=[out])

    def STT(eng, out, a, s, b, op0, op1):
        rd = [a, b]
        sv = s
        if isinstance(s, View):
            rd.append(s); sv = s.ap
        return P.add(eng, lambda e: e.scalar_tensor_tensor(out=out.ap, in0=a.ap, scalar=sv, in1=b.ap, op0=op0, op1=op1), reads=rd, writes=[out])

    grp = {}

    def MM(out, lhsT, rhs, start, stop):
        op = P.add("pe", lambda e: e.matmul(out.ap, lhsT=lhsT.ap, rhs=rhs.ap, start=start, stop=stop), reads=[lhsT, rhs], writes=[out])
        key = out.lo
        if start:
            grp[key] = []
        grp.setdefault(key, []).append(op)
        if stop:
            for g_ in grp[key][:-1]:
                g_.redirect = op
            del grp[key]
        if pg["on"]:
            pg["ops"].append(op)
        return op

    pg = {"on": False, "ops": []}

    def TR(out, in_, idn):
        op = P.add("pe", lambda e: e.transpose(out.ap, in_.ap, idn.ap), reads=[in_, idn], writes=[out])
        if pg["on"]:
            pg["ops"].append(op)
        return op

    def gstart():
        pg["on"] = True
        pg["ops"] = []

    def gend():
        ops_ = pg["ops"]
        for o_ in ops_[:-1]:
            o_.redirect = ops_[-1]
        pg["on"] = False
        pg["ops"] = []

    def MS(eng, v, val):
        return P.add(eng, lambda e: e.memset(v.ap, val), writes=[v])

    banks = []
    for i in range(8):
        b = P.wrap(st.enter_context(nc.psum_tensor("ps%d" % i, [128, 512], F32)), [128, 512], F32, "psum")
        b.base = i * 2048
        banks.append(b)
    bctr = [0]

    def bank():
        b = banks[bctr[0] % 8]
        bctr[0] += 1
        return b

    def bank_bf(b):
        ap = b.h[:, :].bitcast(BF16)

        def v(lo, hi):
            return View(ap[:, lo:hi], "psum", b.base + lo * 2, b.base + hi * 2)
        return v

    sb = P.sbuf
    ident = sb("ident", [128, 128], F32)
    identb = sb("identb", [128, 128], BF16)
    c1024 = sb("c1024", [128, 128], BF16)
    c512 = sb("c512", [128, 128], BF16)
    c128 = sb("c128", [128, 128], BF16)
    triT2 = sb("triT2", [128, 2, 128], F32)
    u2 = sb("u2", [128, 2, 128], F32)
    gcols = sb("gcols", [128, 4, 8], F32)
    gfin = sb("gfin", [128, 8], F32)
    cw = sb("cw", [128, 4, CONVW], F32)
    cb = sb("cb", [128, 4], F32)
    lng = sb("lng", [128, 4], F32)
    lnb = sb("lnb", [128, 4], F32)
    gnorm = sb("gnorm", [128, 4], F32)
    waug_f = sb("waug_f", [128, 256], F32)
    waug = sb("waug", [128, 256], BF16)
    wglr_f = sb("wglr_f", [128, 8, 16], F32)
    wglr = sb("wglr", [128, 8, 128], BF16)
    gaug = sb("gaug", [128, TT], BF16)
    Sst = [sb("S%d" % i, [128, 2, 128], F32) for i in range(n_seq)]
    Sprev = sb("Sprev", [128, 2, 8, 128], BF16)
    hT = sb("hT", [128, KC, TT], F32)
    xT = sb("xT", [128, KC, TT], BF16)
    xin = sb("xin", [128, ST, D], F32)
    osb = sb("osb", [128, ST, D], F32, at=xin.base + ST * D * 4)
    stgpad = sb("stgpad", [128, 256], F32, at=xin.base + 2 * ST * D * 4)
    P.sb_off = xin.base + 2 * ST * D * 4 + 1024
    pin = sb("pin", [128, ST, DPLE], F32)
    pbf = sb("pbf", [128, ST, DPLE], BF16)
    pT = sb("pT", [128, 2, TT], BF16)
    wr_addr = []
    wr_in, wr_out, wr_pp = [], [], []
    for i in range(NSLOT):
        b = sb("wr%d" % i, [128, KC, 256], BF16)
        P.sb_off = b.base + SLOTW * 2
        wr_in.append(b)
        wr_out.append(sb("wro%d" % i, [128, JF, 128], BF16, at=b.base))
        wr_pp.append(sb("wrp%d" % i, [128, 2, 1024], BF16, at=b.base))
    sgb = [sb("sg%d" % i, [128, TT], F32) for i in range(3)]
    sqb = [sb("sq%d" % i, [128, TT], BF16) for i in range(3)]
    rstd_t = [sb("rstd%d" % i, [128, TT], F32) for i in range(2)]
    lnt = sb("lnt", [128, TT], F32)
    qd = sb("qd", [128, 2, TT], BF16)
    qn = sb("qn", [128, 2, TT], BF16)
    kn = sb("kn", [128, 2, TT], BF16)
    kp = sb("kp", [128, 2, TT], BF16)
    vtok = sb("vtok", [128, ST, 512], BF16)
    kdtok = sb("kdtok", [128, ST, 256], BF16)
    scT = [sb("scT%d" % i, [128, 2, 2, 128], BF16) for i in range(2)]
    mt1 = [sb("mt1_%d" % i, [128, 2, 128], F32) for i in range(2)]
    mt2 = [sb("mt2_%d" % i, [128, 2, 128], F32) for i in range(2)]
    osq = sb("osq", [128, 4, TT], BF16)
    region0 = (P.sb_off + 63) // 64 * 64
    hid = sb("hid", [128, JF, TT], BF16)
    ymix = sb("ymix", [128, 8, TT], BF16)
    acc = sb("acc", [128, 4, TT], F32)
    glu = sb("glu", [128, 4, HALO + TT], BF16)
    epos = sb("epos", [128, 2, TT], F32)
    eneg = sb("eneg", [128, 2, TT], F32)
    region1 = P.sb_off
    assert qn.base == qd.base + 2048 and kn.base == qd.base + 4096 and kp.base == qd.base + 6144
    xTn = sb("xTn", [128, KC, TT], BF16, at=qd.base)
    xnb = [sb("xnb%d" % i, [128, D], BF16, at=osq.base + i * 2048) for i in range(2)]
    sscol = sb("sscol", [128, ST], F32)
    tcol = sb("tcol", [128, ST], F32)
    rcol = sb("rcol", [128, ST], F32)
    rsil = sb("rsil", [128, 4, TT], F32, at=hid.base)
    o_sb = sb("o_sb", [128, 4, TT], F32, at=hid.base + 8192)
    Lb = sb("Lb", [128, ST, 256], F32, at=hid.base + 16384)
    eR = [sb("eR%d" % i, [128, 256], F32, at=hid.base + 20480 + i * 1024) for i in range(2)]
    assert hid.base + 22528 >= hid.base + 20480 + 2048
    ctmp = [sb("ctmp0", [128, TT], F32, at=pbf.base), sb("ctmp1", [128, TT], F32, at=pT.base)]
    NSTG = 3
    stg_f = []
    a0 = xin.base
    for i in range(NSTG):
        stg_f.append({"in": sb("sfi%d" % i, [128, KC, 256], F32, at=a0),
                      "out": sb("sfo%d" % i, [128, JF, 128], F32, at=a0),
                      "pp": sb("sfp%d" % i, [128, 2, 1024], F32, at=a0)})
        a0 += SLOTW * 4
    assert a0 <= xin.base + 2 * ST * D * 4 + 1024
    print("SBUF used up to", P.sb_off, "cap", P.sb_cap)

    MS("pool", ident.v(), 0.0)
    P.add("pool", lambda e: e.affine_select(out=ident.v().ap, in_=ident.v().ap, pattern=[[-1, 128]],
                                           compare_op=ALU.not_equal, fill=1.0, base=0, channel_multiplier=1),
          reads=[ident.v()], writes=[ident.v()])
    CP("dve", identb.v(), ident.v())
    MS("pool", c1024.v(), 1.0 / 1024)
    MS("pool", c512.v(), 1.0 / 512)
    MS("pool", c128.v(), 1.0 / 128)
    for e in range(2):
        MS("pool", triT2.v(e), 1.0)
        P.add("pool", (lambda e_: lambda g: g.affine_select(out=triT2.v(e_).ap, in_=triT2.v(e_).ap, pattern=[[1, 128]],
                                                            compare_op=ALU.is_ge, fill=0.0, base=0, channel_multiplier=-1))(e),
              reads=[triT2.v(e)], writes=[triT2.v(e)])
        MS("pool", triT2.v(e, slice(64, 128), p=slice(0, 64)), 0.0)
        MS("pool", u2.v(e), 1.0)
        P.add("pool", (lambda e_: lambda g: g.affine_select(out=u2.v(e_).ap, in_=u2.v(e_).ap, pattern=[[-1, 128]],
                                                            compare_op=ALU.is_gt, fill=0.0, base=0, channel_multiplier=1))(e),
              reads=[u2.v(e)], writes=[u2.v(e)])
        MS("pool", u2.v(e, slice(0, 64), p=slice(64, 128)), 0.0)

    def small_dma(out, ap):
        P.add("sp", lambda e: e.dma_start(out=out.ap, in_=ap, allow_slow_non_contiguous=True), writes=[out], dma=True)

    for i, nm in enumerate(["ffn1_norm", "mix_norm", "ffn2_norm", "ple_norm"]):
        small_dma(gcols.v(i), wd[nm].rearrange("o (c p) -> p (o c)", p=128))
    small_dma(gfin.v(), wd["final_norm"].rearrange("(c p) -> p c", p=128))
    for cc_ in range(4):
        small_dma(cw.v(cc_), wd["conv_w"][0][:, cc_ * 128:(cc_ + 1) * 128].rearrange("j p -> p j"))
    small_dma(cb.v(), wd["conv_b"].rearrange("o (c p) -> p (o c)", p=128))
    small_dma(lng.v(), wd["conv_ln_g"].rearrange("o (c p) -> p (o c)", p=128))
    small_dma(lnb.v(), wd["conv_ln_b"].rearrange("o (c p) -> p (o c)", p=128))
    small_dma(gnorm.v(), wd["gla_norm"].rearrange("o (c p) -> p (o c)", p=128))
    MS("pool", waug_f.v(), 0.0)
    P.dma(waug_f.v(p=slice(0, 16)), ext(wd["gate_w_up"][0]))
    P.dma(waug_f.v(p=slice(16, 17)), ext(wd["gate_b"]))
    CP("dve", waug.v(), waug_f.v())
    MS("pool", gaug.v(), 1.0)
    P.dma(wglr_f.v(), ext(wd["w_in"][0][:, 2560:2576].rearrange("(c p) x -> p c x", p=128)))
    MS("pool", wglr.v(), 0.0)
    for c in range(KC):
        TS("dve", wglr.v(c, slice(0, 16)), wglr_f.v(c), gcols.v(1, slice(c, c + 1)), None, ALU.mult)
    for i in range(n_seq):
        MS("pool", Sst[i].v(), 0.0)

    blocks = []

    def rin(w, c0, n):
        return w[:, c0:c0 + n].rearrange("(c p) x -> p c x", p=128)

    def add_ffn(pref, gi):
        wi = wd[pref + "_w_in"][0]
        wo = wd[pref + "_w_out"][0]
        for j in range(JF):
            blocks.append(("in", [(rin(wi, j * 128, 128), 0, 128), (rin(wi, DFF + j * 128, 128), 128, 256)], gi, None))
        for m in range(KC):
            blocks.append(("out", [(rin(wo, m * 128, 128), 0, 128)], None, 0.5))

    add_ffn("ffn1", 0)
    BLK_MIX = len(blocks)
    wm = wd["w_in"][0]
    for c0 in (0, 256, 512, 768):
        blocks.append(("in", [(rin(wm, c0, 256), 0, 256)], 1, None))
    for b_ in range(6):
        blocks.append(("dg", [], b_, None))
    for c0 in (1024, 1280, 1536, 1792, 2048, 2304):
        blocks.append(("in", [(rin(wm, c0, 256), 0, 256)], 1, None))
    for q in range(4):
        blocks.append(("in", [(rin(wd["w_out"][0], q * 256, 256), 0, 256)], None, None))
    BLK_FFN2 = len(blocks)
    add_ffn("ffn2", 2)
    BLK_PLE = len(blocks)
    for q in range(4):
        blocks.append(("in", [(rin(wd["ple_w_gate"][0], q * 256, 256), 0, 256)], 3, None))
    blocks.append(("pp", [(wd["ple_w_proj"][0].rearrange("(c p) x -> p c x", p=128), 0, 1024)], None, None))
    assert len(blocks) == NBLK
    BLKN = {"in": 2048, "out": 2816, "pp": 2048, "dg": 2816}

    def wsc_view(bi, n):
        return View(wsc_d[bi, :, 0:n], "wsc", bi, bi + 1)

    cmap = ["dve", "act", "dve", "act", "dve", "dve", "act", "dve"]

    def cast1(eng, dst, src, scal):
        if eng == "act":
            A(AF.Copy, dst, src, scale=scal)
        elif scal is None:
            CP(eng, dst, src)
        else:
            TS(eng, dst, src, scal, None, ALU.mult)

    def pro_cast_store(bi, slot):
        kind, srcs, gi, cs = blocks[bi]
        n = BLKN[kind]
        flat_src = wr_out[slot] if n > 2048 else wr_in[slot]
        flat = View(flat_src.h[:, :, :].rearrange("p a b -> p (a b)"), "sbuf", wr_in[slot].base, wr_in[slot].base + n * 2)
        if kind == "dg":
            sbb = wr_out[slot]
            for k_ in range(22):
                idx_ = gi * 22 + k_
                if idx_ >= 4 * CONVW:
                    MS("pool", sbb.v(k_), 0.0)
                    continue
                cc_, j_ = idx_ // CONVW, idx_ % CONVW
                cast1("dve" if k_ % 2 else "act", sbb.v(k_), ident.v(), cw.v(cc_, slice(j_, j_ + 1)))
            P.dma(wsc_view(bi, n), flat, eng="act")
            return
        sf = stg_f[pro["nload_idx"][bi] % NSTG][kind]
        sbb = {"in": wr_in, "out": wr_out, "pp": wr_pp}[kind][slot]
        if gi is not None:
            for c in range(KC):
                cast1(cmap[(c + bi) % 8], sbb.v(c), sf.v(c), gcols.v(gi, slice(c, c + 1)))
        else:
            nk = sf.shape[1]
            cuts = [0, (nk * 5) // 9, nk]
            for eng, a_, b_ in zip(("dve", "act"), cuts[:-1], cuts[1:]):
                if b_ > a_:
                    sc_ = cs if cs is not None else (1.0 if eng == "act" else None)
                    cast1(eng, sbb.v(slice(a_, b_)), sf.v(slice(a_, b_)), sc_)
        P.dma(wsc_view(bi, n), flat, eng="act")

    pro = {"loaded": 0, "nload": 0, "nload_idx": {}}

    def pro_prefetch(upto):
        while pro["loaded"] < min(upto, NBLK):
            bi = pro["loaded"]
            if blocks[bi][0] != "dg":
                pro["nload_idx"][bi] = pro["nload"]
                kind, srcs, gi, cs = blocks[bi]
                sf = stg_f[pro["nload"] % NSTG][kind]
                for (ap, lo, hi) in srcs:
                    P.dma(sf.v(slice(None), slice(lo, hi)), ext(ap))
                pro["nload"] += 1
            pro["loaded"] += 1

    tiles = [(sq_, ti) for sq_ in range(n_seq) for ti in range(n_tiles)]
    total_blocks = len(tiles) * NBLK
    wst = {"issued": 0, "consumed": 0, "released": 0}

    def _wissue():
        while wst["issued"] < total_blocks and wst["issued"] - NSLOT < wst["released"]:
            k = wst["issued"]
            bi = k % NBLK
            n = BLKN[blocks[bi][0]]
            slot = k % NSLOT
            if k < NBLK:
                pro_prefetch(k + NSTG)
                pro_cast_store(bi, slot)
            else:
                src_buf = wr_out[slot] if n > 2048 else wr_in[slot]
                dst = View(src_buf.h[:, :, :].rearrange("p a b -> p (a b)"), "sbuf", wr_in[slot].base, wr_in[slot].base + n * 2)
                P.dma(dst, wsc_view(bi, n))
            wst["issued"] += 1

    def wnext():
        _wissue()
        k = wst["consumed"]
        assert k < wst["issued"], "weight ring too small for the number of blocks held"
        wst["consumed"] += 1
        kind = blocks[k % NBLK][0]
        slot = k % NSLOT
        return {"in": wr_in, "out": wr_out, "pp": wr_pp, "dg": wr_out}[kind][slot]

    def wdone(n=1):
        wst["released"] += n
        assert wst["released"] <= wst["consumed"]
        _wissue()

    rctr = [0]

    def rmsnorm_stats(src, cmat, nchunks, eps_val=EPS, sqdst=None):
        bk = bank()
        for m in range(nchunks):
            sq = sqb[m % 3].v() if sqdst is None else sqdst(m)
            A(AF.Square, sq, src(m))
            MM(bk.v(), cmat.v(), sq, m == 0, m == nchunks - 1)
        A(AF.Ln, lnt.v(), bk.v(), bias=eps_col.v())
        r = rstd_t[rctr[0] % 2]
        rctr[0] += 1
        A(AF.Exp, r.v(), lnt.v(), scale=-0.5)
        return r

    def norm_to_xT():
        r = rmsnorm_stats(lambda m: hT.v(m), c1024, KC, sqdst=lambda m: xT.v(m))
        for m in range(KC):
            TTop("dve", xT.v(m), hT.v(m), r.v(), ALU.mult)

    def ffn_block(j, xsrc):
        W = wnext()
        bg = bank()
        for c in range(KC):
            MM(bg.v(), W.v(c, slice(0, 128)), xsrc.v(c), c == 0, c == KC - 1)
        bu = bank()
        for c in range(KC):
            MM(bu.v(), W.v(c, slice(128, 256)), xsrc.v(c), c == 0, c == KC - 1)
        sg = sgb[j % 3]
        A(AF.Silu, sg.v(), bg.v())
        TTop("dve", hid.v(j), sg.v(), bu.v(), ALU.mult)
        wdone()

    def ffn_up(js, xsrc, hooks=None):
        for j in js:
            ffn_block(j, xsrc)
            if hooks and j in hooks:
                for h_ in hooks[j]:
                    h_()

    def ffn_up_first2(xsrc):
        bb = [(wnext(), bank(), bank()), (wnext(), bank(), bank())]
        for c in range(KC):
            for (W, bg, bu) in bb:
                MM(bg.v(), W.v(c, slice(0, 128)), xsrc.v(c), c == 0, c == KC - 1)
                MM(bu.v(), W.v(c, slice(128, 256)), xsrc.v(c), c == 0, c == KC - 1)
        for jj, (W, bg, bu) in enumerate(bb):
            sg = sgb[jj % 3]
            A(AF.Silu, sg.v(), bg.v())
            TTop("dve", hid.v(jj), sg.v(), bu.v(), ALU.mult)
        wdone(2)

    def ffn_down():
        for m in range(KC):
            W = wnext()
            bk = bank()
            for j in range(JF):
                MM(bk.v(), W.v(j), hid.v(j), j == 0, j == JF - 1)
            TTop("dve", hT.v(m), hT.v(m), bk.v(), ALU.add)
            wdone()

    def prepA(s):
        xb_ = xnb[s % 2]
        A(AF.Square, xb_.v(), xin.v(s), accum=sscol.v(slice(s, s + 1)))
        A(AF.Ln, tcol.v(slice(s, s + 1)), sscol.v(slice(s, s + 1)), scale=1.0 / D, bias=eps_col.v())
        A(AF.Exp, rcol.v(slice(s, s + 1)), tcol.v(slice(s, s + 1)), scale=-0.5)
        A(AF.Copy, xb_.v(), xin.v(s), scale=rcol.v(slice(s, s + 1)))

    def prepB(s):
        xb_ = xnb[s % 2]
        bk = bank()
        bv = bank_bf(bk)
        gstart()
        for c in range(KC):
            TR(bv(c * 128, (c + 1) * 128), xb_.v(slice(c * 128, (c + 1) * 128)), identb.v())
        gend()
        src = View(bk.h[:, :].bitcast(BF16).rearrange("p (c t) -> p c t", c=KC), "psum", bk.base, bk.base + 2048)
        CP("dve", xTn.v(slice(None), slice(s * 128, (s + 1) * 128)), src)

    def load_x(sq_, ti):
        P.dma(xin.v(), ext(x_d[sq_, ti * TT:(ti + 1) * TT, :].rearrange("(s p) d -> p s d", p=128)))

    def load_p(sq_, ti):
        P.dma(pin.v(), ext(p_d[sq_, ti * TT:(ti + 1) * TT, :].rearrange("(s p) d -> p s d", p=128)))

    def x_to_hT():
        for m in range(KC):
            bk = bank()
            gstart()
            for s in range(ST):
                TR(bk.v(slice(s * 128, (s + 1) * 128)), xin.v(s, slice(m * 128, (m + 1) * 128)), ident.v())
            gend()
            CP("act" if m % 2 == 0 else "dve", hT.v(m), bk.v())

    def final_part():
        r = rmsnorm_stats(lambda m: hT.v(m), c1024, KC)
        for m in range(KC):
            STT("dve", hT.v(m), hT.v(m), gfin.v(slice(m, m + 1)), r.v(), ALU.mult, ALU.mult)

    def store_out(sq_, ti, final):
        for s in range(ST):
            for half in range(2):
                bk = bank()
                gstart()
                for mm in range(4):
                    m = half * 4 + mm
                    TR(bk.v(slice(mm * 128, (mm + 1) * 128)), hT.v(m, slice(s * 128, (s + 1) * 128)), ident.v())
                gend()
                CP("act" if half == 0 else "dve", osb.v(s, slice(half * 512, (half + 1) * 512)), bk.v())
        P.dma(View(out_d[sq_, ti * TT:(ti + 1) * TT, :].rearrange("(s p) d -> p s d", p=128), "out", 0, 1), osb.v())

    def mix(sq_, ti):
        S = Sst[sq_]
        norm_to_xT()
        Wa = [wnext(), wnext()]
        Wg = [wnext(), wnext()]
        if ti == 0:
            MS("pool", glu.v(slice(None), slice(0, HALO)), 0.0)
        else:
            CP("pool", glu.v(slice(None), slice(0, HALO)), glu.v(slice(None), slice(TT, TT + HALO)))
        bb = [(cc, bank(), bank()) for cc in range(2)]
        for c in range(KC):
            for (cc, bg, ba) in bb:
                MM(bg.v(), Wg[0].v(c, slice(cc * 128, cc * 128 + 128)), xT.v(c), c == 0, c == KC - 1)
                MM(ba.v(), Wa[0].v(c, slice(cc * 128, cc * 128 + 128)), xT.v(c), c == 0, c == KC - 1)
        for (cc, bg, ba) in bb:
            sg = sgb[cc % 3]
            A(AF.Sigmoid, sg.v(), bg.v())
            TTop("dve", glu.v(cc, slice(HALO, HALO + TT)), sg.v(), ba.v(), ALU.mult)
        for cc in range(2, 4):
            bg = bank()
            for c in range(KC):
                MM(bg.v(), Wg[cc // 2].v(c, slice((cc % 2) * 128, (cc % 2) * 128 + 128)), xT.v(c), c == 0, c == KC - 1)
            ba = bank()
            for c in range(KC):
                MM(ba.v(), Wa[cc // 2].v(c, slice((cc % 2) * 128, (cc % 2) * 128 + 128)), xT.v(c), c == 0, c == KC - 1)
            sg = sgb[cc % 3]
            A(AF.Sigmoid, sg.v(), bg.v())
            TTop("dve", glu.v(cc, slice(HALO, HALO + TT)), sg.v(), ba.v(), ALU.mult)
        wdone(4)
        held = {}
        for cc in range(4):
            bc = bank()
            for j in range(CONVW):
                idx = cc * CONVW + j
                blk = idx // 22
                if blk not in held:
                    held[blk] = wnext()
                MM(bc.v(), held[blk].v(idx % 22), glu.v(cc, slice(j, j + TT)), j == 0, j == CONVW - 1)
                if idx % 22 == 21 or idx == 4 * CONVW - 1:
                    wdone()
            A(AF.Identity, acc.v(cc), bc.v(), bias=cb.v(slice(cc, cc + 1)))

        def conv_emit(n):
            return
        conv_emit(8)
        bk = bank()
        for c in range(KC):
            MM(bk.v(), wglr.v(c), xT.v(c), c == 0, c == KC - 1)
        CP("act", gaug.v(p=slice(0, 16)), bk.v(p=slice(0, 16)))
        for s2 in range(2):
            bpre = bank()
            gstart()
            for s in (2 * s2, 2 * s2 + 1):
                dst = bpre.v(slice((s % 2) * 256, (s % 2) * 256 + 256))
                MM(dst, gaug.v(slice(s * 128, (s + 1) * 128)), waug.v(), True, True)
            gend()
            for s in (2 * s2, 2 * s2 + 1):
                dst = bpre.v(slice((s % 2) * 256, (s % 2) * 256 + 256))
                A(AF.Exp, Lb.v(s), dst, scale=-1.0)
        for s in range(ST):
            A(AF.Ln, Lb.v(s), Lb.v(s), bias=one_col.v())
        bcum = [bank(), bank()]
        for cd in range(2):
            gstart()
            for s in range(ST):
                MM(bcum[cd].v(slice(s * 128, (s + 1) * 128)), Lb.v(s, slice(cd * 128, (cd + 1) * 128)), triT2.v(0), True, True)
            gend()
        for cd in range(2):
            A(AF.Exp, epos.v(cd), bcum[cd].v(), scale=-1.0 / 16)
            A(AF.Exp, eneg.v(cd), bcum[cd].v(), scale=1.0 / 16)
        conv_emit(12)
        Wq = wnext()
        Wk = wnext()
        for cd in range(2):
            bq = bank()
            for c in range(KC):
                MM(bq.v(), Wq.v(c, slice(cd * 128, (cd + 1) * 128)), xT.v(c), c == 0, c == KC - 1)
            STT("dve", qd.v(cd), bq.v(), 0.125, epos.v(cd), ALU.mult, ALU.mult)
            STT("dve", qn.v(cd), bq.v(), 0.125, eneg.v(cd), ALU.mult, ALU.mult)
            bkk = bank()
            for c in range(KC):
                MM(bkk.v(), Wk.v(c, slice(cd * 128, (cd + 1) * 128)), xT.v(c), c == 0, c == KC - 1)
            TTop("dve", kn.v(cd), bkk.v(), eneg.v(cd), ALU.mult)
            TTop("dve", kp.v(cd), bkk.v(), epos.v(cd), ALU.mult)
            conv_emit(6)
        for s in range(ST):
            br = bank()
            MM(br.v(slice(0, 256)), u2.v(0), Lb.v(s), True, True)
            er = eR[s % 2]
            A(AF.Exp, er.v(), br.v(slice(0, 256)), scale=-1.0 / 16)
            bk2_ = bank()
            for c in range(KC):
                MM(bk2_.v(slice(0, 256)), xT.v(c, slice(s * 128, (s + 1) * 128)), Wk.v(c), c == 0, c == KC - 1)
            TTop("dve", kdtok.v(s), bk2_.v(slice(0, 256)), er.v(), ALU.mult)
            conv_emit(3)
        wdone(2)
        Wv = [wnext(), wnext()]
        Wr = [wnext(), wnext()]

        def vtok_proj(s):
            bv = bank()
            gstart()
            for piece in range(2):
                for c in range(KC):
                    MM(bv.v(slice(piece * 256, (piece + 1) * 256)), xT.v(c, slice(s * 128, (s + 1) * 128)), Wv[piece].v(c),
                       c == 0, c == KC - 1)
            gend()
            CP("act", vtok.v(s), bv.v())

        def r_proj(h):
            brr = bank()
            for c in range(KC):
                MM(brr.v(), Wr[h // 2].v(c, slice((h % 2) * 128, (h % 2) * 128 + 128)), xT.v(c), c == 0, c == KC - 1)
            A(AF.Silu, rsil.v(h), brr.v())

        vtok_proj(0)
        for pr in range(2):
            CP("act", Sprev.v(pr, 0), S.v(pr))

        def chain(c8):
            s, cc = c8 // 2, c8 % 2
            rows = slice(cc * 64, (cc + 1) * 64)
            bkv = bank()
            gstart()
            for pr in range(2):
                MM(bkv.v(slice(pr * 256, (pr + 1) * 256)), kdtok.v(s, slice(pr * 128, (pr + 1) * 128), p=rows),
                   vtok.v(s, slice(pr * 256, (pr + 1) * 256), p=rows), True, True)
            gend()
            tl = s * 128 + cc * 64 + 63
            for snap in (True, False):
                if snap and c8 == 7:
                    continue
                for pr in range(2):
                    for e in range(2):
                        pp_ = slice(e * 64, (e + 1) * 64)
                        dst = Sprev.v(pr, c8 + 1, p=pp_) if snap else S.v(pr, p=pp_)
                        STT("dve", dst, S.v(pr, p=pp_), epos.v(pr, slice(tl, tl + 1), p=pp_),
                            bkv.v(slice(pr * 256 + e * 128, pr * 256 + e * 128 + 128), p=pp_), ALU.mult, ALU.add)

        lnst = {}

        def ln_mid(s):
            if s == 0:
                for cc in range(4):
                    CP("act", ymix.v(cc), acc.v(cc))
            elif s == 1:
                for cc in range(4):
                    TTop("dve", acc.v(cc), acc.v(cc), lnst["bm"].v(), ALU.subtract)
                    A(AF.Square, ymix.v(cc), acc.v(cc))
            elif s == 2:
                A(AF.Ln, lnt.v(), lnst["bv"].v(), bias=eps_col.v())
                r_ = rstd_t[rctr[0] % 2]
                rctr[0] += 1
                A(AF.Exp, r_.v(), lnt.v(), scale=-0.5)
                lnst["rc"] = r_
            else:
                for cc in range(4):
                    A(AF.Silu, ymix.v(cc), acc.v(cc), scale=lng.v(slice(cc, cc + 1)), bias=lnb.v(slice(cc, cc + 1)))

        def ln_end(s):
            if s == 0:
                lnst["bm"] = bank()
                for cc in range(4):
                    MM(lnst["bm"].v(), c512.v(), ymix.v(cc), cc == 0, cc == 3)
            elif s == 1:
                lnst["bv"] = bank()
                for cc in range(4):
                    MM(lnst["bv"].v(), c512.v(), ymix.v(cc), cc == 0, cc == 3)
            elif s == 2:
                for cc in range(4):
                    TTop("dve", acc.v(cc), acc.v(cc), lnst["rc"].v(), ALU.mult)

        for s in range(ST):
            ts = slice(s * 128, (s + 1) * 128)
            sc = scT[s % 2]
            chain(2 * s)
            conv_emit(2)
            bse = [bank(), bank()]
            for e in range(2):
                hp = slice(e * 64, (e + 1) * 64)
                gstart()
                for pr in range(2):
                    MM(bse[e].v(slice(pr * 256, pr * 256 + 128)), kn.v(pr, ts, p=hp), qd.v(pr, ts, p=hp), True, True)
                    MM(bse[e].v(slice(pr * 256 + 128, pr * 256 + 256)), kp.v(pr, ts, p=hp), qn.v(pr, ts, p=hp), True, True)
                gend()
            for e in range(2):
                b4 = bse[e].h[:, :].rearrange("p (r l t) -> p r l t", r=2, l=2)
                lo_v = View(b4[:, :, 0, :], "psum", bse[e].base, bse[e].base + 2048)
                up_v = View(b4[:, :, 1, :], "psum", bse[e].base, bse[e].base + 2048)
                t1 = mt1[e]
                t2 = mt2[e]
                TTop("dve", t1.v(), lo_v, triT2.v(), ALU.mult)
                TTop("dve", t2.v(), up_v, u2.v(), ALU.mult)
                TTop("pool", sc.v(slice(None), e), t1.v(), t2.v(), ALU.add)
            ln_mid(s)
            r_proj(s)
            if s + 1 < ST:
                vtok_proj(s + 1)
            chain(2 * s + 1)
            conv_emit(4)
            bo = bank()
            gstart()
            for h in range(4):
                pr, e = h // 2, h % 2
                hp = slice(e * 64, (e + 1) * 64)
                MM(bo.v(slice(h * 128, (h + 1) * 128)), vtok.v(s, slice(h * 128, (h + 1) * 128)), sc.v(pr, e), True, False)
                for cc in range(2):
                    c8 = 2 * s + cc
                    cols = slice(h * 128 + cc * 64, h * 128 + cc * 64 + 64)
                    MM(bo.v(cols), Sprev.v(pr, c8, p=hp), qd.v(pr, slice(s * 128 + cc * 64, s * 128 + cc * 64 + 64), p=hp), False, True)
            gend()
            b3 = bo.h[:, :].rearrange("p (h t) -> p h t", h=4)
            bo_v = View(b3, "psum", bo.base, bo.base + 2048)
            A(AF.Copy, o_sb.v(slice(None), ts), bo_v)
            A(AF.Square, osq.v(slice(None), ts), bo_v)
            ln_end(s)
            conv_emit(6)
        wdone(4)
        conv_emit(1000)
        for h in range(4):
            bk2 = bank()
            MM(bk2.v(), c128.v(), osq.v(h), True, True)
            A(AF.Ln, lnt.v(), bk2.v(), bias=eps_col.v())
            r = rstd_t[rctr[0] % 2]
            rctr[0] += 1
            A(AF.Exp, r.v(), lnt.v(), scale=-0.5)
            TTop("pool", rsil.v(h), rsil.v(h), r.v(), ALU.mult)
            STT("dve", ymix.v(4 + h), o_sb.v(h), gnorm.v(slice(h, h + 1)), rsil.v(h), ALU.mult, ALU.mult)
        Wo = [wnext() for _ in range(4)]
        bks = [bank() for _ in range(4)]
        for m in range(4):
            for kc in range(4):
                MM(bks[m].v(), Wo[m // 2].v(kc, slice((m % 2) * 128, (m % 2) * 128 + 128)), ymix.v(kc), kc == 0, False)
        for m in range(4):
            for kc in range(4, 8):
                MM(bks[m].v(), Wo[m // 2].v(kc, slice((m % 2) * 128, (m % 2) * 128 + 128)), ymix.v(kc), False, kc == 7)
            TTop("dve", hT.v(m), hT.v(m), bks[m].v(), ALU.add)
        for m in range(4, KC):
            bk3 = bank()
            for kc in range(8):
                MM(bk3.v(), Wo[m // 2].v(kc, slice((m % 2) * 128, (m % 2) * 128 + 128)), ymix.v(kc), kc == 0, kc == 7)
            TTop("dve", hT.v(m), hT.v(m), bk3.v(), ALU.add)
        wdone(4)

    def ple():
        CP("pool", pbf.v(), pin.v())
        for kc in range(2):
            bk = bank()
            bv = bank_bf(bk)
            gstart()
            for s in range(ST):
                TR(bv(s * 128, (s + 1) * 128), pbf.v(s, slice(kc * 128, (kc + 1) * 128)), identb.v())
            gend()
            CP("act", pT.v(kc), bv(0, 512))
        norm_to_xT()
        Wg = [wnext() for _ in range(4)]
        Wp = wnext()

        def pproj(m):
            bp = bank()
            for kc in range(2):
                MM(bp.v(), Wp.v(kc, slice(m * 128, (m + 1) * 128)), pT.v(kc), kc == 0, kc == 1)
            return bp

        def fin(m, bg, bp):
            sg = sgb[m % 3]
            A(AF.Sigmoid, sg.v(), bg.v())
            TTop("dve", sg.v(), sg.v(), bp.v(), ALU.mult)
            TTop("pool", hT.v(m), hT.v(m), sg.v(), ALU.add)

        bps = [pproj(m) for m in range(4)]
        bgs = [bank() for _ in range(4)]
        for c in range(KC):
            for m in range(4):
                MM(bgs[m].v(), Wg[m // 2].v(c, slice((m % 2) * 128, (m % 2) * 128 + 128)), xT.v(c), c == 0, c == KC - 1)
        for m in range(4):
            fin(m, bgs[m], bps[m])
        for m in range(4, KC):
            bg = bank()
            for c in range(KC):
                MM(bg.v(), Wg[m // 2].v(c, slice((m % 2) * 128, (m % 2) * 128 + 128)), xT.v(c), c == 0, c == KC - 1)
            bp = pproj(m)
            fin(m, bg, bp)
        wdone(5)

    eps_col = sb("eps_col", [128, 1], F32)
    one_col = sb("one_col", [128, 1], F32)
    MS("pool", eps_col.v(), EPS)
    MS("pool", one_col.v(), 1.0)

    NPRE = 6
    assert stop_after is None
    load_x(*tiles[0])
    load_p(*tiles[0])
    for s_ in range(ST):
        prepA(s_)
        prepB(s_)
    x_to_hT()
    ffn_up(range(0, NPRE), xTn)
    for idx, (sq_, ti) in enumerate(tiles):
        has_next = idx + 1 < len(tiles)
        if idx > 0:
            x_to_hT()
        if has_next and idx > 0:
            load_x(*tiles[idx + 1])
        ffn_up(range(NPRE, JF), xTn)
        ffn_down()
        mix(sq_, ti)
        hooks = None
        if has_next and idx > 0:
            hooks = {3: [lambda: prepA(0)], 6: [lambda: prepB(0), lambda: prepA(1)], 9: [lambda: prepB(1), lambda: prepA(2)],
                     12: [lambda: prepB(2), lambda: prepA(3)], 15: [lambda: prepB(3)]}
        norm_to_xT()
        ffn_up_first2(xT)
        ffn_up(range(2, JF), xT, hooks)
        ffn_down()
        ple()
        if has_next:
            load_p(*tiles[idx + 1])
            if idx == 0:
                load_x(*tiles[idx + 1])
                for s_ in range(ST):
                    prepA(s_)
                    prepB(s_)
        final_part()
        if has_next:
            ffn_up(range(0, NPRE), xTn)
        store_out(sq_, ti, True)
    P.emit(st)
    st.close()
    return nc, P

from concourse.bass_utils import run_bass_kernel_spmd

N_CORES = 8
_PROG_CACHE = {}


def kernel(**inputs):
    x = np.ascontiguousarray(np.asarray(inputs["x"], dtype=np.float32))
    p = np.ascontiguousarray(np.asarray(inputs["p"], dtype=np.float32))
    B, T, _ = x.shape
    spc = B // N_CORES
    key = (spc, T)
    if key not in _PROG_CACHE:
        _PROG_CACHE[key] = build_program(spc, T)[0]
    nc = _PROG_CACHE[key]
    wts = {n: np.ascontiguousarray(np.asarray(inputs[n], dtype=np.float32)) for n in WNAMES}
    in_maps = []
    for c in range(N_CORES):
        m = {"x": x[c * spc:(c + 1) * spc], "p": p[0, c * spc:(c + 1) * spc]}
        m.update(wts)
        in_maps.append(m)
    res = run_bass_kernel_spmd(nc, in_maps, core_ids=list(range(N_CORES)))
    out = np.concatenate([np.asarray(r["out"], dtype=np.float32) for r in res.results], axis=0)
    return out
```

```python
import bisect
import numpy as np
import concourse.bass as bass
import concourse.mybir as mybir

F32 = mybir.dt.float32
BF16 = mybir.dt.bfloat16
AF = mybir.ActivationFunctionType
ALU = mybir.AluOpType

ENGS = ("pe", "act", "dve", "pool", "sp")
N_DMA_SEMS = 24


class _Track:
    def __init__(self):
        self.b = [0]
        self.w = [None]
        self.r = [{}]

    def _split(self, x):
        i = bisect.bisect_right(self.b, x) - 1
        if self.b[i] == x:
            return i
        self.b.insert(i + 1, x)
        self.w.insert(i + 1, self.w[i])
        self.r.insert(i + 1, dict(self.r[i]))
        return i + 1

    def access(self, lo, hi, op, write):
        i = self._split(lo)
        j = self._split(hi)
        deps = set()
        for s in range(i, j):
            if self.w[s] is not None:
                deps.add(self.w[s])
            if write:
                deps.update(self.r[s].values())
                self.w[s] = op
                self.r[s] = {}
            else:
                self.r[s][op.eng] = op
        if write and j - i > 4:
            del self.b[i + 1:j]
            del self.w[i + 1:j]
            del self.r[i + 1:j]
        return deps


class View:
    __slots__ = ("ap", "space", "lo", "hi")

    def __init__(self, ap, space, lo, hi):
        self.ap = ap
        self.space = space
        self.lo = lo
        self.hi = hi


class Buf:
    def __init__(self, handle, shape, esize, space, base):
        self.h = handle
        self.shape = tuple(shape)
        self.esize = esize
        self.space = space
        self.base = base
        st = [1] * len(self.shape)
        for k in range(len(self.shape) - 2, 0, -1):
            st[k] = st[k + 1] * self.shape[k + 1]
        self.strides = st

    def v(self, *idx, p=None):
        nfree = len(self.shape) - 1
        idx = list(idx) + [slice(None)] * (nfree - len(idx))
        lo = 0
        hi = 0
        for k, ix in enumerate(idx):
            dim = self.shape[k + 1]
            stv = self.strides[k + 1]
            if isinstance(ix, int):
                a, bnd = ix, ix
            else:
                a = 0 if ix.start is None else ix.start
                e = dim if ix.stop is None else ix.stop
                assert ix.step in (None, 1)
                bnd = e - 1
            assert 0 <= a <= bnd < dim, (self.shape, idx)
            lo += a * stv
            hi += bnd * stv
        ps = slice(None) if p is None else p
        ap = self.h[(ps,) + tuple(idx)]
        return View(ap, self.space, self.base + lo * self.esize, self.base + (hi + 1) * self.esize)


class Op:
    __slots__ = ("eng", "fn", "deps", "sig", "count", "dsem", "dcount", "is_dma", "idx", "redirect", "line", "wl")

    def __init__(self, eng, fn, is_dma):
        self.eng = eng
        self.fn = fn
        self.deps = []
        self.sig = False
        self.count = None
        self.is_dma = is_dma
        self.dsem = None
        self.dcount = None
        self.redirect = None


class Prog:
    def __init__(self, nc, same_sync=("act", "dve", "pool")):
        self.nc = nc
        self.ops = {e: [] for e in ENGS}
        self.trk = {}
        self.same_sync = set(same_sync)
        self.n_dma = {e: 0 for e in ENGS}
        self.sb_off = 16512
        self.sb_cap = 229344
        self.all_ops = 0

    def sbuf(self, name, shape, dtype, at=None):
        es = 2 if dtype == BF16 else 4
        nbytes = int(np.prod(shape[1:])) * es
        if at is None:
            at = (self.sb_off + 63) // 64 * 64
            self.sb_off = at + nbytes
        assert at >= 16512 and at + nbytes <= self.sb_cap, (name, at, nbytes)
        h = self.nc.alloc_sbuf_tensor_at(name, list(shape), dtype, offset=at)
        return Buf(h, shape, es, "sbuf", at)

    def wrap(self, handle, shape, dtype, space):
        es = 2 if dtype == BF16 else 4
        return Buf(handle, shape, es, space, 0)

    def add(self, eng, fn, reads=(), writes=(), dma=False):
        op = Op(eng, fn, dma)
        import sys as _s
        f_ = _s._getframe(1)
        ln_ = []
        while f_ is not None and len(ln_) < 3:
            ln_.append(f_.f_lineno)
            f_ = f_.f_back
        op.line = ln_
        deps = set()
        for v in reads:
            if v.space is None:
                continue
            t = self.trk.setdefault(v.space, _Track())
            lo, hi = v.lo, v.hi
            if v.space == "psum":
                lo, hi = lo // 2048 * 2048, (hi + 2047) // 2048 * 2048
            deps |= t.access(lo, hi, op, False)
        for v in writes:
            if v.space is None:
                continue
            t = self.trk.setdefault(v.space, _Track())
            lo, hi = v.lo, v.hi
            if v.space == "psum":
                lo, hi = lo // 2048 * 2048, (hi + 2047) // 2048 * 2048
            deps |= t.access(lo, hi, op, True)
        deps.discard(op)
        deps = {(d.redirect or d) for d in deps}
        deps.discard(op)
        for d in deps:
            if d.eng == eng and not d.is_dma and eng not in self.same_sync:
                continue
            if not d.is_dma:
                d.sig = True
            op.deps.append(d)
        if dma:
            k = self.n_dma[eng]
            self.n_dma[eng] += 1
            op.dsem = k % N_DMA_SEMS
            op.dcount = 16 * (k // N_DMA_SEMS + 1)
        self.ops[eng].append(op)
        self.all_ops += 1
        return op

    def dma(self, out, in_, eng="sp", **kw):
        return self.add(eng, lambda e: e.dma_start(out=out.ap, in_=in_.ap, **kw),
                        reads=[in_], writes=[out], dma=True)

    def emit(self, stack):
        nc = self.nc
        esem = {e: stack.enter_context(nc.semaphore("s_" + e)) for e in ENGS if e != "sp"}
        dsem = {}
        for e in ENGS:
            if self.n_dma[e]:
                dsem[e] = [stack.enter_context(nc.semaphore("d_%s_%d" % (e, i)))
                           for i in range(min(N_DMA_SEMS, self.n_dma[e]))]
        for e in ENGS:
            c = 0
            for op in self.ops[e]:
                if op.sig and not op.is_dma:
                    c += 1
                    op.count = c
        final_d = {e: {} for e in ENGS}
        for e in ENGS:
            for op in self.ops[e]:
                if op.is_dma:
                    final_d[e][op.dsem] = op.dcount

        def gen(e):
            def body(engobj):
                waited = {}
                for op in self.ops[e]:
                    need = {}
                    for d in op.deps:
                        if d.is_dma:
                            key = ("d", d.eng, d.dsem)
                            val = d.dcount
                        else:
                            key = ("e", d.eng)
                            val = d.count
                        if need.get(key, 0) < val:
                            need[key] = val
                    if op.is_dma and op.dcount > 16:
                        key = ("d", e, op.dsem)
                        need[key] = max(need.get(key, 0), op.dcount - 16)
                    op.wl = []
                    for key, val in need.items():
                        if waited.get(key, 0) >= val:
                            continue
                        op.wl.append((key, val))
                        waited[key] = val
                        sem = esem[key[1]] if key[0] == "e" else dsem[key[1]][key[2]]
                        engobj.wait_ge(sem, val)
                    ins = op.fn(engobj)
                    if op.is_dma:
                        ins.then_inc(dsem[e][op.dsem], 16)
                    elif op.sig:
                        ins.then_inc(esem[e], 1)
                for si, val in final_d[e].items():
                    if waited.get(("d", e, si), 0) < val:
                        engobj.wait_ge(dsem[e][si], val)
            return body

        with nc.Block() as block:
            block.sync(gen("sp"))
            block.tensor(gen("pe"))
            block.scalar(gen("act"))
            block.vector(gen("dve"))
            block.gpsimd(gen("pool"))
from contextlib import ExitStack

D = 1024
KC = 8
DFF = 2816
JF = 22
DPLE = 256
TT = 512
ST = 4
EPS = 1e-6
NSLOT = 6
import os as _os
DBG = _os.environ.get('KDBG', '')
SLOTW = 2816
CONVW = 31
HALO = CONVW - 1

WNAMES = ["ffn1_norm", "ffn1_w_in", "ffn1_w_out", "mix_norm", "w_in", "conv_w", "conv_b",
          "conv_ln_g", "conv_ln_b", "gate_w_up", "gate_b", "gla_norm", "w_out", "ffn2_norm",
          "ffn2_w_in", "ffn2_w_out", "ple_norm", "ple_w_gate", "ple_w_proj", "final_norm"]
WSHAPES = {"ffn1_norm": [1, 1024], "ffn1_w_in": [1, 1024, 5632], "ffn1_w_out": [1, 2816, 1024],
           "mix_norm": [1, 1024], "w_in": [1, 1024, 2576], "conv_w": [1, 31, 512], "conv_b": [1, 512],
           "conv_ln_g": [1, 512], "conv_ln_b": [1, 512], "gate_w_up": [1, 16, 256], "gate_b": [1, 256],
           "gla_norm": [1, 512], "w_out": [1, 1024, 1024], "ffn2_norm": [1, 1024],
           "ffn2_w_in": [1, 1024, 5632], "ffn2_w_out": [1, 2816, 1024], "ple_norm": [1, 1024],
           "ple_w_gate": [1, 1024, 1024], "ple_w_proj": [1, 256, 1024], "final_norm": [1024]}


def build_program(n_seq, seq_len, stop_after=None, same_sync=("act", "dve", "pool"), mix_stop=99):
    nc = bass.Bass("TRN2", target_bir_lowering=False)
    n_tiles = seq_len // TT
    x_d = nc.dram_tensor("x", [n_seq, seq_len, D], F32, kind="ExternalInput").ap()
    p_d = nc.dram_tensor("p", [n_seq, seq_len, DPLE], F32, kind="ExternalInput").ap()
    wd = {n: nc.dram_tensor(n, WSHAPES[n], F32, kind="ExternalInput").ap() for n in WNAMES}
    out_d = nc.dram_tensor("out", [n_seq, seq_len, D], F32, kind="ExternalOutput").ap()
    NBLK = 85
    wsc_d = nc.dram_tensor("wsc", [NBLK, 128, SLOTW], BF16, kind="Internal").ap()

    st = ExitStack()
    P = Prog(nc, same_sync=same_sync)

    def ext(ap):
        return View(ap, None, 0, 0)

    def A(func, out, in_, scale=None, bias=None, eng="act", accum=None):
        rd = [in_]
        kw = {}
        wr = [out]
        if accum is not None:
            kw["accum_out"] = accum.ap
            wr.append(accum)
        if scale is not None:
            if isinstance(scale, View):
                rd.append(scale); kw["scale"] = scale.ap
            else:
                kw["scale"] = float(scale)
        if bias is not None:
            if isinstance(bias, View):
                rd.append(bias); kw["bias"] = bias.ap
            else:
                kw["bias"] = float(bias)
        return P.add("act", lambda e: e.activation(out=out.ap, in_=in_.ap, func=func, **kw), reads=rd, writes=wr)

    def CP(eng, out, in_):
        if eng == "act":
            return A(AF.Copy, out, in_)
        return P.add(eng, lambda e: e.tensor_copy(out.ap, in_.ap), reads=[in_], writes=[out])

    def TTop(eng, out, a, b, op):
        return P.add(eng, lambda e: e.tensor_tensor(out=out.ap, in0=a.ap, in1=b.ap, op=op), reads=[a, b], writes=[out])

    def TS(eng, out, a, s1, s2, op0, op1=None):
        rd = [a]
        v1 = s1
        v2 = s2
        if isinstance(s1, View):
            rd.append(s1); v1 = s1.ap
        if isinstance(s2, View):
            rd.append(s2); v2 = s2.ap
        if op1 is None:
            return P.add(eng, lambda e: e.tensor_scalar(out=out.ap, in0=a.ap, scalar1=v1, scalar2=None, op0=op0), reads=rd, writes=[out])
        return P.add(eng, lambda e: e.tensor_scalar(out=out.ap, in0=a.ap, scalar1=v1, scalar2=v2, op0=op0, op1=op1), reads=rd, writes=[out])

    def STT(eng, out, a, s, b, op0, op1):
        rd = [a, b]
        sv = s
        if isinstance(s, View):
            rd.append(s); sv = s.ap
        return P.add(eng, lambda e: e.scalar_tensor_tensor(out=out.ap, in0=a.ap, scalar=sv, in1=b.ap, op0=op0, op1=op1), reads=rd, writes=[out])

    grp = {}

    def MM(out, lhsT, rhs, start, stop):
        op = P.add("pe", lambda e: e.matmul(out.ap, lhsT=lhsT.ap, rhs=rhs.ap, start=start, stop=stop), reads=[lhsT, rhs], writes=[out])
        key = out.lo
        if start:
            grp[key] = []
        grp.setdefault(key, []).append(op)
        if stop:
            for g_ in grp[key][:-1]:
                g_.redirect = op
            del grp[key]
        if pg["on"]:
            pg["ops"].append(op)
        return op

    pg = {"on": False, "ops": []}

    def TR(out, in_, idn):
        op = P.add("pe", lambda e: e.transpose(out.ap, in_.ap, idn.ap), reads=[in_, idn], writes=[out])
        if pg["on"]:
            pg["ops"].append(op)
        return op

    def gstart():
        pg["on"] = True
        pg["ops"] = []

    def gend():
        ops_ = pg["ops"]
        for o_ in ops_[:-1]:
            o_.redirect = ops_[-1]
        pg["on"] = False
        pg["ops"] = []

    def MS(eng, v, val):
        return P.add(eng, lambda e: e.memset(v.ap, val), writes=[v])

    banks = []
    for i in range(8):
        b = P.wrap(st.enter_context(nc.psum_tensor("ps%d" % i, [128, 512], F32)), [128, 512], F32, "psum")
        b.base = i * 2048
        banks.append(b)
    bctr = [0]

    def bank():
        b = banks[bctr[0] % 8]
        bctr[0] += 1
        return b

    def bank_bf(b):
        ap = b.h[:, :].bitcast(BF16)

        def v(lo, hi):
            return View(ap[:, lo:hi], "psum", b.base + lo * 2, b.base + hi * 2)
        return v

    sb = P.sbuf
    ident = sb("ident", [128, 128], F32)
    identb = sb("identb", [128, 128], BF16)
    c1024 = sb("c1024", [128, 128], BF16)
    c512 = sb("c512", [128, 128], BF16)
    c128 = sb("c128", [128, 128], BF16)
    triT2 = sb("triT2", [128, 2, 128], F32)
    u2 = sb("u2", [128, 2, 128], F32)
    gcols = sb("gcols", [128, 4, 8], F32)
    gfin = sb("gfin", [128, 8], F32)
    cw = sb("cw", [128, 4, CONVW], F32)
    cb = sb("cb", [128, 4], F32)
    lng = sb("lng", [128, 4], F32)
    lnb = sb("lnb", [128, 4], F32)
    gnorm = sb("gnorm", [128, 4], F32)
    waug_f = sb("waug_f", [128, 256], F32)
    waug = sb("waug", [128, 256], BF16)
    wglr_f = sb("wglr_f", [128, 8, 16], F32)
    wglr = sb("wglr", [128, 8, 128], BF16)
    gaug = sb("gaug", [128, TT], BF16)
    Sst = [sb("S%d" % i, [128, 2, 128], F32) for i in range(n_seq)]
    Sprev = sb("Sprev", [128, 2, 8, 128], BF16)
    hT = sb("hT", [128, KC, TT], F32)
    xT = sb("xT", [128, KC, TT], BF16)
    xin = sb("xin", [128, ST, D], F32)
    osb = sb("osb", [128, ST, D], F32, at=xin.base + ST * D * 4)
    stgpad = sb("stgpad", [128, 256], F32, at=xin.base + 2 * ST * D * 4)
    P.sb_off = xin.base + 2 * ST * D * 4 + 1024
    pin = sb("pin", [128, ST, DPLE], F32)
    pbf = sb("pbf", [128, ST, DPLE], BF16)
    pT = sb("pT", [128, 2, TT], BF16)
    wr_addr = []
    wr_in, wr_out, wr_pp = [], [], []
    for i in range(NSLOT):
        b = sb("wr%d" % i, [128, KC, 256], BF16)
        P.sb_off = b.base + SLOTW * 2
        wr_in.append(b)
        wr_out.append(sb("wro%d" % i, [128, JF, 128], BF16, at=b.base))
        wr_pp.append(sb("wrp%d" % i, [128, 2, 1024], BF16, at=b.base))
    sgb = [sb("sg%d" % i, [128, TT], F32) for i in range(3)]
    sqb = [sb("sq%d" % i, [128, TT], BF16) for i in range(3)]
    rstd_t = [sb("rstd%d" % i, [128, TT], F32) for i in range(2)]
    lnt = sb("lnt", [128, TT], F32)
    qd = sb("qd", [128, 2, TT], BF16)
    qn = sb("qn", [128, 2, TT], BF16)
    kn = sb("kn", [128, 2, TT], BF16)
    kp = sb("kp", [128, 2, TT], BF16)
    vtok = sb("vtok", [128, ST, 512], BF16)
    kdtok = sb("kdtok", [128, ST, 256], BF16)
    scT = [sb("scT%d" % i, [128, 2, 2, 128], BF16) for i in range(2)]
    mt1 = [sb("mt1_%d" % i, [128, 2, 128], F32) for i in range(2)]
    mt2 = [sb("mt2_%d" % i, [128, 2, 128], F32) for i in range(2)]
    osq = sb("osq", [128, 4, TT], BF16)
    region0 = (P.sb_off + 63) // 64 * 64
    hid = sb("hid", [128, JF, TT], BF16)
    ymix = sb("ymix", [128, 8, TT], BF16)
    acc = sb("acc", [128, 4, TT], F32)
    glu = sb("glu", [128, 4, HALO + TT], BF16)
    epos = sb("epos", [128, 2, TT], F32)
    eneg = sb("eneg", [128, 2, TT], F32)
    region1 = P.sb_off
    assert qn.base == qd.base + 2048 and kn.base == qd.base + 4096 and kp.base == qd.base + 6144
    xTn = sb("xTn", [128, KC, TT], BF16, at=qd.base)
    xnb = [sb("xnb%d" % i, [128, D], BF16, at=osq.base + i * 2048) for i in range(2)]
    sscol = sb("sscol", [128, ST], F32)
    tcol = sb("tcol", [128, ST], F32)
    rcol = sb("rcol", [128, ST], F32)
    rsil = sb("rsil", [128, 4, TT], F32, at=hid.base)
    o_sb = sb("o_sb", [128, 4, TT], F32, at=hid.base + 8192)
    Lb = sb("Lb", [128, ST, 256], F32, at=hid.base + 16384)
    eR = [sb("eR%d" % i, [128, 256], F32, at=hid.base + 20480 + i * 1024) for i in range(2)]
    assert hid.base + 22528 >= hid.base + 20480 + 2048
    ctmp = [sb("ctmp0", [128, TT], F32, at=pbf.base), sb("ctmp1", [128, TT], F32, at=pT.base)]
    NSTG = 3
    stg_f = []
    a0 = xin.base
    for i in range(NSTG):
        stg_f.append({"in": sb("sfi%d" % i, [128, KC, 256], F32, at=a0),
                      "out": sb("sfo%d" % i, [128, JF, 128], F32, at=a0),
                      "pp": sb("sfp%d" % i, [128, 2, 1024], F32, at=a0)})
        a0 += SLOTW * 4
    assert a0 <= xin.base + 2 * ST * D * 4 + 1024
    print("SBUF used up to", P.sb_off, "cap", P.sb_cap)

    MS("pool", ident.v(), 0.0)
    P.add("pool", lambda e: e.affine_select(out=ident.v().ap, in_=ident.v().ap, pattern=[[-1, 128]],
                                           compare_op=ALU.not_equal, fill=1.0, base=0, channel_multiplier=1),
          reads=[ident.v()], writes=[ident.v()])
    CP("dve", identb.v(), ident.v())
    MS("pool", c1024.v(), 1.0 / 1024)
    MS("pool", c512.v(), 1.0 / 512)
    MS("pool", c128.v(), 1.0 / 128)
    for e in range(2):
        MS("pool", triT2.v(e), 1.0)
        P.add("pool", (lambda e_: lambda g: g.affine_select(out=triT2.v(e_).ap, in_=triT2.v(e_).ap, pattern=[[1, 128]],
                                                            compare_op=ALU.is_ge, fill=0.0, base=0, channel_multiplier=-1))(e),
              reads=[triT2.v(e)], writes=[triT2.v(e)])
        MS("pool", triT2.v(e, slice(64, 128), p=slice(0, 64)), 0.0)
        MS("pool", u2.v(e), 1.0)
        P.add("pool", (lambda e_: lambda g: g.affine_select(out=u2.v(e_).ap, in_=u2.v(e_).ap, pattern=[[-1, 128]],
                                                            compare_op=ALU.is_gt, fill=0.0, base=0, channel_multiplier=1))(e),
              reads=[u2.v(e)], writes=[u2.v(e)])
        MS("pool", u2.v(e, slice(0, 64), p=slice(64, 128)), 0.0)

    def small_dma(out, ap):
        P.add("sp", lambda e: e.dma_start(out=out.ap, in_=ap, allow_slow_non_contiguous=True), writes=[out], dma=True)

    for i, nm in enumerate(["ffn1_norm", "mix_norm", "ffn2_norm", "ple_norm"]):
        small_dma(gcols.v(i), wd[nm].rearrange("o (c p) -> p (o c)", p=128))
    small_dma(gfin.v(), wd["final_norm"].rearrange("(c p) -> p c", p=128))
    for cc_ in range(4):
        small_dma(cw.v(cc_), wd["conv_w"][0][:, cc_ * 128:(cc_ + 1) * 128].rearrange("j p -> p j"))
    small_dma(cb.v(), wd["conv_b"].rearrange("o (c p) -> p (o c)", p=128))
    small_dma(lng.v(), wd["conv_ln_g"].rearrange("o (c p) -> p (o c)", p=128))
    small_dma(lnb.v(), wd["conv_ln_b"].rearrange("o (c p) -> p (o c)", p=128))
    small_dma(gnorm.v(), wd["gla_norm"].rearrange("o (c p) -> p (o c)", p=128))
    MS("pool", waug_f.v(), 0.0)
    P.dma(waug_f.v(p=slice(0, 16)), ext(wd["gate_w_up"][0]))
    P.dma(waug_f.v(p=slice(16, 17)), ext(wd["gate_b"]))
    CP("dve", waug.v(), waug_f.v())
    MS("pool", gaug.v(), 1.0)
    P.dma(wglr_f.v(), ext(wd["w_in"][0][:, 2560:2576].rearrange("(c p) x -> p c x", p=128)))
    MS("pool", wglr.v(), 0.0)
    for c in range(KC):
        TS("dve", wglr.v(c, slice(0, 16)), wglr_f.v(c), gcols.v(1, slice(c, c + 1)), None, ALU.mult)
    for i in range(n_seq):
        MS("pool", Sst[i].v(), 0.0)

    blocks = []

    def rin(w, c0, n):
        return w[:, c0:c0 + n].rearrange("(c p) x -> p c x", p=128)

    def add_ffn(pref, gi):
        wi = wd[pref + "_w_in"][0]
        wo = wd[pref + "_w_out"][0]
        for j in range(JF):
            blocks.append(("in", [(rin(wi, j * 128, 128), 0, 128), (rin(wi, DFF + j * 128, 128), 128, 256)], gi, None))
        for m in range(KC):
            blocks.append(("out", [(rin(wo, m * 128, 128), 0, 128)], None, 0.5))

    add_ffn("ffn1", 0)
    BLK_MIX = len(blocks)
    wm = wd["w_in"][0]
    for c0 in (0, 256, 512, 768):
        blocks.append(("in", [(rin(wm, c0, 256), 0, 256)], 1, None))
    for b_ in range(6):
        blocks.append(("dg", [], b_, None))
    for c0 in (1024, 1280, 1536, 1792, 2048, 2304):
        blocks.append(("in", [(rin(wm, c0, 256), 0, 256)], 1, None))
    for q in range(4):
        blocks.append(("in", [(rin(wd["w_out"][0], q * 256, 256), 0, 256)], None, None))
    BLK_FFN2 = len(blocks)
    add_ffn("ffn2", 2)
    BLK_PLE = len(blocks)
    for q in range(4):
        blocks.append(("in", [(rin(wd["ple_w_gate"][0], q * 256, 256), 0, 256)], 3, None))
    blocks.append(("pp", [(wd["ple_w_proj"][0].rearrange("(c p) x -> p c x", p=128), 0, 1024)], None, None))
    assert len(blocks) == NBLK
    BLKN = {"in": 2048, "out": 2816, "pp": 2048, "dg": 2816}

    def wsc_view(bi, n):
        return View(wsc_d[bi, :, 0:n], "wsc", bi, bi + 1)

    cmap = ["dve", "act", "dve", "act", "dve", "dve", "act", "dve"]

    def cast1(eng, dst, src, scal):
        if eng == "act":
            A(AF.Copy, dst, src, scale=scal)
        elif scal is None:
            CP(eng, dst, src)
        else:
            TS(eng, dst, src, scal, None, ALU.mult)

    def pro_cast_store(bi, slot):
        kind, srcs, gi, cs = blocks[bi]
        n = BLKN[kind]
        flat_src = wr_out[slot] if n > 2048 else wr_in[slot]
        flat = View(flat_src.h[:, :, :].rearrange("p a b -> p (a b)"), "sbuf", wr_in[slot].base, wr_in[slot].base + n * 2)
        if kind == "dg":
            sbb = wr_out[slot]
            for k_ in range(22):
                idx_ = gi * 22 + k_
                if idx_ >= 4 * CONVW:
                    MS("pool", sbb.v(k_), 0.0)
                    continue
                cc_, j_ = idx_ // CONVW, idx_ % CONVW
                cast1("dve" if k_ % 2 else "act", sbb.v(k_), ident.v(), cw.v(cc_, slice(j_, j_ + 1)))
            P.dma(wsc_view(bi, n), flat, eng="act")
            return
        sf = stg_f[pro["nload_idx"][bi] % NSTG][kind]
        sbb = {"in": wr_in, "out": wr_out, "pp": wr_pp}[kind][slot]
        if gi is not None:
            for c in range(KC):
                cast1(cmap[(c + bi) % 8], sbb.v(c), sf.v(c), gcols.v(gi, slice(c, c + 1)))
        else:
            nk = sf.shape[1]
            cuts = [0, (nk * 5) // 9, nk]
            for eng, a_, b_ in zip(("dve", "act"), cuts[:-1], cuts[1:]):
                if b_ > a_:
                    sc_ = cs if cs is not None else (1.0 if eng == "act" else None)
                    cast1(eng, sbb.v(slice(a_, b_)), sf.v(slice(a_, b_)), sc_)
        P.dma(wsc_view(bi, n), flat, eng="act")

    pro = {"loaded": 0, "nload": 0, "nload_idx": {}}

    def pro_prefetch(upto):
        while pro["loaded"] < min(upto, NBLK):
            bi = pro["loaded"]
            if blocks[bi][0] != "dg":
                pro["nload_idx"][bi] = pro["nload"]
                kind, srcs, gi, cs = blocks[bi]
                sf = stg_f[pro["nload"] % NSTG][kind]
                for (ap, lo, hi) in srcs:
                    P.dma(sf.v(slice(None), slice(lo, hi)), ext(ap))
                pro["nload"] += 1
            pro["loaded"] += 1

    tiles = [(sq_, ti) for sq_ in range(n_seq) for ti in range(n_tiles)]
    total_blocks = len(tiles) * NBLK
    wst = {"issued": 0, "consumed": 0, "released": 0}

    def _wissue():
        while wst["issued"] < total_blocks and wst["issued"] - NSLOT < wst["released"]:
            k = wst["issued"]
            bi = k % NBLK
            n = BLKN[blocks[bi][0]]
            slot = k % NSLOT
            if k < NBLK:
                pro_prefetch(k + NSTG)
                pro_cast_store(bi, slot)
            else:
                src_buf = wr_out[slot] if n > 2048 else wr_in[slot]
                dst = View(src_buf.h[:, :, :].rearrange("p a b -> p (a b)"), "sbuf", wr_in[slot].base, wr_in[slot].base + n * 2)
                P.dma(dst, wsc_view(bi, n))
            wst["issued"] += 1

    def wnext():
        _wissue()
        k = wst["consumed"]
        assert k < wst["issued"], "weight ring too small for the number of blocks held"
        wst["consumed"] += 1
        kind = blocks[k % NBLK][0]
        slot = k % NSLOT
        return {"in": wr_in, "out": wr_out, "pp": wr_pp, "dg": wr_out}[kind][slot]

    def wdone(n=1):
        wst["released"] += n
        assert wst["released"] <= wst["consumed"]
        _wissue()

    rctr = [0]

    def rmsnorm_stats(src, cmat, nchunks, eps_val=EPS, sqdst=None):
        bk = bank()
        for m in range(nchunks):
            sq = sqb[m % 3].v() if sqdst is None else sqdst(m)
            A(AF.Square, sq, src(m))
            MM(bk.v(), cmat.v(), sq, m == 0, m == nchunks - 1)
        A(AF.Ln, lnt.v(), bk.v(), bias=eps_col.v())
        r = rstd_t[rctr[0] % 2]
        rctr[0] += 1
        A(AF.Exp, r.v(), lnt.v(), scale=-0.5)
        return r

    def norm_to_xT():
        r = rmsnorm_stats(lambda m: hT.v(m), c1024, KC, sqdst=lambda m: xT.v(m))
        for m in range(KC):
            TTop("dve", xT.v(m), hT.v(m), r.v(), ALU.mult)

    def ffn_block(j, xsrc):
        W = wnext()
        bg = bank()
        for c in range(KC):
            MM(bg.v(), W.v(c, slice(0, 128)), xsrc.v(c), c == 0, c == KC - 1)
        bu = bank()
        for c in range(KC):
            MM(bu.v(), W.v(c, slice(128, 256)), xsrc.v(c), c == 0, c == KC - 1)
        sg = sgb[j % 3]
        A(AF.Silu, sg.v(), bg.v())
        TTop("dve", hid.v(j), sg.v(), bu.v(), ALU.mult)
        wdone()

    def ffn_up(js, xsrc, hooks=None):
        for j in js:
            ffn_block(j, xsrc)
            if hooks and j in hooks:
                for h_ in hooks[j]:
                    h_()

    def ffn_up_first2(xsrc):
        bb = [(wnext(), bank(), bank()), (wnext(), bank(), bank())]
        for c in range(KC):
            for (W, bg, bu) in bb:
                MM(bg.v(), W.v(c, slice(0, 128)), xsrc.v(c), c == 0, c == KC - 1)
                MM(bu.v(), W.v(c, slice(128, 256)), xsrc.v(c), c == 0, c == KC - 1)
        for jj, (W, bg, bu) in enumerate(bb):
            sg = sgb[jj % 3]
            A(AF.Silu, sg.v(), bg.v())
            TTop("dve", hid.v(jj), sg.v(), bu.v(), ALU.mult)
        wdone(2)

    def ffn_down():
        for m in range(KC):
            W = wnext()
            bk = bank()
            for j in range(JF):
                MM(bk.v(), W.v(j), hid.v(j), j == 0, j == JF - 1)
            TTop("dve", hT.v(m), hT.v(m), bk.v(), ALU.add)
            wdone()

    def prepA(s):
        xb_ = xnb[s % 2]
        A(AF.Square, xb_.v(), xin.v(s), accum=sscol.v(slice(s, s + 1)))
        A(AF.Ln, tcol.v(slice(s, s + 1)), sscol.v(slice(s, s + 1)), scale=1.0 / D, bias=eps_col.v())
        A(AF.Exp, rcol.v(slice(s, s + 1)), tcol.v(slice(s, s + 1)), scale=-0.5)
        A(AF.Copy, xb_.v(), xin.v(s), scale=rcol.v(slice(s, s + 1)))

    def prepB(s):
        xb_ = xnb[s % 2]
        bk = bank()
        bv = bank_bf(bk)
        gstart()
        for c in range(KC):
            TR(bv(c * 128, (c + 1) * 128), xb_.v(slice(c * 128, (c + 1) * 128)), identb.v())
        gend()
        src = View(bk.h[:, :].bitcast(BF16).rearrange("p (c t) -> p c t", c=KC), "psum", bk.base, bk.base + 2048)
        CP("dve", xTn.v(slice(None), slice(s * 128, (s + 1) * 128)), src)

    def load_x(sq_, ti):
        P.dma(xin.v(), ext(x_d[sq_, ti * TT:(ti + 1) * TT, :].rearrange("(s p) d -> p s d", p=128)))

    def load_p(sq_, ti):
        P.dma(pin.v(), ext(p_d[sq_, ti * TT:(ti + 1) * TT, :].rearrange("(s p) d -> p s d", p=128)))

    def x_to_hT():
        for m in range(KC):
            bk = bank()
            gstart()
            for s in range(ST):
                TR(bk.v(slice(s * 128, (s + 1) * 128)), xin.v(s, slice(m * 128, (m + 1) * 128)), ident.v())
            gend()
            CP("act" if m % 2 == 0 else "dve", hT.v(m), bk.v())

    def final_part():
        r = rmsnorm_stats(lambda m: hT.v(m), c1024, KC)
        for m in range(KC):
            STT("dve", hT.v(m), hT.v(m), gfin.v(slice(m, m + 1)), r.v(), ALU.mult, ALU.mult)

    def store_out(sq_, ti, final):
        for s in range(ST):
            for half in range(2):
                bk = bank()
                gstart()
                for mm in range(4):
                    m = half * 4 + mm
                    TR(bk.v(slice(mm * 128, (mm + 1) * 128)), hT.v(m, slice(s * 128, (s + 1) * 128)), ident.v())
                gend()
                CP("act" if half == 0 else "dve", osb.v(s, slice(half * 512, (half + 1) * 512)), bk.v())
        P.dma(View(out_d[sq_, ti * TT:(ti + 1) * TT, :].rearrange("(s p) d -> p s d", p=128), "out", 0, 1), osb.v())

    def mix(sq_, ti):
        S = Sst[sq_]
        norm_to_xT()
        Wa = [wnext(), wnext()]
        Wg = [wnext(), wnext()]
        if ti == 0:
            MS("pool", glu.v(slice(None), slice(0, HALO)), 0.0)
        else:
            CP("pool", glu.v(slice(None), slice(0, HALO)), glu.v(slice(None), slice(TT, TT + HALO)))
        bb = [(cc, bank(), bank()) for cc in range(2)]
        for c in range(KC):
            for (cc, bg, ba) in bb:
                MM(bg.v(), Wg[0].v(c, slice(cc * 128, cc * 128 + 128)), xT.v(c), c == 0, c == KC - 1)
                MM(ba.v(), Wa[0].v(c, slice(cc * 128, cc * 128 + 128)), xT.v(c), c == 0, c == KC - 1)
        for (cc, bg, ba) in bb:
            sg = sgb[cc % 3]
            A(AF.Sigmoid, sg.v(), bg.v())
            TTop("dve", glu.v(cc, slice(HALO, HALO + TT)), sg.v(), ba.v(), ALU.mult)
        for cc in range(2, 4):
            bg = bank()
            for c in range(KC):
                MM(bg.v(), Wg[cc // 2].v(c, slice((cc % 2) * 128, (cc % 2) * 128 + 128)), xT.v(c), c == 0, c == KC - 1)
            ba = bank()
            for c in range(KC):
                MM(ba.v(), Wa[cc // 2].v(c, slice((cc % 2) * 128, (cc % 2) * 128 + 128)), xT.v(c), c == 0, c == KC - 1)
            sg = sgb[cc % 3]
            A(AF.Sigmoid, sg.v(), bg.v())
            TTop("dve", glu.v(cc, slice(HALO, HALO + TT)), sg.v(), ba.v(), ALU.mult)
        wdone(4)
        held = {}
        NPE = 3
        for idx in range(NPE * CONVW):
            cc, j = idx // CONVW, idx % CONVW
            if j == 0:
                bc = bank()
            blk = idx // 22
            if blk not in held:
                held[blk] = wnext()
            MM(bc.v(), held[blk].v(idx % 22), glu.v(cc, slice(j, j + TT)), j == 0, j == CONVW - 1)
            if idx % 22 == 21:
                wdone()
            if j == CONVW - 1:
                A(AF.Identity, acc.v(cc), bc.v(), bias=cb.v(slice(cc, cc + 1)))
        nheld = len(held) - (NPE * CONVW) // 22
        if nheld > 0:
            wdone(nheld)
        for _ in range(6 - len(held)):
            wnext()
            wdone()
        conv_items = [(j, cc) for j in range(CONVW) for cc in range(NPE, 4)]

        def conv_emit(n):
            for _ in range(n):
                if not conv_items:
                    return
                j, cc = conv_items.pop(0)
                src = glu.v(cc, slice(j, j + TT))
                wj = cw.v(cc, slice(j, j + 1))
                if j == 0:
                    TS("dve", acc.v(cc), src, wj, cb.v(slice(cc, cc + 1)), ALU.mult, ALU.add)
                else:
                    STT("dve", acc.v(cc), src, wj, acc.v(cc), ALU.mult, ALU.add)
        conv_emit(8)
        bk = bank()
        for c in range(KC):
            MM(bk.v(), wglr.v(c), xT.v(c), c == 0, c == KC - 1)
        CP("act", gaug.v(p=slice(0, 16)), bk.v(p=slice(0, 16)))
        for s2 in range(2):
            bpre = bank()
            gstart()
            for s in (2 * s2, 2 * s2 + 1):
                dst = bpre.v(slice((s % 2) * 256, (s % 2) * 256 + 256))
                MM(dst, gaug.v(slice(s * 128, (s + 1) * 128)), waug.v(), True, True)
            gend()
            for s in (2 * s2, 2 * s2 + 1):
                dst = bpre.v(slice((s % 2) * 256, (s % 2) * 256 + 256))
                A(AF.Exp, Lb.v(s), dst, scale=-1.0)
        for s in range(ST):
            A(AF.Ln, Lb.v(s), Lb.v(s), bias=one_col.v())
        bcum = [bank(), bank()]
        for cd in range(2):
            gstart()
            for s in range(ST):
                MM(bcum[cd].v(slice(s * 128, (s + 1) * 128)), Lb.v(s, slice(cd * 128, (cd + 1) * 128)), triT2.v(0), True, True)
            gend()
        for cd in range(2):
            A(AF.Exp, epos.v(cd), bcum[cd].v(), scale=-1.0 / 16)
            A(AF.Exp, eneg.v(cd), bcum[cd].v(), scale=1.0 / 16)
        conv_emit(12)
        Wq = wnext()
        Wk = wnext()
        for cd in range(2):
            bq = bank()
            for c in range(KC):
                MM(bq.v(), Wq.v(c, slice(cd * 128, (cd + 1) * 128)), xT.v(c), c == 0, c == KC - 1)
            STT("dve", qd.v(cd), bq.v(), 0.125, epos.v(cd), ALU.mult, ALU.mult)
            STT("dve", qn.v(cd), bq.v(), 0.125, eneg.v(cd), ALU.mult, ALU.mult)
            bkk = bank()
            for c in range(KC):
                MM(bkk.v(), Wk.v(c, slice(cd * 128, (cd + 1) * 128)), xT.v(c), c == 0, c == KC - 1)
            TTop("dve", kn.v(cd), bkk.v(), eneg.v(cd), ALU.mult)
            TTop("dve", kp.v(cd), bkk.v(), epos.v(cd), ALU.mult)
            conv_emit(6)
        for s in range(ST):
            br = bank()
            MM(br.v(slice(0, 256)), u2.v(0), Lb.v(s), True, True)
            er = eR[s % 2]
            A(AF.Exp, er.v(), br.v(slice(0, 256)), scale=-1.0 / 16)
            bk2_ = bank()
            for c in range(KC):
                MM(bk2_.v(slice(0, 256)), xT.v(c, slice(s * 128, (s + 1) * 128)), Wk.v(c), c == 0, c == KC - 1)
            TTop("dve", kdtok.v(s), bk2_.v(slice(0, 256)), er.v(), ALU.mult)
            conv_emit(3)
        wdone(2)
        Wv = [wnext(), wnext()]
        Wr = [wnext(), wnext()]

        def vtok_proj(s):
            bv = bank()
            gstart()
            for piece in range(2):
                for c in range(KC):
                    MM(bv.v(slice(piece * 256, (piece + 1) * 256)), xT.v(c, slice(s * 128, (s + 1) * 128)), Wv[piece].v(c),
                       c == 0, c == KC - 1)
            gend()
            CP("act", vtok.v(s), bv.v())

        def r_proj(h):
            brr = bank()
            for c in range(KC):
                MM(brr.v(), Wr[h // 2].v(c, slice((h % 2) * 128, (h % 2) * 128 + 128)), xT.v(c), c == 0, c == KC - 1)
            A(AF.Silu, rsil.v(h), brr.v())

        vtok_proj(0)
        for pr in range(2):
            CP("act", Sprev.v(pr, 0), S.v(pr))

        def chain(c8):
            s, cc = c8 // 2, c8 % 2
            rows = slice(cc * 64, (cc + 1) * 64)
            bkv = bank()
            gstart()
            for pr in range(2):
                MM(bkv.v(slice(pr * 256, (pr + 1) * 256)), kdtok.v(s, slice(pr * 128, (pr + 1) * 128), p=rows),
                   vtok.v(s, slice(pr * 256, (pr + 1) * 256), p=rows), True, True)
            gend()
            tl = s * 128 + cc * 64 + 63
            for snap in (True, False):
                if snap and c8 == 7:
                    continue
                for pr in range(2):
                    for e in range(2):
                        pp_ = slice(e * 64, (e + 1) * 64)
                        dst = Sprev.v(pr, c8 + 1, p=pp_) if snap else S.v(pr, p=pp_)
                        STT("dve", dst, S.v(pr, p=pp_), epos.v(pr, slice(tl, tl + 1), p=pp_),
                            bkv.v(slice(pr * 256 + e * 128, pr * 256 + e * 128 + 128), p=pp_), ALU.mult, ALU.add)

        lnst = {}

        def ln_mid(s):
            if s == 0:
                for cc in range(4):
                    CP("act", ymix.v(cc), acc.v(cc))
            elif s == 1:
                for cc in range(4):
                    TTop("dve", acc.v(cc), acc.v(cc), lnst["bm"].v(), ALU.subtract)
                    A(AF.Square, ymix.v(cc), acc.v(cc))
            elif s == 2:
                A(AF.Ln, lnt.v(), lnst["bv"].v(), bias=eps_col.v())
                r_ = rstd_t[rctr[0] % 2]
                rctr[0] += 1
                A(AF.Exp, r_.v(), lnt.v(), scale=-0.5)
                lnst["rc"] = r_
            else:
                for cc in range(4):
                    A(AF.Silu, ymix.v(cc), acc.v(cc), scale=lng.v(slice(cc, cc + 1)), bias=lnb.v(slice(cc, cc + 1)))

        def ln_end(s):
            if s == 0:
                lnst["bm"] = bank()
                for cc in range(4):
                    MM(lnst["bm"].v(), c512.v(), ymix.v(cc), cc == 0, cc == 3)
            elif s == 1:
                lnst["bv"] = bank()
                for cc in range(4):
                    MM(lnst["bv"].v(), c512.v(), ymix.v(cc), cc == 0, cc == 3)
            elif s == 2:
                for cc in range(4):
                    TTop("dve", acc.v(cc), acc.v(cc), lnst["rc"].v(), ALU.mult)

        for s in range(ST):
            ts = slice(s * 128, (s + 1) * 128)
            sc = scT[s % 2]
            chain(2 * s)
            conv_emit(2)
            bse = [bank(), bank()]
            for e in range(2):
                hp = slice(e * 64, (e + 1) * 64)
                gstart()
                for pr in range(2):
                    MM(bse[e].v(slice(pr * 256, pr * 256 + 128)), kn.v(pr, ts, p=hp), qd.v(pr, ts, p=hp), True, True)
                    MM(bse[e].v(slice(pr * 256 + 128, pr * 256 + 256)), kp.v(pr, ts, p=hp), qn.v(pr, ts, p=hp), True, True)
                gend()
            for e in range(2):
                b4 = bse[e].h[:, :].rearrange("p (r l t) -> p r l t", r=2, l=2)
                lo_v = View(b4[:, :, 0, :], "psum", bse[e].base, bse[e].base + 2048)
                up_v = View(b4[:, :, 1, :], "psum", bse[e].base, bse[e].base + 2048)
                t1 = mt1[e]
                t2 = mt2[e]
                TTop("dve", t1.v(), lo_v, triT2.v(), ALU.mult)
                TTop("dve", t2.v(), up_v, u2.v(), ALU.mult)
                TTop("pool", sc.v(slice(None), e), t1.v(), t2.v(), ALU.add)
            ln_mid(s)
            r_proj(s)
            if s + 1 < ST:
                vtok_proj(s + 1)
            chain(2 * s + 1)
            conv_emit(4)
            bo = bank()
            gstart()
            for h in range(4):
                pr, e = h // 2, h % 2
                hp = slice(e * 64, (e + 1) * 64)
                MM(bo.v(slice(h * 128, (h + 1) * 128)), vtok.v(s, slice(h * 128, (h + 1) * 128)), sc.v(pr, e), True, False)
                for cc in range(2):
                    c8 = 2 * s + cc
                    cols = slice(h * 128 + cc * 64, h * 128 + cc * 64 + 64)
                    MM(bo.v(cols), Sprev.v(pr, c8, p=hp), qd.v(pr, slice(s * 128 + cc * 64, s * 128 + cc * 64 + 64), p=hp), False, True)
            gend()
            b3 = bo.h[:, :].rearrange("p (h t) -> p h t", h=4)
            bo_v = View(b3, "psum", bo.base, bo.base + 2048)
            A(AF.Copy, o_sb.v(slice(None), ts), bo_v)
            A(AF.Square, osq.v(slice(None), ts), bo_v)
            ln_end(s)
            conv_emit(6)
        wdone(4)
        conv_emit(1000)
        for h in range(4):
            bk2 = bank()
            MM(bk2.v(), c128.v(), osq.v(h), True, True)
            A(AF.Ln, lnt.v(), bk2.v(), bias=eps_col.v())
            r = rstd_t[rctr[0] % 2]
            rctr[0] += 1
            A(AF.Exp, r.v(), lnt.v(), scale=-0.5)
            TTop("pool", rsil.v(h), rsil.v(h), r.v(), ALU.mult)
            STT("dve", ymix.v(4 + h), o_sb.v(h), gnorm.v(slice(h, h + 1)), rsil.v(h), ALU.mult, ALU.mult)
        Wo = [wnext() for _ in range(4)]
        bks = [bank() for _ in range(4)]
        for m in range(4):
            for kc in range(4):
                MM(bks[m].v(), Wo[m // 2].v(kc, slice((m % 2) * 128, (m % 2) * 128 + 128)), ymix.v(kc), kc == 0, False)
        for m in range(4):
            for kc in range(4, 8):
                MM(bks[m].v(), Wo[m // 2].v(kc, slice((m % 2) * 128, (m % 2) * 128 + 128)), ymix.v(kc), False, kc == 7)
            TTop("dve", hT.v(m), hT.v(m), bks[m].v(), ALU.add)
        for m in range(4, KC):
            bk3 = bank()
            for kc in range(8):
                MM(bk3.v(), Wo[m // 2].v(kc, slice((m % 2) * 128, (m % 2) * 128 + 128)), ymix.v(kc), kc == 0, kc == 7)
            TTop("dve", hT.v(m), hT.v(m), bk3.v(), ALU.add)
        wdone(4)

    def ple():
        CP("pool", pbf.v(), pin.v())
        for kc in range(2):
            bk = bank()
            bv = bank_bf(bk)
            gstart()
            for s in range(ST):
                TR(bv(s * 128, (s + 1) * 128), pbf.v(s, slice(kc * 128, (kc + 1) * 128)), identb.v())
            gend()
            CP("act", pT.v(kc), bv(0, 512))
        norm_to_xT()
        Wg = [wnext() for _ in range(4)]
        Wp = wnext()

        def pproj(m):
            bp = bank()
            for kc in range(2):
                MM(bp.v(), Wp.v(kc, slice(m * 128, (m + 1) * 128)), pT.v(kc), kc == 0, kc == 1)
            return bp

        def fin(m, bg, bp):
            sg = sgb[m % 3]
            A(AF.Sigmoid, sg.v(), bg.v())
            TTop("dve", sg.v(), sg.v(), bp.v(), ALU.mult)
            TTop("pool", hT.v(m), hT.v(m), sg.v(), ALU.add)

        bps = [pproj(m) for m in range(4)]
        bgs = [bank() for _ in range(4)]
        for c in range(KC):
            for m in range(4):
                MM(bgs[m].v(), Wg[m // 2].v(c, slice((m % 2) * 128, (m % 2) * 128 + 128)), xT.v(c), c == 0, c == KC - 1)
        for m in range(4):
            fin(m, bgs[m], bps[m])
        for m in range(4, KC):
            bg = bank()
            for c in range(KC):
                MM(bg.v(), Wg[m // 2].v(c, slice((m % 2) * 128, (m % 2) * 128 + 128)), xT.v(c), c == 0, c == KC - 1)
            bp = pproj(m)
            fin(m, bg, bp)
        wdone(5)

    eps_col = sb("eps_col", [128, 1], F32)
    one_col = sb("one_col", [128, 1], F32)
    MS("pool", eps_col.v(), EPS)
    MS("pool", one_col.v(), 1.0)

    NPRE = 6
    assert stop_after is None
    load_x(*tiles[0])
    load_p(*tiles[0])
    for s_ in range(ST):
        prepA(s_)
        prepB(s_)
    x_to_hT()
    ffn_up(range(0, NPRE), xTn)
    for idx, (sq_, ti) in enumerate(tiles):
        has_next = idx + 1 < len(tiles)
        if idx > 0:
            x_to_hT()
        if has_next and idx > 0:
            load_x(*tiles[idx + 1])
        ffn_up(range(NPRE, JF), xTn)
        ffn_down()
        mix(sq_, ti)
        hooks = None
        if has_next and idx > 0:
            hooks = {3: [lambda: prepA(0)], 6: [lambda: prepB(0), lambda: prepA(1)], 9: [lambda: prepB(1), lambda: prepA(2)],
                     12: [lambda: prepB(2), lambda: prepA(3)], 15: [lambda: prepB(3)]}
        norm_to_xT()
        ffn_up_first2(xT)
        ffn_up(range(2, JF), xT, hooks)
        ffn_down()
        ple()
        if has_next:
            load_p(*tiles[idx + 1])
            if idx == 0:
                load_x(*tiles[idx + 1])
                for s_ in range(ST):
                    prepA(s_)
                    prepB(s_)
        final_part()
        if has_next:
            ffn_up(range(0, NPRE), xTn)
        store_out(sq_, ti, True)
    P.emit(st)
    st.close()
    return nc, P

from concourse.bass_utils import run_bass_kernel_spmd

N_CORES = 8
_PROG_CACHE = {}


def kernel(**inputs):
    x = np.ascontiguousarray(np.asarray(inputs["x"], dtype=np.float32))
    p = np.ascontiguousarray(np.asarray(inputs["p"], dtype=np.float32))
    B, T, _ = x.shape
    spc = B // N_CORES
    key = (spc, T)
    if key not in _PROG_CACHE:
        _PROG_CACHE[key] = build_program(spc, T)[0]
    nc = _PROG_CACHE[key]
    wts = {n: np.ascontiguousarray(np.asarray(inputs[n], dtype=np.float32)) for n in WNAMES}
    in_maps = []
    for c in range(N_CORES):
        m = {"x": x[c * spc:(c + 1) * spc], "p": p[0, c * spc:(c + 1) * spc]}
        m.update(wts)
        in_maps.append(m)
    res = run_bass_kernel_spmd(nc, in_maps, core_ids=list(range(N_CORES)))
    out = np.concatenate([np.asarray(r["out"], dtype=np.float32) for r in res.results], axis=0)
    return out
```
